# Optimizing a Trainium2 kernel written in Bass

```python
import jax, jax.numpy as jnp
from jax import lax
import numpy as np

D_MODEL = 1024
BATCH = 8
SEQ = 2048
DEPTH = 4
DEC_BATCH = 128
DEC_SEQ = 4
PAST_LEN = 16384
PAGE_SIZE = 128

N_MIXERS = 3
N_A = (DEPTH + 2) // N_MIXERS
N_B = (DEPTH + 1) // N_MIXERS
N_C = DEPTH // N_MIXERS
CHUNK = 128
A_INNER = 2 * D_MODEL
A_HEADS = 8
A_GROUP = A_INNER // A_HEADS
HEAD_SIZE = 64
B_HEADS = D_MODEL // HEAD_SIZE
DECAY_LORA = 64
ICLR_LORA = 64
GATE_LORA = 128
CONV_W = 3
D_FF = 2816
PLE_DIM = 256
ALPHA = (2 * DEPTH) ** 0.25
BETA = (8 * DEPTH) ** -0.25
LN_EPS = 1e-5
GN_EPS = 64e-5

kernel_name = 'hybrid_gmlp_rwkv7_shortconv_macaron_step'


def layer_norm(x, g, b, eps=LN_EPS):
    xf = x.astype(jnp.float32)
    mu = jnp.mean(xf, -1, keepdims=True)
    var = jnp.mean(jnp.square(xf - mu), -1, keepdims=True)
    return ((xf - mu) * lax.rsqrt(var + eps) * g + b).astype(x.dtype)


def swiglu(x, w_in, w_out):
    gate, up = jnp.split(x @ w_in, 2, axis=-1)
    return (jax.nn.silu(gate) * up) @ w_out


def gmlp_mixer(x, w_in, b_in, ln_g, ln_b, w_s, b_s, w_out):
    bn, t, _ = x.shape
    l = min(t, CHUNK)
    z = jax.nn.gelu(x @ w_in + b_in, approximate=False)
    u, v = jnp.split(z, 2, axis=-1)
    v = layer_norm(v, ln_g, ln_b)
    mask = jnp.tril(jnp.ones((l, l), dtype=bool))
    ws = jnp.where(mask, w_s[:, :l, :l], 0.0).astype(v.dtype)
    vc = v.reshape(bn, t // l, l, A_HEADS, A_GROUP)
    mixed = jnp.einsum('hts,bcshg->bcthg', ws, vc) + b_s[:, :l].T[None, None, :, :, None]
    y = u * mixed.reshape(bn, t, A_INNER).astype(u.dtype)
    return y @ w_out, v


def rwkv7_mixer(x, shift_prev, s0, mu, w_rkv, w0, w1, w2, a0, a1, a2, g1, g2, k_k, k_a, r_k, lnx_g, lnx_b, w_o):
    f32 = jnp.float32
    bn, t, d = x.shape
    x_prev = jnp.concatenate([shift_prev[:, None].astype(x.dtype), x[:, :-1]], axis=1)
    xx = x_prev - x
    xr, xw, xk, xv, xa, xg = [x + xx * mu[i] for i in range(6)]
    rkv = jnp.einsum('pbtd,pde->pbte', jnp.stack([xr, xk, xv]), w_rkv)
    r, k, v = rkv[0], rkv[1], rkv[2]
    w_log = -jax.nn.softplus(-(w0 + jnp.tanh(xw @ w1) @ w2).astype(f32)) - 0.5
    decay = jnp.exp(-jnp.exp(w_log))
    a = jax.nn.sigmoid((a0 + (xa @ a1) @ a2).astype(f32))
    g = jax.nn.sigmoid(xg @ g1) @ g2
    heads = lambda z: z.reshape(bn, t, B_HEADS, HEAD_SIZE)
    kk = heads((k * k_k).astype(f32))
    kk = kk / jnp.maximum(jnp.sqrt(jnp.sum(jnp.square(kk), -1, keepdims=True)), 1e-12)
    kmod = k.astype(f32) * (1.0 + (a - 1.0) * k_a)
    rh, kh, vh, wh, ah = heads(r.astype(f32)), heads(kmod), heads(v.astype(f32)), heads(decay), heads(a)

    def step(s, inp):
        r_t, w_t, k_t, v_t, kk_t, a_t = inp
        sa = jnp.einsum('bhvk,bhk->bhv', s, -kk_t)
        s = s * w_t[:, :, None, :] + sa[..., None] * (kk_t * a_t)[:, :, None, :] + v_t[..., None] * k_t[:, :, None, :]
        return s, jnp.einsum('bhvk,bhk->bhv', s, r_t)

    seq = tuple(jnp.moveaxis(z, 1, 0) for z in (rh, wh, kh, vh, kk, ah))
    s_fin, ys = lax.scan(step, s0.astype(f32), seq)
    ys = jnp.moveaxis(ys, 0, 1)
    m = jnp.mean(ys, -1, keepdims=True)
    var = jnp.mean(jnp.square(ys - m), -1, keepdims=True)
    yn = ((ys - m) * lax.rsqrt(var + GN_EPS)).reshape(bn, t, d) * lnx_g + lnx_b
    bonus = (jnp.sum(rh * kh * r_k, -1, keepdims=True) * vh).reshape(bn, t, d)
    out = ((yn + bonus).astype(x.dtype) * g) @ w_o
    return out, x[:, -1].astype(shift_prev.dtype), s_fin.astype(s0.dtype)


def conv_mixer(x, buf_prev, w_in, conv_w, w_out):
    t = x.shape[1]
    bg, cg, xin = jnp.split(x @ w_in, 3, axis=-1)
    z = cg * xin
    zp = jnp.concatenate([buf_prev.astype(z.dtype), z], axis=1)
    conv = conv_w[0] * zp[:, 0:t] + conv_w[1] * zp[:, 1:t + 1] + conv_w[2] * zp[:, 2:t + 2]
    return (bg * conv) @ w_out, zp[:, -(CONV_W - 1):].astype(buf_prev.dtype)


def run_trunk(x, p, b_wkv, b_shift, c_conv, W, keep_chunk_state):
    new_a_v, new_wkv, new_shift, new_conv = [], [], [], []
    for i in range(DEPTH):
        j = i // N_MIXERS
        kind = i % N_MIXERS
        x = layer_norm(ALPHA * x + 0.5 * swiglu(x, W['ffn_w_in'][i, 0], W['ffn_w_out'][i, 0]), W['ln_g'][i, 0], W['ln_b'][i, 0])
        if kind == 0:
            h, v = gmlp_mixer(x, W['a_w_in'][j], W['a_b_in'][j], W['a_ln_g'][j], W['a_ln_b'][j], W['a_w_s'][j], W['a_b_s'][j], W['a_w_out'][j])
            if keep_chunk_state:
                new_a_v.append(v)
        elif kind == 1:
            h, sh, s = rwkv7_mixer(x, b_shift[j], b_wkv[j], W['b_mu'][j], W['b_w_rkv'][j], W['b_w0'][j], W['b_w1'][j], W['b_w2'][j],
                                   W['b_a0'][j], W['b_a1'][j], W['b_a2'][j], W['b_g1'][j], W['b_g2'][j], W['b_k_k'][j], W['b_k_a'][j],
                                   W['b_r_k'][j], W['b_lnx_g'][j], W['b_lnx_b'][j], W['b_w_o'][j])
            new_shift.append(sh)
            new_wkv.append(s)
        else:
            h, buf = conv_mixer(x, c_conv[j], W['c_w_in'][j], W['c_conv_w'][j], W['c_w_out'][j])
            new_conv.append(buf)
        x = layer_norm(ALPHA * x + h, W['ln_g'][i, 1], W['ln_b'][i, 1])
        x = layer_norm(ALPHA * x + 0.5 * swiglu(x, W['ffn_w_in'][i, 1], W['ffn_w_out'][i, 1]), W['ln_g'][i, 2], W['ln_b'][i, 2])
        x = x + jax.nn.sigmoid(x @ W['ple_w_gate'][i]) * (p[i].astype(x.dtype) @ W['ple_w_proj'][i])
    a_state = jnp.stack(new_a_v) if keep_chunk_state else None
    return x, a_state, jnp.stack(new_wkv), jnp.stack(new_shift), jnp.stack(new_conv)


def setup_inputs(seed: int = 0) -> dict:
    key = jax.random.key(seed)
    kit = iter(jax.random.split(key, 64))
    D = D_MODEL

    def nrm(shape, scale):
        return scale * jax.random.normal(next(kit), shape, jnp.float32)

    return {
        'x_prompt': nrm((BATCH, SEQ, D), 1.0),
        'x_sample': nrm((DEC_BATCH, DEC_SEQ, D), 1.0),
        'state_b_wkv': nrm((N_B, DEC_BATCH, B_HEADS, HEAD_SIZE, HEAD_SIZE), 0.5),
        'state_b_shift': nrm((N_B, DEC_BATCH, D), 1.0),
        'state_c_conv': nrm((N_C, DEC_BATCH, CONV_W - 1, D), 1.0),
        'p_prompt': nrm((DEPTH, BATCH, SEQ, PLE_DIM), 1.0),
        'p_sample': nrm((DEPTH, DEC_BATCH, DEC_SEQ, PLE_DIM), 1.0),
        'ln_g': 1.0 + nrm((DEPTH, 3, D), 0.02),
        'ln_b': nrm((DEPTH, 3, D), 0.02),
        'ffn_w_in': nrm((DEPTH, 2, D, 2 * D_FF), D ** -0.5),
        'ffn_w_out': nrm((DEPTH, 2, D_FF, D), BETA * D_FF ** -0.5),
        'ple_w_gate': nrm((DEPTH, D, D), D ** -0.5),
        'ple_w_proj': nrm((DEPTH, PLE_DIM, D), BETA * PLE_DIM ** -0.5),
        'a_w_in': nrm((N_A, D, 2 * A_INNER), D ** -0.5),
        'a_b_in': nrm((N_A, 2 * A_INNER), 0.02),
        'a_ln_g': 1.0 + nrm((N_A, A_INNER), 0.02),
        'a_ln_b': nrm((N_A, A_INNER), 0.02),
        'a_w_s': nrm((N_A, A_HEADS, CHUNK, CHUNK), CHUNK ** -0.5),
        'a_b_s': 1.0 + nrm((N_A, A_HEADS, CHUNK), 0.02),
        'a_w_out': nrm((N_A, A_INNER, D), BETA * A_INNER ** -0.5),
        'b_mu': jax.random.uniform(next(kit), (N_B, 6, D), jnp.float32),
        'b_w_rkv': nrm((N_B, 3, D, D), D ** -0.5),
        'b_w0': nrm((N_B, D), 0.5) - 1.0,
        'b_w1': nrm((N_B, D, DECAY_LORA), D ** -0.5),
        'b_w2': nrm((N_B, DECAY_LORA, D), 0.1 * DECAY_LORA ** -0.5),
        'b_a0': nrm((N_B, D), 0.1),
        'b_a1': nrm((N_B, D, ICLR_LORA), D ** -0.5),
        'b_a2': nrm((N_B, ICLR_LORA, D), 0.1 * ICLR_LORA ** -0.5),
        'b_g1': nrm((N_B, D, GATE_LORA), D ** -0.5),
        'b_g2': nrm((N_B, GATE_LORA, D), GATE_LORA ** -0.5),
        'b_k_k': 0.85 + nrm((N_B, D), 0.02),
        'b_k_a': 1.0 + nrm((N_B, D), 0.02),
        'b_r_k': nrm((N_B, B_HEADS, HEAD_SIZE), 0.1),
        'b_lnx_g': 1.0 + nrm((N_B, D), 0.02),
        'b_lnx_b': nrm((N_B, D), 0.02),
        'b_w_o': nrm((N_B, D, D), BETA * D ** -0.5),
        'c_w_in': nrm((N_C, D, 3 * D), D ** -0.5),
        'c_conv_w': nrm((N_C, CONV_W, D), CONV_W ** -0.5),
        'c_w_out': nrm((N_C, D, D), BETA * D ** -0.5),
    }


def reference(x_prompt, x_sample, state_b_wkv, state_b_shift, state_c_conv, p_prompt, p_sample,
              ln_g, ln_b, ffn_w_in, ffn_w_out, ple_w_gate, ple_w_proj,
              a_w_in, a_b_in, a_ln_g, a_ln_b, a_w_s, a_b_s, a_w_out,
              b_mu, b_w_rkv, b_w0, b_w1, b_w2, b_a0, b_a1, b_a2, b_g1, b_g2, b_k_k, b_k_a, b_r_k,
              b_lnx_g, b_lnx_b, b_w_o, c_w_in, c_conv_w, c_w_out):
    W = dict(ln_g=ln_g, ln_b=ln_b, ffn_w_in=ffn_w_in, ffn_w_out=ffn_w_out, ple_w_gate=ple_w_gate, ple_w_proj=ple_w_proj,
             a_w_in=a_w_in, a_b_in=a_b_in, a_ln_g=a_ln_g, a_ln_b=a_ln_b, a_w_s=a_w_s, a_b_s=a_b_s, a_w_out=a_w_out,
             b_mu=b_mu, b_w_rkv=b_w_rkv, b_w0=b_w0, b_w1=b_w1, b_w2=b_w2, b_a0=b_a0, b_a1=b_a1, b_a2=b_a2,
             b_g1=b_g1, b_g2=b_g2, b_k_k=b_k_k, b_k_a=b_k_a, b_r_k=b_r_k, b_lnx_g=b_lnx_g, b_lnx_b=b_lnx_b, b_w_o=b_w_o,
             c_w_in=c_w_in, c_conv_w=c_conv_w, c_w_out=c_w_out)
    bp = x_prompt.shape[0]
    zero_wkv = jnp.zeros((N_B, bp) + state_b_wkv.shape[2:], state_b_wkv.dtype)
    zero_shift = jnp.zeros((N_B, bp, D_MODEL), state_b_shift.dtype)
    zero_conv = jnp.zeros((N_C, bp, CONV_W - 1, D_MODEL), state_c_conv.dtype)
    y_prompt, _, wkv_p, shift_p, conv_p = run_trunk(x_prompt, p_prompt, zero_wkv, zero_shift, zero_conv, W, False)
    y_sample, a_v_s, wkv_s, shift_s, conv_s = run_trunk(x_sample, p_sample, state_b_wkv, state_b_shift, state_c_conv, W, True)
    return (y_prompt, y_sample, a_v_s, wkv_p, shift_p, conv_p, wkv_s, shift_s, conv_s)
```

```python
import numpy as np
import concourse.bass as bass
import concourse.mybir as mybir
from concourse.bass_utils import run_bass_kernel_spmd
from contextlib import ExitStack

F32 = mybir.dt.float32
BF16 = mybir.dt.bfloat16
AF = mybir.ActivationFunctionType
ALU = mybir.AluOpType
AX = mybir.AxisListType

ENG_ATTR = {'pe': 'tensor', 'act': 'scalar', 'dve': 'vector', 'pool': 'gpsimd', 'sp': 'sync'}
SEM_LIMIT = 30000
NDMA_SEMS = 12


def _is_ap(v):
    return hasattr(v, 'tensor') and hasattr(v, 'ap') and hasattr(v, 'offset')


class Inst:
    __slots__ = ('eng', 'fn', 'is_dma', 'deps', 'sig', 'tok', 'waits', 'ord', 'dma_idx')

    def __init__(self, eng, fn, is_dma):
        self.eng = eng
        self.fn = fn
        self.is_dma = is_dma
        self.deps = {}
        self.sig = False
        self.tok = None
        self.waits = []
        self.ord = 0
        self.dma_idx = -1


class Prog:
    def __init__(self, nc):
        self.nc = nc
        self.insts = []
        self.reg = {}
        self.buckets = {}
        self.sb_off = 0
        self.out_dmas = []

    def sbuf(self, name, shape, dtype, offset):
        t = self.nc.alloc_sbuf_tensor_at(name, list(shape), dtype, offset=offset)
        es = 2 if dtype == BF16 else 4
        fs = int(np.prod(shape[1:]))
        self.reg[t.name] = ('sb', offset, fs, es)
        return t

    def psum(self, name, shape, dtype=F32):
        t = self.nc.alloc_psum_tensor(name, list(shape), dtype)
        fs = int(np.prod(shape[1:]))
        self.reg[t.name] = ('ps', 0, fs, 4)
        return t

    def _rect(self, ap):
        info = self.reg.get(ap.tensor.name)
        if info is None:
            return None
        space, base, fs, es = info
        off = int(ap.offset)
        dims = ap.ap
        p0 = off // fs
        lo = off % fs
        npart = dims[0][1] if dims[0][0] != 0 else 1
        ext = 0
        for st, cnt in dims[1:]:
            ext += (cnt - 1) * abs(st)
        hi = lo + ext + 1
        if space == 'ps':
            b0 = (lo * es) // 2048
            b1 = (hi * es - 1) // 2048
            return (space, 0, 128, b0 * 2048, (b1 + 1) * 2048)
        return (space, p0, p0 + npart, base + lo * es, base + hi * es)

    def _track(self, inst, rect, is_write):
        space, p0, p1, lo, hi = rect
        BS = 1024
        b0, b1 = lo // BS, (hi - 1) // BS
        key = (p0, p1, lo, hi, inst.eng if not inst.is_dma else ('dma', id(inst)), is_write)
        for b in range(b0, b1 + 1):
            d = self.buckets.setdefault((space, b), {})
            dead = []
            for k, rec in d.items():
                q0, q1, l2, h2, other, w2 = rec
                if q0 >= p1 or q1 <= p0 or l2 >= hi or h2 <= lo:
                    continue
                if other is inst:
                    continue
                if is_write or w2:
                    kind = 'raw' if (w2 and not is_write) else 'waw_war'
                    prev = inst.deps.get(id(other))
                    if prev is None or kind == 'raw':
                        inst.deps[id(other)] = (other, kind)
                if is_write and q0 >= p0 and q1 <= p1 and l2 >= lo and h2 <= hi:
                    dead.append(k)
            for k in dead:
                del d[k]
            d[key] = (p0, p1, lo, hi, inst, is_write)

    def add(self, eng, fn, reads, writes, is_dma=False):
        inst = Inst(eng, fn, is_dma)
        for ap in reads:
            r = self._rect(ap)
            if r is not None:
                self._track(inst, r, False)
        for ap in writes:
            r = self._rect(ap)
            if r is not None:
                self._track(inst, r, True)
        self.insts.append(inst)
        return inst

    def op(self, eng, meth, **kw):
        reads, writes = [], []
        for k, v in kw.items():
            if _is_ap(v):
                (writes if k in ('out', 'accum_out', 'ap') else reads).append(v)
        is_dma = (meth == 'dma_start')
        inst = self.add(eng, lambda e: getattr(e, meth)(**kw), reads, writes, is_dma)
        if eng == 'pe':
            src = kw.get('lhsT', kw.get('in_'))
            ro = self._rect(src)
            rw = self._rect(kw['out'])
            if ro is not None and rw is not None:
                rows = (ro[1], ro[2])
                if not hasattr(self, 'pe_rows'):
                    self.pe_rows = {}
                for bnk in range(rw[3] // 2048, (rw[4] - 1) // 2048 + 1):
                    prev = self.pe_rows.get(bnk)
                    if prev is not None and (prev[0][1] <= rows[0] or rows[1] <= prev[0][0]):
                        inst.deps[id(prev[1])] = (prev[1], 'rowgrp')
                    self.pe_rows[bnk] = (rows, inst)
        if is_dma and _is_ap(kw['out']) and self.reg.get(kw['out'].tensor.name) is None:
            self.out_dmas.append(inst)
        return inst

    def emit(self):
        nc = self.nc
        with ExitStack() as es:
            fence = Inst('sp', None, False)
            for d in self.out_dmas:
                fence.deps[id(d)] = (d, 'raw')
            self.insts.append(fence)
            for inst in self.insts:
                nd = {}
                for k, (d, kind) in inst.deps.items():
                    if d.eng == 'pe' and inst.eng == 'pe' and not d.is_dma and not inst.is_dma and kind != 'rowgrp':
                        continue
                    nd[k] = (d, kind)
                    d.sig = True
                inst.deps = nd
            eng_sem = {}
            eng_cnt = {}
            eng_ord = {}
            dma_sems = {}
            dma_cnt = {}
            dma_hist = {}
            nsem = [0]

            def newsem():
                nsem[0] += 1
                return es.enter_context(nc.semaphore("s%d" % nsem[0]))

            per_eng = {e: [] for e in ENG_ATTR}
            for inst in self.insts:
                per_eng[inst.eng].append(inst)
                if inst.is_dma:
                    q = inst.eng
                    if q not in dma_sems:
                        dma_sems[q] = [newsem() for _ in range(NDMA_SEMS)]
                        dma_cnt[q] = [0] * NDMA_SEMS
                        dma_hist[q] = []
                    i = len(dma_hist[q])
                    s = i % NDMA_SEMS
                    dma_cnt[q][s] += 16
                    inst.tok = (dma_sems[q][s], dma_cnt[q][s])
                    inst.sig = True
                    inst.dma_idx = i
                    dma_hist[q].append(inst)
                elif inst.sig:
                    e = inst.eng
                    if e not in eng_sem or eng_cnt[e] >= SEM_LIMIT:
                        eng_sem[e] = newsem()
                        eng_cnt[e] = 0
                    eng_cnt[e] += 1
                    eng_ord[e] = eng_ord.get(e, 0) + 1
                    inst.tok = (eng_sem[e], eng_cnt[e])
                    inst.ord = eng_ord[e]
            waited = {e: {} for e in ENG_ATTR}
            dwaited = {e: set() for e in ENG_ATTR}
            for inst in self.insts:
                e = inst.eng
                deps = [d for d, _ in inst.deps.values()]
                if inst.is_dma and inst.dma_idx >= NDMA_SEMS:
                    deps.append(dma_hist[e][inst.dma_idx - NDMA_SEMS])
                for d in deps:
                    if d.is_dma:
                        if id(d) in dwaited[e]:
                            continue
                        dwaited[e].add(id(d))
                        inst.waits.append(d.tok)
                    else:
                        if waited[e].get(d.eng, 0) >= d.ord:
                            continue
                        waited[e][d.eng] = d.ord
                        inst.waits.append(d.tok)
            self.nsem = nsem[0]

            def run(ename, eobj):
                for inst in per_eng[ename]:
                    for (s, v) in inst.waits:
                        eobj.wait_ge(s, v)
                    if inst.fn is None:
                        continue
                    r = inst.fn(eobj)
                    if inst.sig:
                        r.then_inc(inst.tok[0], 16 if inst.is_dma else 1)

            with nc.Block() as block:
                @block.sync
                def _(e):
                    run('sp', e)

                @block.scalar
                def _(e):
                    run('act', e)

                @block.vector
                def _(e):
                    run('dve', e)

                @block.gpsimd
                def _(e):
                    run('pool', e)

                @block.tensor
                def _(e):
                    run('pe', e)

D = 1024
TOK = 2112
ALPHA = 8.0 ** 0.25
LN_EPS = 1e-5
GN_EPS = 64e-5
T512 = [(0, 512), (512, 512), (1024, 512), (1536, 512), (2048, 64)]
T128 = [(i * 128, 128) for i in range(16)] + [(2048, 64)]
HALVES = [T512[0:2], T512[2:5]]
SB_BASE = 16512
SB_END = 229376
STG_ELEMS = 3072


def build_program(stub_mixers=False, n_layers=4, kinds=None, _reqs=None):
    if _reqs is None:
        _reqs = build_program(stub_mixers, n_layers, kinds, _reqs="dry")
    DRY = (_reqs == "dry")
    REQ = []
    nc = bass.Bass("TRN2", target_bir_lowering=False)
    P = Prog(nc)

    def din(name, shape):
        return nc.dram_tensor(name, list(shape), F32, kind="ExternalInput").ap()

    def dout(name, shape):
        return nc.dram_tensor(name, list(shape), F32, kind="ExternalOutput").ap()

    x_p = din("x_p", [2048, 1024]); x_s = din("x_s", [64, 1024])
    wkv_in = din("wkv_in", [16, 16, 64, 64]); shift_in = din("shift_in", [16, 1024])
    conv_in = din("conv_in", [16, 2, 1024])
    p_p = din("p_p", [4, 2048, 256]); p_s = din("p_s", [4, 64, 256])
    W = {}
    for nm, shp in [("ln_g", [4, 3, 1024]), ("ln_b", [4, 3, 1024]), ("ffn_w_in", [4, 2, 1024, 5632]),
                    ("ffn_w_out", [4, 2, 2816, 1024]), ("ple_w_gate", [4, 1024, 1024]), ("ple_w_proj", [4, 256, 1024]),
                    ("a_w_in", [2, 1024, 4096]), ("a_b_in", [2, 4096]), ("a_ln_g", [2, 2048]), ("a_ln_b", [2, 2048]),
                    ("a_w_sT", [2, 8, 128, 128]), ("a_w_sS", [2, 8, 64, 64]), ("a_b_s", [2, 8, 128]), ("a_b_sS", [2, 8, 64]),
                    ("a_w_out", [2, 2048, 1024]),
                    ("b_mu", [1, 6, 1024]), ("b_w_rkv", [1, 3, 1024, 1024]), ("b_w0", [1, 1024]), ("b_w1", [1, 1024, 64]),
                    ("b_w2", [1, 64, 1024]), ("b_a0", [1, 1024]), ("b_a1", [1, 1024, 64]), ("b_a2", [1, 64, 1024]),
                    ("b_g1", [1, 1024, 128]), ("b_g2", [1, 128, 1024]), ("b_k_k", [1, 1024]), ("b_k_a", [1, 1024]),
                    ("b_r_k", [1, 1024]), ("b_lnx_g", [1, 1024]), ("b_lnx_b", [1, 1024]), ("b_w_o", [1, 1024, 1024]),
                    ("c_w_in", [1, 1024, 3072]), ("c_conv_w", [1, 3, 1024]), ("c_w_out", [1, 1024, 1024])]:
        W[nm] = din(nm, shp)
    y_p = dout("y_p", [2048, 1024]); y_s = dout("y_s", [64, 1024])
    o_av = dout("o_av", [2, 64, 2048]); o_wkv_p = dout("o_wkv_p", [16, 64, 64]); o_shift_p = dout("o_shift_p", [1, 1024])
    o_conv_p = dout("o_conv_p", [2, 1024]); o_wkv_s = dout("o_wkv_s", [16, 16, 64, 64])
    o_shift_s = dout("o_shift_s", [16, 1024]); o_conv_s = dout("o_conv_s", [16, 2, 1024])

    cur = [SB_BASE]
    ncnt = [0]

    def alloc(shape, dt, at=None):
        ncnt[0] += 1
        nb = int(np.prod(shape[1:])) * (2 if dt == BF16 else 4)
        if at is None:
            at = cur[0]
            cur[0] = (at + nb + 63) // 64 * 64
            assert cur[0] <= SB_END, "sbuf overflow"
        else:
            assert at + nb <= SB_END, ("sbuf overflow", at, nb)
        return P.sbuf("t%d" % ncnt[0], shape, dt, at)

    ident = alloc([128, 128], F32)
    ones = alloc([128, 128], F32)
    LNG = alloc([128, 96], F32)
    LNB = alloc([128, 96], F32)
    VEC = alloc([128, 256], F32)
    EPSC = alloc([128, 2], F32)
    epsi = {float(LN_EPS): 0, float(4 * LN_EPS): 1}
    onesb = alloc([128, 128], BF16)
    identb = alloc([128, 128], BF16)
    XF = alloc([128, 8, TOK], F32)
    XB = alloc([128, 8, TOK], BF16)
    XB_OFF = cur[0] - 8 * TOK * 2
    STG = [alloc([128, STG_ELEMS], F32) for _ in range(2)]
    WBS = [alloc([128, STG_ELEMS], BF16) for _ in range(2)]
    STG_OFF = cur[0] - 2 * STG_ELEMS * 2 - 2 * STG_ELEMS * 4
    LNT_OFF = cur[0]
    LNT_SIZE = 16384
    cur[0] += LNT_SIZE
    PH_OFF = cur[0]
    PH_SIZE = SB_END - PH_OFF
    PS = P.psum("ps", [128, 8, 512])
    BSLOTS = WBS + [alloc([128, STG_ELEMS], BF16, at=STG_OFF + i * STG_ELEMS * 2) for i in range(4)]

    def sub(off0, shape, dt):
        return alloc(shape, dt, at=off0)
    SQ = [sub(LNT_OFF + i * 2048, [128, 512], F32) for i in range(2)]
    MEAN = sub(LNT_OFF + 4096, [128, 512], F32)
    VAR = sub(LNT_OFF + 6144, [128, 512], F32)
    RSTD = sub(LNT_OFF + 8192, [128, 512], F32)
    XC = [sub(LNT_OFF + 10240 + i * 2048, [128, 512], F32) for i in range(2)]
    TMPA = [sub(LNT_OFF + 10240 + i * 2048, [128, 512], F32) for i in range(2)]
    TMPB = [sub(LNT_OFF + i * 2048, [128, 512], F32) for i in range(2)]

    bankc = [0]
    tmc = [0]

    reserved = set()

    def bank():
        while True:
            b = bankc[0] % 8
            bankc[0] += 1
            if b not in reserved:
                return b

    def bcast_rows(ap2d, nparts):
        return bass.AP(ap2d.tensor, ap2d.offset, [[0, nparts], [1, int(ap2d.shape[-1])]])

    def op(eng, meth, **kw):
        if DRY:
            return None
        return P.op(eng, meth, **kw)

    op('pool', 'memset', ap=ident[:], constant=1.0)
    op('pool', 'affine_select', out=ident[:], in_=ident[:], pattern=[[-1, 128]], compare_op=ALU.is_equal,
       fill=0.0, base=0, channel_multiplier=1)
    op('dve', 'memset', ap=ones[:], constant=1.0)
    op('dve', 'memset', ap=EPSC[:, 0:1], constant=float(LN_EPS))
    op('dve', 'memset', ap=EPSC[:, 0:1], constant=float(LN_EPS))
    op('dve', 'memset', ap=EPSC[:, 1:2], constant=float(4 * LN_EPS))

    IOS = [sub(PH_OFF + i * 4096, [128, 1024], F32) for i in range(2)]
    iosc = [0]

    def load_cols(dst_ap, src2d, R):
        s = IOS[iosc[0] % 2]; iosc[0] += 1
        op('sp', 'dma_start', out=s[0:R, 0:128], in_=src2d)
        b = bank()
        op('pe', 'transpose', out=PS[:, b, 0:R], in_=s[0:R, 0:128], identity=ident[0:R, 0:R])
        op('dve', 'tensor_copy', out=dst_ap, in_=PS[:, b, 0:R])

    load_cols(LNG[:], W["ln_g"].rearrange("l i (c p) -> (l i c) p", p=128), 96)
    load_cols(LNB[:], W["ln_b"].rearrange("l i (c p) -> (l i c) p", p=128), 96)
    VC = {}
    vcur = [0]

    def vec_cols(key, src2d, R):
        load_cols(VEC[:, vcur[0]:vcur[0] + R], src2d, R)
        VC[key] = vcur[0]
        vcur[0] += R

    wsc = [0]
    cast_engs = ['dve', 'act']

    WCACHE = {}
    bsc = [0]
    issued = {}
    nreq = [0]
    nissued = [0]

    live = {}
    SLOTN = ['B2', 'B3', 'B4', 'B5', 'W0', 'W1']

    def _slot_ap(nm):
        return BSLOTS[{'W0': 0, 'W1': 1, 'B2': 2, 'B3': 3, 'B4': 4, 'B5': 5}[nm]]

    def _try_issue(r):
        parts, K, key = _reqs[r]
        kc = max(1, K // 128)
        kp = min(K, 128)
        ntot = sum(int(a.shape[1]) for a in parts)
        assert kc * ntot <= STG_ELEMS
        if key is not None and key in WCACHE:
            free = [s_ for s_ in SLOTN if s_ not in live]
            if not free:
                return None
            live[free[0]] = r
            dt_, wr = WCACHE[key]
            bs = _slot_ap(free[0])
            inst = op('sp', 'dma_start', out=bs[0:kp, 0:kc * ntot], in_=dt_)
            inst.deps[id(wr)] = (wr, 'raw')
            return bs[0:kp, 0:kc * ntot].rearrange("p (c n) -> p c n", c=kc)
        wfree = [s_ for s_ in ('W0', 'W1') if s_ not in live]
        tfree = [t for t in (0, 1) if ('B%d' % (2 * t + 2)) not in live and ('B%d' % (2 * t + 3)) not in live]
        if not wfree or not tfree:
            return None
        t = tfree[wsc[0] % len(tfree)]
        slot = 0 if wfree[0] == 'W0' else 1
        if len(wfree) == 2:
            slot = wsc[0] % 2
        live['W%d' % slot] = r
        ce = cast_engs[wsc[0] % len(cast_engs)]
        wsc[0] += 1
        sv = STG[t][0:kp, 0:kc * ntot].rearrange("p (c n) -> p c n", c=kc)
        wv = WBS[slot][0:kp, 0:kc * ntot].rearrange("p (c n) -> p c n", c=kc)
        col = 0
        for a in parts:
            n = int(a.shape[1])
            src = a.rearrange("(c p) n -> p c n", p=kp)
            op('sp', 'dma_start', out=sv[:, :, col:col + n], in_=src)
            col += n
        if ce == 'act':
            op('act', 'activation', out=wv, in_=sv, func=AF.Copy)
        else:
            op(ce, 'tensor_copy', out=wv, in_=sv)
        if key is not None:
            dt_ = nc.dram_tensor("wc%d" % len(WCACHE), [kp, kc * ntot], BF16, kind="Internal").ap()
            wr = op('pool', 'dma_start', out=dt_, in_=WBS[slot][0:kp, 0:kc * ntot])
            WCACHE[key] = (dt_, wr)
        return wv

    def wload(parts, K, key=None):
        i = nreq[0]; nreq[0] += 1
        kc = max(1, K // 128); kp = min(K, 128)
        ntot = sum(int(a.shape[1]) for a in parts)
        if DRY:
            REQ.append((parts, K, key))
            return WBS[0][0:kp, 0:kc * ntot].rearrange("p (c n) -> p c n", c=kc)
        for s_ in [s_ for s_, r_ in live.items() if r_ < i]:
            del live[s_]
        while nissued[0] < len(_reqs) and nissued[0] <= i + 4:
            r = nissued[0]
            k2 = _reqs[r][2]
            cached = (k2 is not None and k2 in WCACHE)
            if not cached and r > i + 1:
                break
            ap_ = _try_issue(r)
            if ap_ is None:
                assert r > i, "no weight slot for a mandatory load"
                break
            issued[r] = ap_
            nissued[0] += 1
        return issued.pop(i)

    def pipelined(blocks, K, body, keys=None):
        for i in range(len(blocks)):
            body(i, wload(blocks[i], K, None if keys is None else keys[i]))

    def load_x():
        for ti, (t0, n) in enumerate(T128):
            s = IOS[iosc[0] % 2]; iosc[0] += 1
            src = x_p[t0:t0 + n, :] if t0 < 2048 else x_s
            op('sp', 'dma_start', out=s[0:n, :], in_=src)
            for hh in range(2):
                b = bank()
                for q in range(4):
                    c = hh * 4 + q
                    op('pe', 'transpose', out=PS[:, b, q * n:(q + 1) * n], in_=s[0:n, c * 128:(c + 1) * 128],
                       identity=ident[0:n, 0:n])
                pv = PS[:, b, 0:4 * n].rearrange("p (a t) -> p a t", a=4)
                op('dve', 'tensor_copy', out=XF[:, hh * 4:(hh + 1) * 4, t0:t0 + n], in_=pv)
                op('act', 'activation', out=XB[:, hh * 4:(hh + 1) * 4, t0:t0 + n], in_=XF[:, hh * 4:(hh + 1) * 4, t0:t0 + n], func=AF.Copy)

    def store_y():
        for ti, (t0, n) in enumerate(T128):
            s = IOS[iosc[0] % 2]; iosc[0] += 1
            for hh in range(2):
                b = bank()
                for q in range(4):
                    c = hh * 4 + q
                    op('pe', 'transpose', out=PS[0:n, b, q * 128:(q + 1) * 128], in_=XF[:, c, t0:t0 + n], identity=ident[:])
                if hh == 0:
                    op('dve', 'tensor_copy', out=s[0:n, 0:512], in_=PS[0:n, b, :])
                else:
                    op('act', 'activation', out=s[0:n, 512:1024], in_=PS[0:n, b, :], func=AF.Copy)
            dst = y_p[t0:t0 + n, :] if t0 < 2048 else y_s
            op('sp', 'dma_start', out=dst, in_=s[0:n, :])

    def layer_norm(gi, eps, tiles=None):
        for (t0, n) in (T512 if tiles is None else tiles):
            b1 = bank(); b2 = bank()
            for c in range(8):
                op('pe', 'matmul', out=PS[:, b1, 0:n], lhsT=ones[:], rhs=XF[:, c, t0:t0 + n], start=(c == 0), stop=(c == 7))
            for c in range(8):
                sq = SQ[c % 2]
                op('pool', 'tensor_tensor', out=sq[:, 0:n], in0=XF[:, c, t0:t0 + n], in1=XF[:, c, t0:t0 + n], op=ALU.mult)
                op('pe', 'matmul', out=PS[:, b2, 0:n], lhsT=ones[:], rhs=sq[:, 0:n], start=(c == 0), stop=(c == 7))
            op('dve', 'tensor_scalar', out=MEAN[:, 0:n], in0=PS[:, b1, 0:n], scalar1=1.0 / D, scalar2=None, op0=ALU.mult)
            op('dve', 'tensor_tensor', out=RSTD[:, 0:n], in0=MEAN[:, 0:n], in1=MEAN[:, 0:n], op=ALU.mult)
            op('dve', 'scalar_tensor_tensor', out=VAR[:, 0:n], in0=PS[:, b2, 0:n], scalar=1.0 / D, in1=RSTD[:, 0:n],
               op0=ALU.mult, op1=ALU.subtract)
            op('act', 'activation', out=VAR[:, 0:n], in_=VAR[:, 0:n], func=AF.Ln, bias=EPSC[:, epsi[float(eps)]:epsi[float(eps)] + 1], scale=1.0)
            op('act', 'activation', out=RSTD[:, 0:n], in_=VAR[:, 0:n], func=AF.Exp, scale=-0.5)
            for c in range(8):
                xc = XC[c % 2]
                op('pool' if c % 2 == 1 else 'dve', 'tensor_tensor', out=xc[:, 0:n], in0=XF[:, c, t0:t0 + n], in1=MEAN[:, 0:n], op=ALU.subtract)
                op('dve', 'tensor_tensor', out=xc[:, 0:n], in0=xc[:, 0:n], in1=RSTD[:, 0:n], op=ALU.mult)
                g = LNG[:, gi * 8 + c:gi * 8 + c + 1]
                bb = LNB[:, gi * 8 + c:gi * 8 + c + 1]
                op('act', 'activation', out=XF[:, c, t0:t0 + n], in_=xc[:, 0:n], func=AF.Identity, scale=g, bias=bb)
                if c % 4 != 3:
                    op('act', 'activation', out=XB[:, c, t0:t0 + n], in_=xc[:, 0:n], func=AF.Identity, scale=g, bias=bb)
                else:
                    op('dve', 'tensor_scalar', out=XB[:, c, t0:t0 + n], in0=xc[:, 0:n], scalar1=g, scalar2=bb, op0=ALU.mult, op1=ALU.add)

    H = sub(PH_OFF, [128, 22, 1088], BF16)

    def ffn(l, i, pre_ln=None, side=None):
        w_in = W["ffn_w_in"][l, i]
        w_out = W["ffn_w_out"][l, i]
        if pre_ln is not None:
            layer_norm(pre_ln[0], pre_ln[1], tiles=HALVES[0])
        for hi_, half in enumerate(HALVES):
            h0 = half[0][0]

            def body_in(j, cw):
                for (t0, n) in half:
                    bg = bank(); bu = bank()
                    for kc in range(8):
                        op('pe', 'matmul', out=PS[:, bg, 0:n], lhsT=cw[:, kc, 0:128], rhs=XB[:, kc, t0:t0 + n],
                           start=(kc == 0), stop=(kc == 7))
                    for kc in range(8):
                        op('pe', 'matmul', out=PS[:, bu, 0:n], lhsT=cw[:, kc, 128:256], rhs=XB[:, kc, t0:t0 + n],
                           start=(kc == 0), stop=(kc == 7))
                    tm = TMPA[tmc[0] % 2]; tmc[0] += 1
                    op('act', 'activation', out=tm[:, 0:n], in_=PS[:, bg, 0:n], func=AF.Silu)
                    op('dve', 'tensor_tensor', out=H[:, j, t0 - h0:t0 - h0 + n], in0=tm[:, 0:n], in1=PS[:, bu, 0:n], op=ALU.mult)
                if side is not None:
                    next(side, None)

            pipelined([[w_in[:, j * 128:(j + 1) * 128], w_in[:, 2816 + j * 128:2816 + (j + 1) * 128]] for j in range(22)],
                      1024, body_in)
            if pre_ln is not None and hi_ == 0:
                layer_norm(pre_ln[0], pre_ln[1], tiles=HALVES[1])

            def body_out(m, cw):
                for (t0, n) in half:
                    b = bank()
                    for fc in range(22):
                        op('pe', 'matmul', out=PS[:, b, 0:n], lhsT=cw[:, fc, :], rhs=H[:, fc, t0 - h0:t0 - h0 + n],
                           start=(fc == 0), stop=(fc == 21))
                    op('dve', 'scalar_tensor_tensor', out=XF[:, m, t0:t0 + n], in0=XF[:, m, t0:t0 + n], scalar=2.0 * ALPHA,
                       in1=PS[:, b, 0:n], op0=ALU.mult, op1=ALU.add)

            pipelined([[w_out[:, m * 128:(m + 1) * 128]] for m in range(8)], 2816, body_out)

    PT = sub(PH_OFF + 8192, [128, 2, TOK], BF16)

    def ple(l):
        for gi in range(4):
            s = IOS[iosc[0] % 2]; iosc[0] += 1
            t0 = gi * 512
            op('sp', 'dma_start', out=s[:, :].rearrange("p (n d) -> p n d", n=4),
               in_=p_p[l, t0:t0 + 512, :].rearrange("(n p) d -> p n d", p=128))
            for k2 in range(2):
                b = bank()
                for q in range(4):
                    op('pe', 'transpose', out=PS[:, b, q * 128:(q + 1) * 128],
                       in_=s[:, q * 256 + k2 * 128:q * 256 + (k2 + 1) * 128], identity=ident[:])
                if k2 == 0:
                    op('dve', 'tensor_copy', out=PT[:, k2, t0:t0 + 512], in_=PS[:, b, :])
                else:
                    op('act', 'activation', out=PT[:, k2, t0:t0 + 512], in_=PS[:, b, :], func=AF.Copy)
        s = IOS[iosc[0] % 2]; iosc[0] += 1
        op('sp', 'dma_start', out=s[0:64, 0:256], in_=p_s[l])
        b = bank()
        for k2 in range(2):
            op('pe', 'transpose', out=PS[:, b, k2 * 64:(k2 + 1) * 64], in_=s[0:64, k2 * 128:(k2 + 1) * 128], identity=ident[0:64, 0:64])
        op('dve', 'tensor_copy', out=PT[:, :, 2048:2112], in_=PS[:, b, 0:128].rearrange("p (a t) -> p a t", a=2))
        wp = wload([W["ple_w_proj"][l]], 256)
        WP = sub(PH_OFF + 8192 + 2 * TOK * 2 + 64, [128, 2, 1024], BF16)
        op('dve', 'tensor_copy', out=WP[:], in_=wp)
        wg = W["ple_w_gate"][l]

        def body(mb, cw):
            for mm_ in range(2):
                m = mb * 2 + mm_
                for (t0, n) in T512:
                    bg = bank(); bp = bank()
                    for kc in range(8):
                        op('pe', 'matmul', out=PS[:, bg, 0:n], lhsT=cw[:, kc, mm_ * 128:(mm_ + 1) * 128], rhs=XB[:, kc, t0:t0 + n],
                           start=(kc == 0), stop=(kc == 7))
                    for k2 in range(2):
                        op('pe', 'matmul', out=PS[:, bp, 0:n], lhsT=WP[:, k2, m * 128:(m + 1) * 128], rhs=PT[:, k2, t0:t0 + n],
                           start=(k2 == 0), stop=(k2 == 1))
                    tm = TMPA[tmc[0] % 2]; tmc[0] += 1
                    op('act', 'activation', out=tm[:, 0:n], in_=PS[:, bg, 0:n], func=AF.Sigmoid)
                    op('dve', 'tensor_tensor', out=tm[:, 0:n], in0=tm[:, 0:n], in1=PS[:, bp, 0:n], op=ALU.mult)
                    op('dve', 'tensor_tensor', out=XF[:, m, t0:t0 + n], in0=XF[:, m, t0:t0 + n], in1=tm[:, 0:n], op=ALU.add)

        pipelined([[wg[:, mb * 256:(mb + 1) * 256]] for mb in range(4)], 1024, body)
        for (t0, n) in T512:
            op('act', 'activation', out=XB[:, 0:4, t0:t0 + n], in_=XF[:, 0:4, t0:t0 + n], func=AF.Copy)
            op('dve', 'tensor_copy', out=XB[:, 4:8, t0:t0 + n], in_=XF[:, 4:8, t0:t0 + n])

    def mixer_stub():
        for (t0, n) in T512:
            op('dve', 'tensor_scalar', out=XF[:, :, t0:t0 + n], in0=XF[:, :, t0:t0 + n], scalar1=ALPHA, scalar2=None, op0=ALU.mult)

    op('dve', 'memset', ap=onesb[:], constant=1.0)
    op('act', 'activation', out=identb[:], in_=ident[:], func=AF.Copy)

    def setup_vecs():
        vec_cols("a_b_u0", W["a_b_in"][0:1, 0:2048].rearrange("o (c p) -> (o c) p", p=128), 16)
        vec_cols("a_b_u1", W["a_b_in"][1:2, 0:2048].rearrange("o (c p) -> (o c) p", p=128), 16)
        vec_cols("a_g0", W["a_ln_g"][0:1, :].rearrange("o (c p) -> (o c) p", p=128), 16)
        vec_cols("a_g1", W["a_ln_g"][1:2, :].rearrange("o (c p) -> (o c) p", p=128), 16)
        vec_cols("a_b0", W["a_ln_b"][0:1, :].rearrange("o (c p) -> (o c) p", p=128), 16)
        vec_cols("a_b1", W["a_ln_b"][1:2, :].rearrange("o (c p) -> (o c) p", p=128), 16)
        vec_cols("c_cw", W["c_conv_w"][0].rearrange("i (c p) -> (i c) p", p=128), 24)
        vec_cols("b_mu", W["b_mu"][0].rearrange("i (c p) -> (i c) p", p=128), 48)
        for nm in ["b_w0", "b_a0", "b_k_k", "b_k_a", "b_r_k"]:
            vec_cols(nm, W[nm].rearrange("o (c p) -> (o c) p", p=128), 8)

    setup_vecs()

    def vcol(key, i):
        return VEC[:, VC[key] + i:VC[key] + i + 1]

    def mixer_c(j):
        w_in = W["c_w_in"][j]
        w_out = W["c_w_out"][j]
        Z = sub(PH_OFF, [128, 8, 514], F32)
        ZS = sub(PH_OFF + 16448, [128, 8, 16, 6], F32)
        YC = sub(PH_OFF + 16448 + 3072, [128, 8, 512], BF16)
        CT = sub(PH_OFF + 16448 + 3072 + 8192, [128, 8, 32], F32)
        IOC = sub(PH_OFF + 16448 + 3072 + 8192 + 1024, [128, 1024], F32)
        op('dve', 'memset', ap=Z[:, :, 0:2], constant=0.0)
        op('sp', 'dma_start', out=IOC[0:32, :], in_=conv_in.rearrange("b i d -> (b i) d"))
        for hh in range(2):
            b = bank()
            for q in range(4):
                c = hh * 4 + q
                op('pe', 'transpose', out=PS[:, b, q * 32:(q + 1) * 32], in_=IOC[0:32, c * 128:(c + 1) * 128], identity=ident[0:32, 0:32])
            for q in range(4):
                c = hh * 4 + q
                op('dve', 'tensor_copy', out=ZS[:, c, :, 0:2], in_=PS[:, b, q * 32:(q + 1) * 32].rearrange("p (b i) -> p b i", i=2))
        for gi, (t0, n) in enumerate(T512):
            samp = (t0 >= 2048)

            def zview(m, lo):
                if samp:
                    return ZS[:, m, :, lo:lo + 4]
                return Z[:, m, lo:lo + n]

            def v3(ap):
                return ap.rearrange("p (b t) -> p b t", t=4) if samp else ap

            def body_in(m, cw):
                b1 = bank(); b2 = bank(); b3 = bank()
                for bi, bb in enumerate((b1, b2, b3)):
                    for kc in range(8):
                        op('pe', 'matmul', out=PS[:, bb, 0:n], lhsT=cw[:, kc, bi * 128:(bi + 1) * 128], rhs=XB[:, kc, t0:t0 + n],
                           start=(kc == 0), stop=(kc == 7))
                tm = TMPA[tmc[0] % 2]; tmc[0] += 1
                cv = TMPB[tmc[0] % 2]
                op('act', 'activation', out=tm[:, 0:n], in_=PS[:, b1, 0:n], func=AF.Copy)
                op('dve', 'tensor_tensor', out=zview(m, 2), in0=v3(tm[:, 0:n]), in1=v3(PS[:, b2, 0:n]), op=ALU.mult)
                op('dve', 'tensor_scalar', out=v3(cv[:, 0:n]), in0=zview(m, 0), scalar1=vcol("c_cw", 0 * 8 + m), scalar2=None, op0=ALU.mult)
                op('dve', 'scalar_tensor_tensor', out=v3(cv[:, 0:n]), in0=zview(m, 1), scalar=vcol("c_cw", 1 * 8 + m), in1=v3(cv[:, 0:n]),
                   op0=ALU.mult, op1=ALU.add)
                op('dve', 'scalar_tensor_tensor', out=v3(cv[:, 0:n]), in0=zview(m, 2), scalar=vcol("c_cw", 2 * 8 + m), in1=v3(cv[:, 0:n]),
                   op0=ALU.mult, op1=ALU.add)
                op('dve', 'tensor_tensor', out=YC[:, m, 0:n], in0=cv[:, 0:n], in1=PS[:, b3, 0:n], op=ALU.mult)

            pipelined([[w_in[:, 1024 + m * 128:1024 + (m + 1) * 128], w_in[:, 2048 + m * 128:2048 + (m + 1) * 128],
                        w_in[:, m * 128:(m + 1) * 128]] for m in range(8)], 1024, body_in, keys=[("c", "in", m) for m in range(8)])

            def body_out(mb, cw):
                for mm_ in range(2):
                    m = mb * 2 + mm_
                    b = bank()
                    for kc in range(8):
                        op('pe', 'matmul', out=PS[:, b, 0:n], lhsT=cw[:, kc, mm_ * 128:(mm_ + 1) * 128], rhs=YC[:, kc, 0:n],
                           start=(kc == 0), stop=(kc == 7))
                    op('dve', 'scalar_tensor_tensor', out=XF[:, m, t0:t0 + n], in0=XF[:, m, t0:t0 + n], scalar=ALPHA,
                       in1=PS[:, b, 0:n], op0=ALU.mult, op1=ALU.add)

            pipelined([[w_out[:, mb * 256:(mb + 1) * 256]] for mb in range(4)], 1024, body_out, keys=[("c", "out", mb) for mb in range(4)])
            if not samp:
                if gi == 3:
                    for c in range(8):
                        op('sp', 'dma_start', out=o_conv_p[:, c * 128:(c + 1) * 128].rearrange("i p -> p i"), in_=Z[:, c, 512:514],
                           allow_slow_non_contiguous=True)
                else:
                    op('dve', 'tensor_copy', out=Z[:, :, 0:2], in_=Z[:, :, 512:514])
            else:
                for c in range(8):
                    op('dve', 'tensor_copy', out=CT[:, c, :].rearrange("p (b i) -> p b i", i=2), in_=ZS[:, c, :, 4:6])
                for hh in range(2):
                    b = bank()
                    for q in range(4):
                        c = hh * 4 + q
                        op('pe', 'transpose', out=PS[0:32, b, q * 128:(q + 1) * 128], in_=CT[:, c, :], identity=ident[:])
                    op('dve', 'tensor_copy', out=IOC[0:32, hh * 512:(hh + 1) * 512], in_=PS[0:32, b, :])
                op('sp', 'dma_start', out=o_conv_s.rearrange("b i d -> (b i) d"), in_=IOC[0:32, :])

    GA = [(i * 256, 256) for i in range(8)] + [(2048, 64)]

    def mixer_a(j):
        w_in = W["a_w_in"][j]
        w_out = W["a_w_out"][j]
        o = [PH_OFF]

        def pa(shape, dt):
            nb = int(np.prod(shape[1:])) * (2 if dt == BF16 else 4)
            t = sub(o[0], shape, dt)
            o[0] = (o[0] + nb + 63) // 64 * 64
            return t
        VF = pa([128, 2, 2048], F32)
        VNS = [pa([128, 2, 2048], BF16) for _ in range(2)]
        YA = pa([128, 16, 256], BF16)
        WST = pa([128, 8, 128], BF16)
        WSS = pa([128, 8, 64], BF16)
        RBC = sub(LNT_OFF + 4096, [128, 8, 128], F32)
        RSBC = sub(LNT_OFF + 8192, [128, 8, 64], F32)
        BSBC = pa([128, 8, 128], F32)
        BSSBC = sub(LNT_OFF + 14336, [128, 8, 64], F32)
        BVT = [pa([128, 256], F32) for _ in range(4)]
        BBT = [pa([128, 128], F32) for _ in range(2)]
        STAT = pa([128, 2, 24], F32)
        MV = pa([128, 2, 2], F32)
        RSTDA = pa([128, 2, 1], F32)
        assert o[0] <= SB_END
        GBROW = sub(PH_OFF + 8192, [33, 2048], F32)
        sv = VF[:, 0, 0:1024].rearrange("p (h t) -> p h t", h=8)
        op('sp', 'dma_start', out=sv, in_=W["a_w_sT"][j].rearrange("h s t -> s h t"))
        op('pool', 'affine_select', out=sv, in_=sv, pattern=[[0, 8], [1, 128]], compare_op=ALU.is_ge, fill=0.0,
           base=0, channel_multiplier=-1)
        op('dve', 'tensor_copy', out=WST[:], in_=sv)
        sv2 = VF[0:64, 1, 0:512].rearrange("p (h t) -> p h t", h=8)
        op('sp', 'dma_start', out=sv2, in_=W["a_w_sS"][j].rearrange("h s t -> s h t"))
        op('pool', 'affine_select', out=sv2, in_=sv2, pattern=[[0, 8], [1, 64]], compare_op=ALU.is_ge, fill=0.0,
           base=0, channel_multiplier=-1)
        op('dve', 'tensor_copy', out=WSS[0:64], in_=sv2)
        for hh in range(2):
            b = bank()
            for q in range(4):
                h = hh * 4 + q
                op('pe', 'matmul', out=PS[:, b, q * 128:(q + 1) * 128], lhsT=onesb[:], rhs=WST[:, h, :], start=True, stop=True)
            op('dve', 'tensor_copy', out=RBC[:, hh * 4:(hh + 1) * 4, :], in_=PS[:, b, :].rearrange("p (h t) -> p h t", h=4))
        b = bank()
        for h in range(8):
            op('pe', 'matmul', out=PS[:, b, h * 64:(h + 1) * 64], lhsT=onesb[0:64, :], rhs=WSS[0:64, h, :], start=True, stop=True)
        op('dve', 'tensor_copy', out=RSBC[:], in_=PS[:, b, :].rearrange("p (h t) -> p h t", h=8))
        op('sp', 'dma_start', out=BSBC[:].rearrange("p h t -> p (h t)"),
           in_=bcast_rows(W["a_b_s"][j:j + 1].rearrange("o h t -> o (h t)"), 128))
        op('sp', 'dma_start', out=BSSBC[:].rearrange("p h t -> p (h t)"),
           in_=bcast_rows(W["a_b_sS"][j:j + 1].rearrange("o h t -> o (h t)"), 128))

        def p1(gidx):
            g0, gn = GA[gidx]
            VN = VNS[gidx % 2]
            samp = (g0 >= 2048)
            tiles = [(g0, 64)] if samp else [(g0, 128), (g0 + 128, 128)]
            for q in range(8):
                cw = wload([w_in[:, 2048 + q * 256:2048 + (q + 1) * 256]], 1024, ("a", j, "v", q))
                bv = BVT[q % 4]
                op('pool', 'dma_start', out=bv[:], in_=bcast_rows(W["a_b_in"][j:j + 1, 2048 + q * 256:2048 + (q + 1) * 256], 128))
                for i, (t0, nt) in enumerate(tiles):
                    b = bank()
                    for kc in range(8):
                        op('pe', 'matmul', out=PS[0:nt, b, 0:256], lhsT=XB[:, kc, t0:t0 + nt], rhs=cw[:, kc, :],
                           start=(kc == 0), stop=(kc == 7))
                    op('dve', 'tensor_tensor', out=VF[0:nt, i, q * 256:(q + 1) * 256], in0=PS[0:nt, b, 0:256], in1=bv[0:nt, :], op=ALU.add)
                yield
            for i, (t0, nt) in enumerate(tiles):
                op('act', 'activation', out=VF[0:nt, i, :], in_=VF[0:nt, i, :], func=AF.Gelu)
                for q in range(4):
                    op('dve', 'bn_stats', out=STAT[0:nt, i, q * 6:(q + 1) * 6], in_=VF[0:nt, i, q * 512:(q + 1) * 512])
                op('dve', 'bn_aggr', out=MV[0:nt, i, :], in_=STAT[0:nt, i, :])
                op('act', 'activation', out=RSTDA[0:nt, i, :], in_=MV[0:nt, i, 1:2], func=AF.Sqrt, bias=LN_EPS, scale=1.0)
                op('dve', 'reciprocal', out=RSTDA[0:nt, i, :], in_=RSTDA[0:nt, i, :])
                yield
                op('dve', 'tensor_scalar', out=VF[0:nt, i, :], in0=VF[0:nt, i, :], scalar1=MV[0:nt, i, 0:1], scalar2=RSTDA[0:nt, i, :],
                   op0=ALU.subtract, op1=ALU.mult)
                op('act', 'activation', out=VN[0:nt, i, :], in_=VF[0:nt, i, :], func=AF.Copy)
                yield
            if samp:
                op('sp', 'dma_start', out=GBROW[0:1, :], in_=W["a_ln_g"][j:j + 1, :])
                op('sp', 'dma_start', out=GBROW[32:33, :], in_=W["a_ln_b"][j:j + 1, :])
                for q in range(4):
                    bg = bank(); bb = bank()
                    op('pe', 'matmul', out=PS[0:64, bg, :], lhsT=ones[0:1, 0:64], rhs=GBROW[0:1, q * 512:(q + 1) * 512], start=True, stop=True)
                    op('pe', 'matmul', out=PS[0:64, bb, :], lhsT=ones[32:33, 0:64], rhs=GBROW[32:33, q * 512:(q + 1) * 512], start=True, stop=True)
                    op('dve', 'tensor_tensor', out=VF[0:64, 0, q * 512:(q + 1) * 512], in0=VF[0:64, 0, q * 512:(q + 1) * 512], in1=PS[0:64, bg, :], op=ALU.mult)
                    op('dve', 'tensor_tensor', out=VF[0:64, 0, q * 512:(q + 1) * 512], in0=VF[0:64, 0, q * 512:(q + 1) * 512], in1=PS[0:64, bb, :], op=ALU.add)
                op('sp', 'dma_start', out=o_av[j], in_=VF[0:64, 0, :])
                yield

        def p23(gidx):
            g0, gn = GA[gidx]
            VN = VNS[gidx % 2]
            samp = (g0 >= 2048)
            tiles = [(g0, 64)] if samp else [(g0, 128), (g0 + 128, 128)]
            for bi in range(8):
                cw = wload([w_in[:, bi * 256:(bi + 1) * 256]], 1024, ("a", j, "u", bi))
                for f2 in range(2):
                    fc = bi * 2 + f2
                    h = fc // 2
                    bu = bank(); bm = bank()
                    for kc in range(8):
                        op('pe', 'matmul', out=PS[:, bu, 0:gn], lhsT=cw[:, kc, f2 * 128:(f2 + 1) * 128], rhs=XB[:, kc, g0:g0 + gn],
                           start=(kc == 0), stop=(kc == 7))
                    for i, (t0, nt) in enumerate(tiles):
                        wsp = WSS[0:64, h, :] if samp else WST[:, h, :]
                        op('pe', 'matmul', out=PS[:, bm, i * 128:i * 128 + nt], lhsT=VN[0:nt, i, fc * 128:(fc + 1) * 128], rhs=wsp,
                           start=True, stop=True)
                    tm = TMPA[tmc[0] % 2]
                    t2 = TMPB[tmc[0] % 2]
                    bbt = BBT[tmc[0] % 2]; tmc[0] += 1
                    op('act', 'activation', out=tm[:, 0:gn], in_=PS[:, bu, 0:gn], func=AF.Gelu, bias=vcol("a_b_u%d" % j, fc), scale=1.0)
                    nt = tiles[0][1]
                    rb = RSBC[:, h, :] if samp else RBC[:, h, :]
                    bs = BSSBC[:, h, :] if samp else BSBC[:, h, :]
                    op('dve', 'scalar_tensor_tensor', out=bbt[:, 0:nt], in0=rb, scalar=vcol("a_b%d" % j, fc), in1=bs, op0=ALU.mult, op1=ALU.add)
                    for i in range(len(tiles)):
                        op('dve', 'scalar_tensor_tensor', out=t2[:, i * 128:i * 128 + nt], in0=PS[:, bm, i * 128:i * 128 + nt],
                           scalar=vcol("a_g%d" % j, fc), in1=bbt[:, 0:nt], op0=ALU.mult, op1=ALU.add)
                    op('dve', 'tensor_tensor', out=YA[:, fc, 0:gn], in0=t2[:, 0:gn], in1=tm[:, 0:gn], op=ALU.mult)
                yield
            for m in range(8):
                cw = wload([w_out[:, m * 128:(m + 1) * 128]], 2048, ("a", j, "o", m))
                b = bank()
                for fc in range(16):
                    op('pe', 'matmul', out=PS[:, b, 0:gn], lhsT=cw[:, fc, :], rhs=YA[:, fc, 0:gn], start=(fc == 0), stop=(fc == 15))
                op('dve', 'scalar_tensor_tensor', out=XF[:, m, g0:g0 + gn], in0=XF[:, m, g0:g0 + gn], scalar=ALPHA,
                   in1=PS[:, b, 0:gn], op0=ALU.mult, op1=ALU.add)
                yield

        def interleave(ga, gb):
            gens = [g_ for g_ in (ga, gb) if g_ is not None]
            while gens:
                for g_ in list(gens):
                    try:
                        next(g_)
                    except StopIteration:
                        gens.remove(g_)

        interleave(p1(0), None)
        for gidx in range(len(GA)):
            interleave(p1(gidx + 1) if gidx + 1 < len(GA) else None, p23(gidx))

    def mixer_b(j):
        NEG_EM05 = -float(np.exp(-0.5))
        o1 = [XB_OFF]
        o2 = [LNT_OFF]

        def p1(shape, dt):
            nb_ = int(np.prod(shape[1:])) * (2 if dt == BF16 else 4)
            t = sub(o1[0], shape, dt)
            o1[0] = (o1[0] + nb_ + 63) // 64 * 64
            assert o1[0] <= XB_OFF + 8 * TOK * 2
            return t

        def p2(shape, dt, at=None):
            nb_ = int(np.prod(shape[1:])) * (2 if dt == BF16 else 4)
            if at is not None:
                return sub(at, shape, dt)
            t = sub(o2[0], shape, dt)
            o2[0] = (o2[0] + nb_ + 63) // 64 * 64
            assert o2[0] <= SB_END
            return t

        BLKf = p1([128, 128], F32)
        BLKb = p1([128, 128], BF16)
        IND2 = p1([128, 2], F32)
        EY = [p1([128, 256], BF16) for _ in range(2)]
        CARRY = p1([128, 8, 1], F32)
        SHIFTS = p1([128, 8, 16, 1], F32)
        CT2 = p1([128, 8, 16], F32)
        SEL = [p1([128, 64], BF16) for _ in range(4)]
        LOR = p1([128, 3, 128], BF16)
        RS = p1([128, 16, 1], F32)
        GS = p1([128, 4, 16, 1], F32)
        M_SU = p1([128, 128], F32)
        M_SL = p1([128, 128], F32)
        M_UI = p1([128, 128], F32)
        WC = p1([128, 8, 2, 1], F32)
        TMPS = p1([128, 2, 64], F32)
        tmps2_off = o1[0]
        o1[0] += 512
        g0t2_off = o1[0]
        o1[0] += 2048
        fm_off = o1[0]
        o1[0] += 5 * 4096 + 2 * 2048
        assert o1[0] <= XB_OFF + 8 * TOK * 2
        VTOK = p2([128, 1024], F32)
        VTOKB_OFF = o2[0]
        VTOKB = p2([128, 1024], BF16)
        GTOK = p2([128, 1024], F32)
        STS = [p2([128, 1, 512], F32) for _ in range(3)]
        ST_P = STS[0]
        grp_off = o2[0]
        GRP = [p2([128, 4, 128], F32) for _ in range(4)]
        MAKT, NRBT, NRKT, ZC = GRP
        PA, PTA, PB, PTB, ZCB = [p2([128, 4, 128], BF16) for _ in range(5)]
        ST_SS = [p2([128, 2, 512], F32, at=grp_off + i * 4096) for i in range(2)]
        PTMP_OFF = o2[0]
        o2[0] += 4096
        G0T = p2([128, 2, 2, 128], F32)
        H0 = p2([128, 2, 2, 64], F32)
        RQ4 = p2([128, 2, 2, 2, 128], F32)
        TMPS_2 = sub(tmps2_off, [128, 2, 64], F32)
        G0T_2 = sub(g0t2_off, [128, 2, 2, 128], F32)
        LXG = p2([128, 1024], F32)
        LXB = p2([128, 1024], F32)
        scr = o2[0]
        o2[0] += 20480
        assert o2[0] <= SB_END
        T1 = p2([128, 2, 512], F32, at=scr)
        T2 = p2([128, 2, 512], F32, at=scr + 4096)
        TMPb = p2([128, 2, 512], BF16, at=scr + 8192)
        TY = p2([128, 2, 512], BF16, at=scr + 10240)
        YT = p2([128, 16, 64], F32, at=scr + 12288)
        SQT = p2([128, 16, 64], F32, at=scr + 16384)
        SGT = p2([128, 1024], F32, at=scr + 12288)
        ATOK = p2([128, 1024], F32, at=scr)
        BTOK = p2([128, 1024], F32, at=scr + 4096)
        KTOK = p2([128, 1024], F32, at=scr + 8192)
        SN = p2([128, 1024], F32, at=scr + 12288)
        SO = p2([128, 1024], F32, at=scr + 16384)

        def fm_tiles(nt):
            d = {}
            names = ["R", "WDEC", "KMOD", "NKK", "KKA"]
            for i, nm in enumerate(names):
                d[nm] = sub(fm_off + i * 4096, [128, 8, nt], F32)
            d["MIX"] = [sub(fm_off + 5 * 4096 + i * 2048, [128, 8, nt], BF16) for i in range(2)]
            for i, nm in enumerate(["XX", "KRAW", "AA", "TA", "TB"]):
                d[nm] = sub(scr + i * 4096, [128, 8, nt], F32)
            d["ZFM"] = sub(scr + 8192, [128, 8, nt], BF16)
            return d

        op('dve', 'memset', ap=BLKf[:], constant=0.0)
        op('dve', 'memset', ap=BLKf[0:64, 0:64], constant=1.0)
        op('dve', 'memset', ap=BLKf[64:128, 64:128], constant=1.0)
        op('dve', 'tensor_copy', out=BLKb[:], in_=BLKf[:])
        op('dve', 'memset', ap=IND2[:], constant=0.0)
        op('dve', 'memset', ap=IND2[0:64, 0:1], constant=1.0)
        op('dve', 'memset', ap=IND2[64:128, 1:2], constant=1.0)
        for h2 in range(2):
            op('dve', 'memset', ap=EY[h2][:], constant=0.0)
            op('dve', 'memset', ap=EY[h2][h2 * 64:(h2 + 1) * 64, 127:128], constant=1.0)
        op('dve', 'memset', ap=CARRY[:], constant=0.0)
        op('dve', 'memset', ap=ST_P[:], constant=0.0)
        for msk, cm, pat, cmp_ in [(M_SU, -1, 1, ALU.is_gt), (M_SL, 1, -1, ALU.is_gt), (M_UI, -1, 1, ALU.is_ge)]:
            op('pool', 'memset', ap=msk[:], constant=1.0)
            op('pool', 'affine_select', out=msk[:], in_=msk[:], pattern=[[pat, 128]], compare_op=cmp_, fill=0.0, base=0, channel_multiplier=cm)
            op('pool', 'memset', ap=msk[0:64, 64:128], constant=0.0)
            op('pool', 'memset', ap=msk[64:128, 0:64], constant=0.0)
        stbase = [0]
        op('pool', 'memset', ap=G0T[:], constant=0.0)
        op('pool', 'memset', ap=RQ4[:], constant=0.0)
        op('sp', 'dma_start', out=LXG[:], in_=bcast_rows(W["b_lnx_g"][j:j + 1, :], 128))
        op('sp', 'dma_start', out=LXB[:], in_=bcast_rows(W["b_lnx_b"][j:j + 1, :], 128))
        op('sp', 'dma_start', out=SN[0:16, :], in_=shift_in)
        for hh in range(2):
            b = bank()
            for q in range(4):
                op('pe', 'transpose', out=PS[:, b, q * 16:(q + 1) * 16], in_=SN[0:16, (hh * 4 + q) * 128:(hh * 4 + q + 1) * 128],
                   identity=ident[0:16, 0:16])
            op('dve', 'tensor_copy', out=SHIFTS[:, hh * 4:(hh + 1) * 4, :, 0], in_=PS[:, b, 0:64].rearrange("p (c b) -> p c b", c=4))

        wr = W["b_w_rkv"][j, 0]; wk = W["b_w_rkv"][j, 1]; wv = W["b_w_rkv"][j, 2]
        mixc = [0]
        YB = [0, 1]
        SAB = [2, 3]
        VBB = [4, 5]
        selc = [0]

        def vc8(key):
            return VEC[:, VC[key]:VC[key] + 8]

        def b_tile(t0, nt, samp, first_tile, last_prompt):
            F = fm_tiles(nt)
            XX, KRAW, AA, TA, TB = F["XX"], F["KRAW"], F["AA"], F["TA"], F["TB"]
            R_, WDEC, KMOD, NKK, KKA, ZFM = F["R"], F["WDEC"], F["KMOD"], F["NKK"], F["KKA"], F["ZFM"]
            X = XF[:, :, t0:t0 + nt]

            def bc8(key, i0=0):
                return VEC[:, VC[key] + i0:VC[key] + i0 + 8].unsqueeze(2).to_broadcast([128, 8, nt])
            if not samp:
                op('dve', 'tensor_tensor', out=XX[:, :, 1:nt], in0=XF[:, :, t0:t0 + nt - 1], in1=XF[:, :, t0 + 1:t0 + nt], op=ALU.subtract)
                op('dve', 'tensor_tensor', out=XX[:, :, 0:1], in0=CARRY[:], in1=XF[:, :, t0:t0 + 1], op=ALU.subtract)
                op('act', 'activation', out=CARRY[:], in_=XF[:, :, t0 + nt - 1:t0 + nt], func=AF.Copy)
                if last_prompt:
                    b = bank()
                    op('pe', 'transpose', out=PS[0:8, b, 0:128], in_=CARRY[:, :, 0], identity=ident[:])
                    op('dve', 'tensor_copy', out=SO[0:8, 0:128], in_=PS[0:8, b, 0:128])
                    op('sp', 'dma_start', out=o_shift_p.rearrange("o (c p) -> (o c) p", p=128), in_=SO[0:8, 0:128])
            else:
                for c in range(8):
                    xv_ = XF[:, c, t0:t0 + nt].rearrange("p (b t) -> p b t", t=4)
                    xxv = XX[:, c, :].rearrange("p (b t) -> p b t", t=4)
                    op('dve', 'tensor_tensor', out=xxv[:, :, 1:4], in0=xv_[:, :, 0:3], in1=xv_[:, :, 1:4], op=ALU.subtract)
                    op('dve', 'tensor_tensor', out=xxv[:, :, 0:1], in0=SHIFTS[:, c, :, :], in1=xv_[:, :, 0:1], op=ALU.subtract)
                    op('dve', 'tensor_copy', out=CT2[:, c, :], in_=xv_[:, :, 3])
                for hh in range(2):
                    b = bank()
                    for q in range(4):
                        op('pe', 'transpose', out=PS[0:16, b, q * 128:(q + 1) * 128], in_=CT2[:, hh * 4 + q, :], identity=ident[:])
                    op('dve', 'tensor_copy', out=SO[0:16, hh * 512:(hh + 1) * 512], in_=PS[0:16, b, :])
                op('sp', 'dma_start', out=o_shift_s, in_=SO[0:16, :])

            PTMP = sub(PTMP_OFF, [128, 8, nt], F32)

            def mix(i):
                mx = F["MIX"][mixc[0] % 2]; mixc[0] += 1
                op('dve', 'tensor_tensor', out=PTMP[:], in0=XX[:], in1=bc8("b_mu", i * 8), op=ALU.mult)
                op('dve', 'tensor_tensor', out=mx[:], in0=PTMP[:], in1=X, op=ALU.add)
                return mx

            def proj_fm(mx, wmat, epi, kname):
                def body(mb, cw):
                    for mm_ in range(2):
                        m = mb * 2 + mm_
                        b = bank()
                        for kc in range(8):
                            op('pe', 'matmul', out=PS[:, b, 0:nt], lhsT=cw[:, kc, mm_ * 128:(mm_ + 1) * 128], rhs=mx[:, kc, :],
                               start=(kc == 0), stop=(kc == 7))
                        epi(m, PS[:, b, 0:nt])
                pipelined([[wmat[:, mb * 256:(mb + 1) * 256]] for mb in range(4)], 1024, body, keys=[("b", kname, mb) for mb in range(4)])

            mx = mix(0)
            proj_fm(mx, wr, lambda m, ps: op('act', 'activation', out=R_[:, m, :], in_=ps, func=AF.Copy), "r")
            mx = mix(1)
            cw = wload([W["b_w1"][j]], 1024, key=("b", "b_w1"))
            b = bank()
            for kc in range(8):
                op('pe', 'matmul', out=PS[0:64, b, 0:nt], lhsT=cw[:, kc, 0:64], rhs=mx[:, kc, :], start=(kc == 0), stop=(kc == 7))
            op('act', 'activation', out=LOR[0:64, 0, 0:nt], in_=PS[0:64, b, 0:nt], func=AF.Tanh)
            cw = wload([W["b_w2"][j]], 64, key=("b", "b_w2"))
            for m in range(8):
                b = bank()
                op('pe', 'matmul', out=PS[:, b, 0:nt], lhsT=cw[0:64, 0, m * 128:(m + 1) * 128], rhs=LOR[0:64, 0, 0:nt], start=True, stop=True)
                op('act', 'activation', out=WDEC[:, m, :], in_=PS[:, b, 0:nt], func=AF.Sigmoid, bias=vcol("b_w0", m), scale=1.0)
            if samp:
                op('act', 'activation', out=WDEC[:], in_=WDEC[:], func=AF.Exp, scale=NEG_EM05)
            mx = mix(2)
            proj_fm(mx, wk, lambda m, ps: op('act', 'activation', out=KRAW[:, m, :], in_=ps, func=AF.Copy), "k")
            mx = mix(3)

            def body_v(q, cw):
                b = bank()
                for kc in range(8):
                    op('pe', 'matmul', out=PS[0:nt, b, 0:256], lhsT=mx[:, kc, :], rhs=cw[:, kc, :], start=(kc == 0), stop=(kc == 7))
                op('act', 'activation', out=VTOK[0:nt, q * 256:(q + 1) * 256], in_=PS[0:nt, b, 0:256], func=AF.Copy)
            pipelined([[wv[:, q * 256:(q + 1) * 256]] for q in range(4)], 1024, body_v, keys=[("b", "v", q) for q in range(4)])
            if samp:
                op('act', 'activation', out=VTOKB[0:nt, :], in_=VTOK[0:nt, :], func=AF.Copy)
            mx = mix(4)
            cw = wload([W["b_a1"][j]], 1024, key=("b", "b_a1"))
            b = bank()
            for kc in range(8):
                op('pe', 'matmul', out=PS[0:64, b, 0:nt], lhsT=cw[:, kc, 0:64], rhs=mx[:, kc, :], start=(kc == 0), stop=(kc == 7))
            op('act', 'activation', out=LOR[0:64, 1, 0:nt], in_=PS[0:64, b, 0:nt], func=AF.Copy)
            cw = wload([W["b_a2"][j]], 64, key=("b", "b_a2"))
            for m in range(8):
                b = bank()
                op('pe', 'matmul', out=PS[:, b, 0:nt], lhsT=cw[0:64, 0, m * 128:(m + 1) * 128], rhs=LOR[0:64, 1, 0:nt], start=True, stop=True)
                op('act', 'activation', out=AA[:, m, :], in_=PS[:, b, 0:nt], func=AF.Sigmoid, bias=vcol("b_a0", m), scale=1.0)
            mx = mix(5)
            cw = wload([W["b_g1"][j]], 1024, key=("b", "b_g1"))
            b = bank()
            for kc in range(8):
                op('pe', 'matmul', out=PS[:, b, 0:nt], lhsT=cw[:, kc, 0:128], rhs=mx[:, kc, :], start=(kc == 0), stop=(kc == 7))
            op('act', 'activation', out=LOR[:, 2, 0:nt], in_=PS[:, b, 0:nt], func=AF.Sigmoid)
            cw = wload([W["b_g2"][j]], 128, key=("b", "b_g2"))
            for q in range(2):
                b = bank()
                op('pe', 'matmul', out=PS[0:nt, b, :], lhsT=LOR[:, 2, 0:nt], rhs=cw[:, 0, q * 512:(q + 1) * 512], start=True, stop=True)
                op('act', 'activation', out=GTOK[0:nt, q * 512:(q + 1) * 512], in_=PS[0:nt, b, :], func=AF.Copy)

            op('dve', 'tensor_tensor', out=TA[:], in0=KRAW[:], in1=bc8("b_k_k"), op=ALU.mult)
            op('dve', 'tensor_tensor', out=TB[:], in0=TA[:], in1=TA[:], op=ALU.mult)
            taf = TA[:].rearrange("p c t -> p (c t)")
            tbf = TB[:].rearrange("p c t -> p (c t)")
            nflat = 8 * nt
            for q0 in range(0, nflat, 512):
                b = bank()
                op('pe', 'matmul', out=PS[:, b, :], lhsT=BLKf[:], rhs=tbf[:, q0:q0 + 512], start=True, stop=True)
                op('act', 'activation', out=tbf[:, q0:q0 + 512], in_=PS[:, b, :], func=AF.Sqrt)
            op('dve', 'tensor_scalar', out=TB[:], in0=TB[:], scalar1=1e-12, scalar2=None, op0=ALU.max)
            op('dve', 'reciprocal', out=TB[:], in_=TB[:])
            op('dve', 'tensor_tensor', out=TA[:], in0=TA[:], in1=TB[:], op=ALU.mult)
            op('act', 'activation', out=NKK[:], in_=TA[:], func=AF.Copy, scale=-1.0)
            op('dve', 'tensor_tensor', out=KKA[:], in0=TA[:], in1=AA[:], op=ALU.mult)
            op('dve', 'scalar_tensor_tensor', out=TB[:], in0=AA[:], scalar=-1.0, in1=bc8("b_k_a"), op0=ALU.add, op1=ALU.mult)
            op('dve', 'scalar_tensor_tensor', out=KMOD[:], in0=TB[:], scalar=1.0, in1=KRAW[:], op0=ALU.add, op1=ALU.mult)
            op('dve', 'tensor_tensor', out=TB[:], in0=R_[:], in1=KMOD[:], op=ALU.mult)
            op('dve', 'tensor_tensor', out=TB[:], in0=TB[:], in1=bc8("b_r_k"), op=ALU.mult)
            b = bank()
            for c in range(8):
                op('pe', 'matmul', out=PS[0:nt, b, c * 2:(c + 1) * 2], lhsT=TB[:, c, :], rhs=IND2[:], start=True, stop=True)
            op('dve', 'tensor_copy', out=RS[0:nt, :, 0], in_=PS[0:nt, b, 0:16])

            for bb_ in range(6):
                reserved.add(bb_)
            stepc = [0]

            def scan_step(toks, ST, nb, first, last, t_local, b0):
                k = stepc[0]; stepc[0] += 1
                S = ST[:, 0:nb, :].rearrange("p b (c v) -> p b c v", v=64)

                def bcv(T):
                    if not samp:
                        return T[:, :, toks[0]:toks[0] + 1].unsqueeze(1).to_broadcast([128, 1, 8, 64])
                    v4 = T[:, :, :].rearrange("p c (b t) -> p c b t", t=4)[:, :, b0:b0 + nb, t_local:t_local + 1]
                    return v4.rearrange("p c b o -> p b c o").to_broadcast([128, nb, 8, 64])

                def v4(tile_):
                    return tile_[:, 0:nb, :].rearrange("p b (c v) -> p b c v", v=64)
                if nb == 1:
                    sa = [SAB[k % 2]]; vb = [VBB[k % 2]]
                    sa_ps = PS[:, sa[0]:sa[0] + 1, :]
                    vb_ps = PS[:, vb[0]:vb[0] + 1, :]
                else:
                    sa = SAB; vb = VBB
                    sa_ps = PS[:, 2:4, :]
                    vb_ps = PS[:, 4:6, :]
                op('dve', 'tensor_tensor', out=v4(TMPb), in0=S, in1=bcv(NKK), op=ALU.mult)
                for bi in range(nb):
                    op('pe', 'matmul', out=PS[:, sa[bi], :], lhsT=BLKb[:], rhs=TMPb[:, bi, :], start=True, stop=True)
                for bi in range(nb):
                    sl = SEL[selc[0] % 4]; selc[0] += 1
                    op('pool', 'tensor_copy', out=sl[0:nt, :], in_=identb[0:nt, toks[bi]:toks[bi] + 1].to_broadcast([nt, 64]))
                    vsrc = VTOKB[0:nt, :].rearrange("t (c h v) -> t c h v", h=2, v=64)
                    for h2 in range(2):
                        op('pe', 'matmul', out=PS[h2 * 64:(h2 + 1) * 64, vb[bi], :], lhsT=sl[0:nt, :], rhs=vsrc[:, :, h2, :],
                           start=True, stop=True)
                op('dve', 'tensor_tensor', out=v4(T1), in0=sa_ps.rearrange("p b (c v) -> p b c v", v=64), in1=bcv(KKA), op=ALU.mult)
                op('dve', 'tensor_tensor', out=S, in0=S, in1=bcv(WDEC), op=ALU.mult)
                op('dve', 'tensor_tensor', out=S, in0=S, in1=v4(T1), op=ALU.add)
                op('dve', 'tensor_tensor', out=v4(T2), in0=vb_ps.rearrange("p b (c v) -> p b c v", v=64), in1=bcv(KMOD), op=ALU.mult)
                op('dve', 'tensor_tensor', out=S, in0=S, in1=v4(T2), op=ALU.add)
                op('dve', 'tensor_tensor', out=v4(TY), in0=S, in1=bcv(R_), op=ALU.mult)
                for bi in range(nb):
                    for h2 in range(2):
                        op('pe', 'matmul', out=PS[0:nt, YB[h2], :], lhsT=EY[h2][:, 127 - toks[bi]:127 - toks[bi] + nt], rhs=TY[:, bi, :],
                           start=(first and bi == 0), stop=(last and bi == nb - 1))


            def chunked():
                SG = WDEC
                for hh in range(2):
                    b = bank()
                    for q in range(4):
                        op('pe', 'transpose', out=PS[:, b, q * 128:(q + 1) * 128], in_=SG[:, hh * 4 + q, :], identity=ident[:])
                    op('act', 'activation', out=SGT[:, hh * 512:(hh + 1) * 512], in_=PS[:, b, :], func=AF.Copy)
                EIN, EINV, EEX = XX, KRAW, AA
                for hh in range(2):
                    bi = bank(); be = bank()
                    for q in range(4):
                        c = hh * 4 + q
                        op('pe', 'matmul', out=PS[:, bi, q * 128:(q + 1) * 128], lhsT=SGT[:, c * 128:(c + 1) * 128], rhs=M_UI[:], start=True, stop=True)
                        op('pe', 'matmul', out=PS[:, be, q * 128:(q + 1) * 128], lhsT=SGT[:, c * 128:(c + 1) * 128], rhs=M_SU[:], start=True, stop=True)
                    pvi = PS[:, bi, :].rearrange("p (a t) -> p a t", a=4)
                    pve = PS[:, be, :].rearrange("p (a t) -> p a t", a=4)
                    op('act', 'activation', out=EIN[:, hh * 4:(hh + 1) * 4, :], in_=pvi, func=AF.Exp, scale=NEG_EM05)
                    op('act', 'activation', out=EINV[:, hh * 4:(hh + 1) * 4, :], in_=pvi, func=AF.Exp, scale=-NEG_EM05)
                    op('act', 'activation', out=EEX[:, hh * 4:(hh + 1) * 4, :], in_=pve, func=AF.Exp, scale=NEG_EM05)
                op('act', 'activation', out=WC[:], in_=EIN[:].rearrange("p c (q t) -> p c q t", t=64)[:, :, :, 63:64], func=AF.Copy)
                At, Bt, Kt, Rt = NKK, KKA, KMOD, R_
                op('dve', 'tensor_tensor', out=At[:], in0=NKK[:], in1=EEX[:], op=ALU.mult)
                op('dve', 'tensor_tensor', out=Bt[:], in0=KKA[:], in1=EINV[:], op=ALU.mult)
                op('dve', 'tensor_tensor', out=Kt[:], in0=KMOD[:], in1=EINV[:], op=ALU.mult)
                op('dve', 'tensor_tensor', out=Rt[:], in0=R_[:], in1=EIN[:], op=ALU.mult)
                for src, dst in [(At, ATOK), (Bt, BTOK), (Kt, KTOK)]:
                    for hh in range(2):
                        b = bank()
                        for q in range(4):
                            op('pe', 'transpose', out=PS[:, b, q * 128:(q + 1) * 128], in_=src[:, hh * 4 + q, :], identity=ident[:])
                        op('act', 'activation', out=dst[:, hh * 512:(hh + 1) * 512], in_=PS[:, b, :], func=AF.Copy)
                base = stbase[0]
                T1s = dict(PA=PA, PTA=PTA, PB=PB, PTB=PTB, MAKT=MAKT, NRBT=NRBT, NRKT=NRKT, ZC=ZC, ZCB=ZCB, G0T=G0T, H0=H0, RQ4=RQ4, TMPS=TMPS)
                s4 = scr + 16384
                wd = fm_off + 1 * 4096
                mxo = fm_off + 5 * 4096
                T2s = dict(MAKT=sub(s4, [128, 4, 128], F32), NRBT=sub(s4 + 2048, [128, 4, 128], F32),
                           NRKT=sub(wd, [128, 4, 128], F32), ZC=sub(wd + 2048, [128, 4, 128], F32),
                           PA=sub(mxo, [128, 4, 128], BF16), PTA=sub(mxo + 1024, [128, 4, 128], BF16),
                           PB=sub(mxo + 2048, [128, 4, 128], BF16), PTB=sub(mxo + 3072, [128, 4, 128], BF16),
                           ZCB=sub(VTOKB_OFF, [128, 4, 128], BF16), H0=sub(VTOKB_OFF + 1024, [128, 2, 2, 64], F32),
                           RQ4=sub(PTMP_OFF, [128, 2, 2, 2, 128], F32), G0T=G0T_2, TMPS=TMPS_2)
                op('pool', 'memset', ap=T2s['RQ4'][:], constant=0.0)
                op('pool', 'memset', ap=T2s['G0T'][:], constant=0.0)

                def grp_gen(g, T):
                    PA_, PTA_, PB_, PTB_ = T['PA'], T['PTA'], T['PB'], T['PTB']
                    MAKT_, NRBT_, NRKT_, ZC_, ZCB_ = T['MAKT'], T['NRBT'], T['NRKT'], T['ZC'], T['ZCB']
                    G0T_, H0_, RQ4_, TMPS_ = T['G0T'], T['H0'], T['RQ4'], T['TMPS']
                    heads = [(2 * g + cc, h2) for cc in range(2) for h2 in range(2)]

                    def fm(Tl, c, h2):
                        return Tl[h2 * 64:(h2 + 1) * 64, c, :]

                    def col(hl):
                        return g * 256 + hl * 64

                    def v4(ps_ap, a):
                        return ps_ap.rearrange("p (a t) -> p a t", a=a)
                    for dst, lt, rt, msk in [(PA_, Bt, At, M_SU), (PTA_, At, Bt, M_SL), (MAKT_, Kt, At, M_SU), (NRBT_, Bt, Rt, M_UI), (NRKT_, Kt, Rt, M_UI)]:
                        for h2 in range(2):
                            b = bank()
                            for cc in range(2):
                                c = 2 * g + cc
                                op('pe', 'matmul', out=PS[:, b, cc * 128:(cc + 1) * 128], lhsT=fm(lt, c, h2), rhs=fm(rt, c, h2), start=True, stop=True)
                            dv = dst[:].rearrange("p (cc h) t -> p cc h t", h=2)[:, :, h2, :]
                            op('dve', 'tensor_tensor', out=dv, in0=v4(PS[:, b, 0:256], 2), in1=msk[:].unsqueeze(1).to_broadcast([128, 2, 128]), op=ALU.mult)
                        yield
                    for cc in range(2):
                        hs = slice(2 * cc, 2 * cc + 2)
                        b = bank()
                        for hl in range(2 * cc, 2 * cc + 2):
                            op('pe', 'matmul', out=PS[:, b, (hl % 2) * 64:(hl % 2 + 1) * 64], lhsT=MAKT_[:, hl, :], rhs=VTOK[:, col(hl):col(hl) + 64], start=True, stop=True)
                        op('act', 'activation', out=ZC_[:, hs, 64:128], in_=v4(PS[:, b, 0:128], 2), func=AF.Copy)
                        op('dve', 'tensor_copy', out=ZC_[:, hs, 0:64], in_=ATOK[:, g * 256 + cc * 128:g * 256 + (cc + 1) * 128].rearrange("p (a t) -> p a t", a=2))
                        op('act', 'activation', out=ZCB_[:, hs, :], in_=ZC_[:, hs, :], func=AF.Copy)
                    yield
                    Pc, PTc, Pn, PTn = PA_, PTA_, PB_, PTB_
                    for s_ in range(6):
                        for cc in range(2):
                            hs = slice(2 * cc, 2 * cc + 2)
                            b = bank()
                            for hl in range(2 * cc, 2 * cc + 2):
                                op('pe', 'matmul', out=PS[:, b, (hl % 2) * 128:(hl % 2 + 1) * 128], lhsT=Pc[:, hl, :], rhs=ZCB_[:, hl, :], start=True, stop=True)
                            op('dve', 'tensor_tensor', out=ZC_[:, hs, :], in0=ZC_[:, hs, :], in1=v4(PS[:, b, 0:256], 2), op=ALU.add)
                            if s_ < 5:
                                op('act', 'activation', out=ZCB_[:, hs, :], in_=ZC_[:, hs, :], func=AF.Copy)
                                b1 = bank(); b2 = bank()
                                for hl in range(2 * cc, 2 * cc + 2):
                                    op('pe', 'matmul', out=PS[:, b1, (hl % 2) * 128:(hl % 2 + 1) * 128], lhsT=PTc[:, hl, :], rhs=Pc[:, hl, :], start=True, stop=True)
                                for hl in range(2 * cc, 2 * cc + 2):
                                    op('pe', 'matmul', out=PS[:, b2, (hl % 2) * 128:(hl % 2 + 1) * 128], lhsT=Pc[:, hl, :], rhs=PTc[:, hl, :], start=True, stop=True)
                                op('act', 'activation', out=Pn[:, hs, :], in_=v4(PS[:, b1, 0:256], 2), func=AF.Copy)
                                op('dve', 'tensor_copy', out=PTn[:, hs, :], in_=v4(PS[:, b2, 0:256], 2))
                            yield
                        Pc, PTc, Pn, PTn = Pn, PTn, Pc, PTc
                    b = bank()
                    for hl, (c, h2) in enumerate(heads):
                        cc = c - 2 * g
                        op('pe', 'matmul', out=PS[h2 * 64:(h2 + 1) * 64, b, cc * 128:(cc + 1) * 128], lhsT=ZC_[:, hl, 0:64], rhs=NRBT_[:, hl, :], start=True, stop=True)
                    for h2 in range(2):
                        for q in range(2):
                            hp = slice(h2 * 64, (h2 + 1) * 64)
                            op('dve', 'tensor_tensor', out=RQ4_[hp, q, :, h2, q * 64:(q + 1) * 64], in0=Rt[hp, 2 * g:2 * g + 2, q * 64:(q + 1) * 64],
                               in1=v4(PS[hp, b, 0:256], 2)[:, :, q * 64:(q + 1) * 64], op=ALU.add)
                    yield
                    for q in range(2):
                        r0, r1 = q * 64, (q + 1) * 64
                        bg = bank(); bh = bank()
                        for hl, (c, h2) in enumerate(heads):
                            cc = c - 2 * g
                            hp = slice(h2 * 64, (h2 + 1) * 64)
                            op('pe', 'matmul', out=PS[hp, bg, cc * 64:(cc + 1) * 64], lhsT=ZC_[r0:r1, hl, 0:64], rhs=BTOK[r0:r1, col(hl):col(hl) + 64], start=True, stop=True)
                            op('pe', 'matmul', out=PS[hp, bh, cc * 64:(cc + 1) * 64], lhsT=BTOK[r0:r1, col(hl):col(hl) + 64], rhs=ZC_[r0:r1, hl, 64:128], start=True, stop=False)
                            op('pe', 'matmul', out=PS[hp, bh, cc * 64:(cc + 1) * 64], lhsT=KTOK[r0:r1, col(hl):col(hl) + 64], rhs=VTOK[r0:r1, col(hl):col(hl) + 64], start=False, stop=True)
                        for h2 in range(2):
                            hp = slice(h2 * 64, (h2 + 1) * 64)
                            op('act', 'activation', out=G0T_[hp, :, q, h2 * 64:(h2 + 1) * 64], in_=v4(PS[hp, bg, 0:128], 2), func=AF.Copy)
                        op('act', 'activation', out=H0_[:, :, q, :], in_=v4(PS[:, bh, 0:128], 2), func=AF.Copy)
                        yield
                    for q in range(2):
                        Sc = STS[(base + q) % 3][:, 0, :].rearrange("p (c v) -> p c v", v=64)
                        Sn = STS[(base + q + 1) % 3][:, 0, :].rearrange("p (c v) -> p c v", v=64)
                        b = bank()
                        for cc in range(2):
                            c = 2 * g + cc
                            op('pe', 'matmul', out=PS[:, b, cc * 64:(cc + 1) * 64], lhsT=G0T_[:, cc, q, :], rhs=Sc[:, c, :], start=True, stop=True)
                        op('dve', 'tensor_tensor', out=TMPS_[:], in0=v4(PS[:, b, 0:128], 2), in1=Sc[:, 2 * g:2 * g + 2, :], op=ALU.add)
                        op('dve', 'tensor_tensor', out=TMPS_[:], in0=TMPS_[:], in1=H0_[:, :, q, :], op=ALU.add)
                        op('dve', 'tensor_tensor', out=Sn[:, 2 * g:2 * g + 2, :], in0=TMPS_[:], in1=WC[:, 2 * g:2 * g + 2, q, :].to_broadcast([128, 2, 64]), op=ALU.mult)
                        yield
                    by = bank()
                    for hl, (c, h2) in enumerate(heads):
                        cc = c - 2 * g
                        o0 = hl * 64
                        op('pe', 'matmul', out=PS[:, by, o0:o0 + 64], lhsT=NRBT_[:, hl, :], rhs=ZC_[:, hl, 64:128], start=True, stop=False)
                        op('pe', 'matmul', out=PS[:, by, o0:o0 + 64], lhsT=NRKT_[:, hl, :], rhs=VTOK[:, col(hl):col(hl) + 64], start=False, stop=False)
                        for q in range(2):
                            Sq = STS[(base + q) % 3][:, 0, :].rearrange("p (c v) -> p c v", v=64)
                            op('pe', 'matmul', out=PS[:, by, o0:o0 + 64], lhsT=RQ4_[:, q, cc, h2, :], rhs=Sq[:, c, :], start=False, stop=(q == 1))
                    op('act', 'activation', out=YT[:, g * 4:(g + 1) * 4, :], in_=v4(PS[:, by, 0:256], 4), func=AF.Copy)
                    yield

                def interleave2(ga, gb):
                    gens = [ga, gb]
                    while gens:
                        for g_ in list(gens):
                            try:
                                next(g_)
                            except StopIteration:
                                gens.remove(g_)

                interleave2(grp_gen(0, T1s), grp_gen(1, T2s))
                interleave2(grp_gen(2, T1s), grp_gen(3, T2s))
                stbase[0] = (base + 2) % 3

            def store_state(ST, bi, dst):
                for hh in range(2):
                    b = bank()
                    for q in range(4):
                        c = hh * 4 + q
                        op('pe', 'transpose', out=PS[0:64, b, q * 128:(q + 1) * 128], in_=ST[:, bi, c * 64:(c + 1) * 64], identity=ident[:])
                    op('act', 'activation', out=SO[0:64, hh * 512:(hh + 1) * 512], in_=PS[0:64, b, :], func=AF.Copy)
                op('sp', 'dma_start', out=dst.rearrange("h v k -> v h k"), in_=SO[0:64, :].rearrange("v (h k) -> v h k", k=64))

            if not samp:
                for bb_ in range(6):
                    reserved.discard(bb_)
                chunked()
                if last_prompt:
                    store_state(STS[stbase[0]], 0, o_wkv_p)
            else:
                def load_states(g):
                    for bi in range(2):
                        op('sp', 'dma_start', out=SN[0:64, :].rearrange("v (h k) -> v h k", k=64), in_=wkv_in[g * 2 + bi].rearrange("h v k -> v h k"))
                        b = bank()
                        for c in range(8):
                            op('pe', 'transpose', out=PS[:, b, c * 64:(c + 1) * 64], in_=SN[0:64, c * 128:(c + 1) * 128], identity=ident[0:64, 0:64])
                        op('act', 'activation', out=ST_SS[g % 2][:, bi, :], in_=PS[:, b, :], func=AF.Copy)
                load_states(0)
                for g in range(8):
                    b0 = g * 2
                    ST_S = ST_SS[g % 2]
                    if g + 1 < 8:
                        load_states(g + 1)
                    for t in range(4):
                        toks = [(b0 + bi) * 4 + t for bi in range(2)]
                        scan_step(toks, ST_S, 2, (g == 0 and t == 0), (g == 7 and t == 3), t, b0)
                    for bi in range(2):
                        store_state(ST_S, bi, o_wkv_s[b0 + bi])
            for bb_ in range(6):
                reserved.discard(bb_)

            if samp:
                op('act', 'activation', out=YT[0:nt, :, :].rearrange("t (c h) v -> t c h v", h=2)[:, :, 0, :],
                   in_=PS[0:nt, YB[0], :].rearrange("t (c v) -> t c v", v=64), func=AF.Copy)
                op('dve', 'tensor_copy', out=YT[0:nt, :, :].rearrange("t (c h) v -> t c h v", h=2)[:, :, 1, :],
                   in_=PS[0:nt, YB[1], :].rearrange("t (c v) -> t c v", v=64))
            op('dve', 'tensor_reduce', out=GS[0:nt, 0, :, 0], in_=YT[0:nt, :, :], axis=AX.X, op=ALU.add)
            op('act', 'activation', out=SQT[0:nt, :, :], in_=YT[0:nt, :, :], func=AF.Square)
            op('dve', 'tensor_reduce', out=GS[0:nt, 1, :, 0], in_=SQT[0:nt, :, :], axis=AX.X, op=ALU.add)
            op('dve', 'tensor_scalar', out=GS[0:nt, 2, :, :], in0=GS[0:nt, 0, :, :], scalar1=1.0 / 64, scalar2=None, op0=ALU.mult)
            op('dve', 'tensor_tensor', out=GS[0:nt, 3, :, :], in0=GS[0:nt, 2, :, :], in1=GS[0:nt, 2, :, :], op=ALU.mult)
            op('dve', 'scalar_tensor_tensor', out=GS[0:nt, 1, :, :], in0=GS[0:nt, 1, :, :], scalar=1.0 / 64, in1=GS[0:nt, 3, :, :],
               op0=ALU.mult, op1=ALU.subtract)
            op('act', 'activation', out=GS[0:nt, 1, :, :], in_=GS[0:nt, 1, :, :], func=AF.Sqrt, bias=GN_EPS, scale=1.0)
            op('dve', 'reciprocal', out=GS[0:nt, 1, :, :], in_=GS[0:nt, 1, :, :])
            op('dve', 'tensor_tensor', out=YT[0:nt], in0=YT[0:nt], in1=GS[0:nt, 2, :, :].to_broadcast([nt, 16, 64]), op=ALU.subtract)
            op('dve', 'tensor_tensor', out=YT[0:nt], in0=YT[0:nt], in1=GS[0:nt, 1, :, :].to_broadcast([nt, 16, 64]), op=ALU.mult)
            ytf = YT[0:nt, :, :].rearrange("t h v -> t (h v)")
            op('dve', 'tensor_tensor', out=ytf, in0=ytf, in1=LXG[0:nt, :], op=ALU.mult)
            op('dve', 'tensor_tensor', out=ytf, in0=ytf, in1=LXB[0:nt, :], op=ALU.add)
            op('dve', 'tensor_tensor', out=SQT[0:nt], in0=VTOK[0:nt, :].rearrange("t (h v) -> t h v", v=64),
               in1=RS[0:nt, :, :].to_broadcast([nt, 16, 64]), op=ALU.mult)
            op('dve', 'tensor_tensor', out=YT[0:nt], in0=YT[0:nt], in1=SQT[0:nt], op=ALU.add)
            op('dve', 'tensor_tensor', out=ytf, in0=ytf, in1=GTOK[0:nt, :], op=ALU.mult)
            for hh in range(2):
                b = bank()
                for q in range(4):
                    c = hh * 4 + q
                    op('pe', 'transpose', out=PS[:, b, q * nt:(q + 1) * nt], in_=ytf[:, c * 128:(c + 1) * 128], identity=ident[0:nt, 0:nt])
                op('act', 'activation', out=ZFM[:, hh * 4:(hh + 1) * 4, :], in_=PS[:, b, 0:4 * nt].rearrange("p (a t) -> p a t", a=4), func=AF.Copy)

            def body_o(mb, cw):
                for mm_ in range(2):
                    m = mb * 2 + mm_
                    b = bank()
                    for kc in range(8):
                        op('pe', 'matmul', out=PS[:, b, 0:nt], lhsT=cw[:, kc, mm_ * 128:(mm_ + 1) * 128], rhs=ZFM[:, kc, :],
                           start=(kc == 0), stop=(kc == 7))
                    op('dve', 'scalar_tensor_tensor', out=XF[:, m, t0:t0 + nt], in0=XF[:, m, t0:t0 + nt], scalar=ALPHA,
                       in1=PS[:, b, 0:nt], op0=ALU.mult, op1=ALU.add)
            pipelined([[W["b_w_o"][j][:, mb * 256:(mb + 1) * 256]] for mb in range(4)], 1024, body_o, keys=[("b", "o", mb) for mb in range(4)])

        for ti, (t0, nt) in enumerate(T128):
            b_tile(t0, nt, t0 >= 2048, ti == 0, ti == 15)

    def warm_cache(kind, j):
        specs = []
        if kind == 0:
            w_in = W["a_w_in"][j]; w_out = W["a_w_out"][j]
            specs += [([w_in[:, 2048 + q * 256:2048 + (q + 1) * 256]], 1024, ("a", j, "v", q)) for q in range(8)]
            specs += [([w_in[:, bi * 256:(bi + 1) * 256]], 1024, ("a", j, "u", bi)) for bi in range(8)]
            specs += [([w_out[:, m * 128:(m + 1) * 128]], 2048, ("a", j, "o", m)) for m in range(8)]
        elif kind == 1:
            wr = W["b_w_rkv"][j, 0]; wk = W["b_w_rkv"][j, 1]; wv = W["b_w_rkv"][j, 2]
            specs += [([wr[:, mb * 256:(mb + 1) * 256]], 1024, ("b", "r", mb)) for mb in range(4)]
            specs += [([W["b_w1"][j]], 1024, ("b", "b_w1")), ([W["b_w2"][j]], 64, ("b", "b_w2"))]
            specs += [([wk[:, mb * 256:(mb + 1) * 256]], 1024, ("b", "k", mb)) for mb in range(4)]
            specs += [([wv[:, q * 256:(q + 1) * 256]], 1024, ("b", "v", q)) for q in range(4)]
            specs += [([W["b_a1"][j]], 1024, ("b", "b_a1")), ([W["b_a2"][j]], 64, ("b", "b_a2")),
                      ([W["b_g1"][j]], 1024, ("b", "b_g1")), ([W["b_g2"][j]], 128, ("b", "b_g2"))]
            specs += [([W["b_w_o"][j][:, mb * 256:(mb + 1) * 256]], 1024, ("b", "o", mb)) for mb in range(4)]
        else:
            w_in = W["c_w_in"][j]; w_out = W["c_w_out"][j]
            specs += [([w_in[:, 1024 + m * 128:1024 + (m + 1) * 128], w_in[:, 2048 + m * 128:2048 + (m + 1) * 128],
                        w_in[:, m * 128:(m + 1) * 128]], 1024, ("c", "in", m)) for m in range(8)]
            specs += [([w_out[:, mb * 256:(mb + 1) * 256]], 1024, ("c", "out", mb)) for mb in range(4)]
        for parts, K, key in specs:
            wload(parts, K, key)
            yield

    def kind_of(l):
        return (l % 3) if kinds is None else kinds[l]

    load_x()
    for l in range(n_layers):
        ffn(l, 0)
        layer_norm(l * 3 + 0, 4 * LN_EPS)
        kind = (l % 3) if kinds is None else kinds[l]
        if stub_mixers:
            mixer_stub()
        elif kind == 0:
            mixer_a(l // 3 if kinds is None else 0)
        elif kind == 1:
            mixer_b(0)
        else:
            mixer_c(0)
        ffn(l, 1, pre_ln=(l * 3 + 1, LN_EPS))
        layer_norm(l * 3 + 2, 4 * LN_EPS)
        ple(l)
    store_y()
    if DRY:
        return REQ
    P.emit()
    return nc, P


def _shard_inputs(inp, c):
    f = lambda a: np.ascontiguousarray(a, dtype=np.float32)
    sl = slice(16 * c, 16 * c + 16)
    m = {}
    m["x_p"] = f(inp["x_prompt"][c])
    m["x_s"] = f(inp["x_sample"][sl].reshape(64, 1024))
    m["wkv_in"] = f(inp["state_b_wkv"][0, sl])
    m["shift_in"] = f(inp["state_b_shift"][0, sl])
    m["conv_in"] = f(inp["state_c_conv"][0, sl])
    m["p_p"] = f(inp["p_prompt"][:, c])
    m["p_s"] = f(inp["p_sample"][:, sl].reshape(4, 64, 256))
    return m


def _weights(inp):
    f = lambda a: np.ascontiguousarray(a, dtype=np.float32)
    w = {}
    for k in ["ln_g", "ln_b", "ffn_w_in", "ffn_w_out", "ple_w_gate", "ple_w_proj", "a_w_in", "a_b_in", "a_ln_g", "a_ln_b",
              "a_b_s", "a_w_out", "b_mu", "b_w_rkv", "b_w0", "b_w1", "b_w2", "b_a0", "b_a1", "b_a2", "b_g1", "b_g2",
              "b_k_k", "b_k_a", "b_lnx_g", "b_lnx_b", "b_w_o", "c_w_in", "c_conv_w", "c_w_out"]:
        w[k] = f(inp[k])
    ws = np.asarray(inp["a_w_s"], dtype=np.float32)
    w["a_w_sT"] = f(np.transpose(ws, (0, 1, 3, 2)))
    blk = np.zeros((2, 8, 64, 64), np.float32)
    for b in range(16):
        blk[:, :, 4 * b:4 * b + 4, 4 * b:4 * b + 4] = np.transpose(ws[:, :, :4, :4], (0, 1, 3, 2))
    w["a_w_sS"] = blk
    w["a_b_sS"] = f(np.tile(np.asarray(inp["a_b_s"], dtype=np.float32)[:, :, :4], (1, 1, 16)))
    w["b_r_k"] = f(np.asarray(inp["b_r_k"]).reshape(1, 1024))
    return w


def kernel(**inp):
    nc, _ = build_program()
    w = _weights(inp)
    in_maps = []
    for c in range(8):
        m = _shard_inputs(inp, c)
        m.update(w)
        in_maps.append(m)
    res = run_bass_kernel_spmd(nc, in_maps, core_ids=list(range(8)))
    R = res.results
    cat = lambda k: [np.asarray(R[c][k]) for c in range(8)]
    y_p = np.stack(cat("y_p"), 0)
    y_s = np.concatenate([a.reshape(16, 4, 1024) for a in cat("y_s")], 0)
    a_v = np.concatenate([a.reshape(2, 16, 4, 2048) for a in cat("o_av")], 1)
    wkv_p = np.stack(cat("o_wkv_p"), 0)[None]
    shift_p = np.stack([a.reshape(1024) for a in cat("o_shift_p")], 0)[None]
    conv_p = np.stack(cat("o_conv_p"), 0)[None]
    wkv_s = np.concatenate(cat("o_wkv_s"), 0)[None]
    shift_s = np.concatenate(cat("o_shift_s"), 0)[None]
    conv_s = np.concatenate(cat("o_conv_s"), 0)[None]
    outs = (y_p, y_s, a_v, wkv_p, shift_p, conv_p, wkv_s, shift_s, conv_s)
    return tuple(np.ascontiguousarray(o, dtype=np.float32) for o in outs)
```

```python
import numpy as np
import concourse.bass as bass
import concourse.mybir as mybir
from concourse.bass_utils import run_bass_kernel_spmd
from contextlib import ExitStack

F32 = mybir.dt.float32
BF16 = mybir.dt.bfloat16
AF = mybir.ActivationFunctionType
ALU = mybir.AluOpType
AX = mybir.AxisListType

ENG_ATTR = {'pe': 'tensor', 'act': 'scalar', 'dve': 'vector', 'pool': 'gpsimd', 'sp': 'sync'}
SEM_LIMIT = 30000
NDMA_SEMS = 12


def _is_ap(v):
    return hasattr(v, 'tensor') and hasattr(v, 'ap') and hasattr(v, 'offset')


class Inst:
    __slots__ = ('eng', 'fn', 'is_dma', 'deps', 'sig', 'tok', 'waits', 'ord', 'dma_idx')

    def __init__(self, eng, fn, is_dma):
        self.eng = eng
        self.fn = fn
        self.is_dma = is_dma
        self.deps = {}
        self.sig = False
        self.tok = None
        self.waits = []
        self.ord = 0
        self.dma_idx = -1


class Prog:
    def __init__(self, nc):
        self.nc = nc
        self.insts = []
        self.reg = {}
        self.buckets = {}
        self.sb_off = 0
        self.out_dmas = []

    def sbuf(self, name, shape, dtype, offset):
        t = self.nc.alloc_sbuf_tensor_at(name, list(shape), dtype, offset=offset)
        es = 2 if dtype == BF16 else 4
        fs = int(np.prod(shape[1:]))
        self.reg[t.name] = ('sb', offset, fs, es)
        return t

    def psum(self, name, shape, dtype=F32):
        t = self.nc.alloc_psum_tensor(name, list(shape), dtype)
        fs = int(np.prod(shape[1:]))
        self.reg[t.name] = ('ps', 0, fs, 4)
        return t

    def _rect(self, ap):
        info = self.reg.get(ap.tensor.name)
        if info is None:
            return None
        space, base, fs, es = info
        off = int(ap.offset)
        dims = ap.ap
        p0 = off // fs
        lo = off % fs
        npart = dims[0][1] if dims[0][0] != 0 else 1
        ext = 0
        for st, cnt in dims[1:]:
            ext += (cnt - 1) * abs(st)
        hi = lo + ext + 1
        if space == 'ps':
            b0 = (lo * es) // 2048
            b1 = (hi * es - 1) // 2048
            return (space, 0, 128, b0 * 2048, (b1 + 1) * 2048)
        return (space, p0, p0 + npart, base + lo * es, base + hi * es)

    def _track(self, inst, rect, is_write):
        space, p0, p1, lo, hi = rect
        BS = 1024
        b0, b1 = lo // BS, (hi - 1) // BS
        key = (p0, p1, lo, hi, inst.eng if not inst.is_dma else ('dma', id(inst)), is_write)
        for b in range(b0, b1 + 1):
            d = self.buckets.setdefault((space, b), {})
            dead = []
            for k, rec in d.items():
                q0, q1, l2, h2, other, w2 = rec
                if q0 >= p1 or q1 <= p0 or l2 >= hi or h2 <= lo:
                    continue
                if other is inst:
                    continue
                if is_write or w2:
                    kind = 'raw' if (w2 and not is_write) else 'waw_war'
                    prev = inst.deps.get(id(other))
                    if prev is None or kind == 'raw':
                        inst.deps[id(other)] = (other, kind)
                if is_write and q0 >= p0 and q1 <= p1 and l2 >= lo and h2 <= hi:
                    dead.append(k)
            for k in dead:
                del d[k]
            d[key] = (p0, p1, lo, hi, inst, is_write)

    def add(self, eng, fn, reads, writes, is_dma=False):
        inst = Inst(eng, fn, is_dma)
        for ap in reads:
            r = self._rect(ap)
            if r is not None:
                self._track(inst, r, False)
        for ap in writes:
            r = self._rect(ap)
            if r is not None:
                self._track(inst, r, True)
        self.insts.append(inst)
        return inst

    def op(self, eng, meth, **kw):
        reads, writes = [], []
        for k, v in kw.items():
            if _is_ap(v):
                (writes if k in ('out', 'accum_out', 'ap') else reads).append(v)
        is_dma = (meth == 'dma_start')
        inst = self.add(eng, lambda e: getattr(e, meth)(**kw), reads, writes, is_dma)
        if eng == 'pe':
            src = kw.get('lhsT', kw.get('in_'))
            ro = self._rect(src)
            rw = self._rect(kw['out'])
            if ro is not None and rw is not None:
                rows = (ro[1], ro[2])
                if not hasattr(self, 'pe_rows'):
                    self.pe_rows = {}
                for bnk in range(rw[3] // 2048, (rw[4] - 1) // 2048 + 1):
                    prev = self.pe_rows.get(bnk)
                    if prev is not None and (prev[0][1] <= rows[0] or rows[1] <= prev[0][0]):
                        inst.deps[id(prev[1])] = (prev[1], 'rowgrp')
                    self.pe_rows[bnk] = (rows, inst)
        if is_dma and _is_ap(kw['out']) and self.reg.get(kw['out'].tensor.name) is None:
            self.out_dmas.append(inst)
        return inst

    def emit(self):
        nc = self.nc
        with ExitStack() as es:
            fence = Inst('sp', None, False)
            for d in self.out_dmas:
                fence.deps[id(d)] = (d, 'raw')
            self.insts.append(fence)
            for inst in self.insts:
                nd = {}
                for k, (d, kind) in inst.deps.items():
                    if d.eng == 'pe' and inst.eng == 'pe' and not d.is_dma and not inst.is_dma and kind != 'rowgrp':
                        continue
                    nd[k] = (d, kind)
                    d.sig = True
                inst.deps = nd
            eng_sem = {}
            eng_cnt = {}
            eng_ord = {}
            dma_sems = {}
            dma_cnt = {}
            dma_hist = {}
            nsem = [0]

            def newsem():
                nsem[0] += 1
                return es.enter_context(nc.semaphore("s%d" % nsem[0]))

            per_eng = {e: [] for e in ENG_ATTR}
            for inst in self.insts:
                per_eng[inst.eng].append(inst)
                if inst.is_dma:
                    q = inst.eng
                    if q not in dma_sems:
                        dma_sems[q] = [newsem() for _ in range(NDMA_SEMS)]
                        dma_cnt[q] = [0] * NDMA_SEMS
                        dma_hist[q] = []
                    i = len(dma_hist[q])
                    s = i % NDMA_SEMS
                    dma_cnt[q][s] += 16
                    inst.tok = (dma_sems[q][s], dma_cnt[q][s])
                    inst.sig = True
                    inst.dma_idx = i
                    dma_hist[q].append(inst)
                elif inst.sig:
                    e = inst.eng
                    if e not in eng_sem or eng_cnt[e] >= SEM_LIMIT:
                        eng_sem[e] = newsem()
                        eng_cnt[e] = 0
                    eng_cnt[e] += 1
                    eng_ord[e] = eng_ord.get(e, 0) + 1
                    inst.tok = (eng_sem[e], eng_cnt[e])
                    inst.ord = eng_ord[e]
            waited = {e: {} for e in ENG_ATTR}
            dwaited = {e: set() for e in ENG_ATTR}
            for inst in self.insts:
                e = inst.eng
                deps = [d for d, _ in inst.deps.values()]
                if inst.is_dma and inst.dma_idx >= NDMA_SEMS:
                    deps.append(dma_hist[e][inst.dma_idx - NDMA_SEMS])
                for d in deps:
                    if d.is_dma:
                        if id(d) in dwaited[e]:
                            continue
                        dwaited[e].add(id(d))
                        inst.waits.append(d.tok)
                    else:
                        if waited[e].get(d.eng, 0) >= d.ord:
                            continue
                        waited[e][d.eng] = d.ord
                        inst.waits.append(d.tok)
            self.nsem = nsem[0]

            def run(ename, eobj):
                for inst in per_eng[ename]:
                    for (s, v) in inst.waits:
                        eobj.wait_ge(s, v)
                    if inst.fn is None:
                        continue
                    r = inst.fn(eobj)
                    if inst.sig:
                        r.then_inc(inst.tok[0], 16 if inst.is_dma else 1)

            with nc.Block() as block:
                @block.sync
                def _(e):
                    run('sp', e)

                @block.scalar
                def _(e):
                    run('act', e)

                @block.vector
                def _(e):
                    run('dve', e)

                @block.gpsimd
                def _(e):
                    run('pool', e)

                @block.tensor
                def _(e):
                    run('pe', e)

D = 1024
TOK = 2112
ALPHA = 8.0 ** 0.25
LN_EPS = 1e-5
GN_EPS = 64e-5
T512 = [(0, 512), (512, 512), (1024, 512), (1536, 512), (2048, 64)]
T128 = [(i * 128, 128) for i in range(16)] + [(2048, 64)]
HALVES = [T512[0:2], T512[2:5]]
SB_BASE = 16512
SB_END = 229376
STG_ELEMS = 3072


def build_program(stub_mixers=False, n_layers=4, kinds=None, _reqs=None):
    if _reqs is None:
        _reqs = build_program(stub_mixers, n_layers, kinds, _reqs="dry")
    DRY = (_reqs == "dry")
    REQ = []
    nc = bass.Bass("TRN2", target_bir_lowering=False)
    P = Prog(nc)

    def din(name, shape):
        return nc.dram_tensor(name, list(shape), F32, kind="ExternalInput").ap()

    def dout(name, shape):
        return nc.dram_tensor(name, list(shape), F32, kind="ExternalOutput").ap()

    x_p = din("x_p", [2048, 1024]); x_s = din("x_s", [64, 1024])
    wkv_in = din("wkv_in", [16, 16, 64, 64]); shift_in = din("shift_in", [16, 1024])
    conv_in = din("conv_in", [16, 2, 1024])
    p_p = din("p_p", [4, 2048, 256]); p_s = din("p_s", [4, 64, 256])
    W = {}
    for nm, shp in [("ln_g", [4, 3, 1024]), ("ln_b", [4, 3, 1024]), ("ffn_w_in", [4, 2, 1024, 5632]),
                    ("ffn_w_out", [4, 2, 2816, 1024]), ("ple_w_gate", [4, 1024, 1024]), ("ple_w_proj", [4, 256, 1024]),
                    ("a_w_in", [2, 1024, 4096]), ("a_b_in", [2, 4096]), ("a_ln_g", [2, 2048]), ("a_ln_b", [2, 2048]),
                    ("a_w_sT", [2, 8, 128, 128]), ("a_w_sS", [2, 8, 64, 64]), ("a_b_s", [2, 8, 128]), ("a_b_sS", [2, 8, 64]),
                    ("a_w_out", [2, 2048, 1024]),
                    ("b_mu", [1, 6, 1024]), ("b_w_rkv", [1, 3, 1024, 1024]), ("b_w0", [1, 1024]), ("b_w1", [1, 1024, 64]),
                    ("b_w2", [1, 64, 1024]), ("b_a0", [1, 1024]), ("b_a1", [1, 1024, 64]), ("b_a2", [1, 64, 1024]),
                    ("b_g1", [1, 1024, 128]), ("b_g2", [1, 128, 1024]), ("b_k_k", [1, 1024]), ("b_k_a", [1, 1024]),
                    ("b_r_k", [1, 1024]), ("b_lnx_g", [1, 1024]), ("b_lnx_b", [1, 1024]), ("b_w_o", [1, 1024, 1024]),
                    ("c_w_in", [1, 1024, 3072]), ("c_conv_w", [1, 3, 1024]), ("c_w_out", [1, 1024, 1024])]:
        W[nm] = din(nm, shp)
    y_p = dout("y_p", [2048, 1024]); y_s = dout("y_s", [64, 1024])
    o_av = dout("o_av", [2, 64, 2048]); o_wkv_p = dout("o_wkv_p", [16, 64, 64]); o_shift_p = dout("o_shift_p", [1, 1024])
    o_conv_p = dout("o_conv_p", [2, 1024]); o_wkv_s = dout("o_wkv_s", [16, 16, 64, 64])
    o_shift_s = dout("o_shift_s", [16, 1024]); o_conv_s = dout("o_conv_s", [16, 2, 1024])

    cur = [SB_BASE]
    ncnt = [0]

    def alloc(shape, dt, at=None):
        ncnt[0] += 1
        nb = int(np.prod(shape[1:])) * (2 if dt == BF16 else 4)
        if at is None:
            at = cur[0]
            cur[0] = (at + nb + 63) // 64 * 64
            assert cur[0] <= SB_END, "sbuf overflow"
        else:
            assert at + nb <= SB_END, ("sbuf overflow", at, nb)
        return P.sbuf("t%d" % ncnt[0], shape, dt, at)

    ident = alloc([128, 128], F32)
    ones = alloc([128, 128], F32)
    LNG = alloc([128, 96], F32)
    LNB = alloc([128, 96], F32)
    VEC = alloc([128, 256], F32)
    EPSC = alloc([128, 2], F32)
    epsi = {float(LN_EPS): 0, float(4 * LN_EPS): 1}
    onesb = alloc([128, 128], BF16)
    identb = alloc([128, 128], BF16)
    XF = alloc([128, 8, TOK], F32)
    XB = alloc([128, 8, TOK], BF16)
    XB_OFF = cur[0] - 8 * TOK * 2
    STG = [alloc([128, STG_ELEMS], F32) for _ in range(2)]
    WBS = [alloc([128, STG_ELEMS], BF16) for _ in range(2)]
    STG_OFF = cur[0] - 2 * STG_ELEMS * 2 - 2 * STG_ELEMS * 4
    LNT_OFF = cur[0]
    LNT_SIZE = 16384
    cur[0] += LNT_SIZE
    PH_OFF = cur[0]
    PH_SIZE = SB_END - PH_OFF
    PS = P.psum("ps", [128, 8, 512])
    BSLOTS = WBS + [alloc([128, STG_ELEMS], BF16, at=STG_OFF + i * STG_ELEMS * 2) for i in range(4)]

    def sub(off0, shape, dt):
        return alloc(shape, dt, at=off0)
    SQ = [sub(LNT_OFF + i * 2048, [128, 512], F32) for i in range(2)]
    MEAN = sub(LNT_OFF + 4096, [128, 512], F32)
    VAR = sub(LNT_OFF + 6144, [128, 512], F32)
    RSTD = sub(LNT_OFF + 8192, [128, 512], F32)
    XC = [sub(LNT_OFF + 10240 + i * 2048, [128, 512], F32) for i in range(2)]
    TMPA = [sub(LNT_OFF + 10240 + i * 2048, [128, 512], F32) for i in range(2)]
    TMPB = [sub(LNT_OFF + i * 2048, [128, 512], F32) for i in range(2)]

    bankc = [0]
    tmc = [0]

    reserved = set()

    def bank():
        while True:
            b = bankc[0] % 8
            bankc[0] += 1
            if b not in reserved:
                return b

    def bcast_rows(ap2d, nparts):
        return bass.AP(ap2d.tensor, ap2d.offset, [[0, nparts], [1, int(ap2d.shape[-1])]])

    def op(eng, meth, **kw):
        if DRY:
            return None
        return P.op(eng, meth, **kw)

    op('pool', 'memset', ap=ident[:], constant=1.0)
    op('pool', 'affine_select', out=ident[:], in_=ident[:], pattern=[[-1, 128]], compare_op=ALU.is_equal,
       fill=0.0, base=0, channel_multiplier=1)
    op('dve', 'memset', ap=ones[:], constant=1.0)
    op('dve', 'memset', ap=EPSC[:, 0:1], constant=float(LN_EPS))
    op('dve', 'memset', ap=EPSC[:, 1:2], constant=float(4 * LN_EPS))

    IOS = [sub(PH_OFF + i * 4096, [128, 1024], F32) for i in range(2)]
    iosc = [0]

    def load_cols(dst_ap, src2d, R):
        s = IOS[iosc[0] % 2]; iosc[0] += 1
        op('sp', 'dma_start', out=s[0:R, 0:128], in_=src2d)
        b = bank()
        op('pe', 'transpose', out=PS[:, b, 0:R], in_=s[0:R, 0:128], identity=ident[0:R, 0:R])
        op('dve', 'tensor_copy', out=dst_ap, in_=PS[:, b, 0:R])

    load_cols(LNG[:], W["ln_g"].rearrange("l i (c p) -> (l i c) p", p=128), 96)
    load_cols(LNB[:], W["ln_b"].rearrange("l i (c p) -> (l i c) p", p=128), 96)
    VC = {}
    vcur = [0]

    def vec_cols(key, src2d, R):
        load_cols(VEC[:, vcur[0]:vcur[0] + R], src2d, R)
        VC[key] = vcur[0]
        vcur[0] += R

    wsc = [0]
    cast_engs = ['dve', 'act']

    WCACHE = {}
    bsc = [0]
    issued = {}
    nreq = [0]
    nissued = [0]

    live = {}
    SLOTN = ['B2', 'B3', 'B4', 'B5', 'W0', 'W1']

    def _slot_ap(nm):
        return BSLOTS[{'W0': 0, 'W1': 1, 'B2': 2, 'B3': 3, 'B4': 4, 'B5': 5}[nm]]

    def _try_issue(r):
        parts, K, key = _reqs[r]
        kc = max(1, K // 128)
        kp = min(K, 128)
        ntot = sum(int(a.shape[1]) for a in parts)
        assert kc * ntot <= STG_ELEMS
        if key is not None and key in WCACHE:
            free = [s_ for s_ in SLOTN if s_ not in live]
            if not free:
                return None
            live[free[0]] = r
            dt_, wr = WCACHE[key]
            bs = _slot_ap(free[0])
            inst = op('sp', 'dma_start', out=bs[0:kp, 0:kc * ntot], in_=dt_)
            inst.deps[id(wr)] = (wr, 'raw')
            return bs[0:kp, 0:kc * ntot].rearrange("p (c n) -> p c n", c=kc)
        wfree = [s_ for s_ in ('W0', 'W1') if s_ not in live]
        tfree = [t for t in (0, 1) if ('B%d' % (2 * t + 2)) not in live and ('B%d' % (2 * t + 3)) not in live]
        if not wfree or not tfree:
            return None
        t = tfree[wsc[0] % len(tfree)]
        slot = 0 if wfree[0] == 'W0' else 1
        if len(wfree) == 2:
            slot = wsc[0] % 2
        live['W%d' % slot] = r
        ce = cast_engs[wsc[0] % len(cast_engs)]
        wsc[0] += 1
        sv = STG[t][0:kp, 0:kc * ntot].rearrange("p (c n) -> p c n", c=kc)
        wv = WBS[slot][0:kp, 0:kc * ntot].rearrange("p (c n) -> p c n", c=kc)
        col = 0
        for a in parts:
            n = int(a.shape[1])
            src = a.rearrange("(c p) n -> p c n", p=kp)
            op('sp', 'dma_start', out=sv[:, :, col:col + n], in_=src)
            col += n
        if ce == 'act':
            op('act', 'activation', out=wv, in_=sv, func=AF.Copy)
        else:
            op(ce, 'tensor_copy', out=wv, in_=sv)
        if key is not None:
            dt_ = nc.dram_tensor("wc%d" % len(WCACHE), [kp, kc * ntot], BF16, kind="Internal").ap()
            wr = op('pool', 'dma_start', out=dt_, in_=WBS[slot][0:kp, 0:kc * ntot])
            WCACHE[key] = (dt_, wr)
        return wv

    def wload(parts, K, key=None):
        i = nreq[0]; nreq[0] += 1
        kc = max(1, K // 128); kp = min(K, 128)
        ntot = sum(int(a.shape[1]) for a in parts)
        if DRY:
            REQ.append((parts, K, key))
            return WBS[0][0:kp, 0:kc * ntot].rearrange("p (c n) -> p c n", c=kc)
        for s_ in [s_ for s_, r_ in live.items() if r_ < i]:
            del live[s_]
        while nissued[0] < len(_reqs) and nissued[0] <= i + 4:
            r = nissued[0]
            k2 = _reqs[r][2]
            cached = (k2 is not None and k2 in WCACHE)
            if not cached and r > i + 1:
                break
            ap_ = _try_issue(r)
            if ap_ is None:
                assert r > i, "no weight slot for a mandatory load"
                break
            issued[r] = ap_
            nissued[0] += 1
        return issued.pop(i)

    def pipelined(blocks, K, body, keys=None):
        for i in range(len(blocks)):
            body(i, wload(blocks[i], K, None if keys is None else keys[i]))

    def load_x():
        for ti, (t0, n) in enumerate(T128):
            s = IOS[iosc[0] % 2]; iosc[0] += 1
            src = x_p[t0:t0 + n, :] if t0 < 2048 else x_s
            op('sp', 'dma_start', out=s[0:n, :], in_=src)
            for hh in range(2):
                b = bank()
                for q in range(4):
                    c = hh * 4 + q
                    op('pe', 'transpose', out=PS[:, b, q * n:(q + 1) * n], in_=s[0:n, c * 128:(c + 1) * 128],
                       identity=ident[0:n, 0:n])
                pv = PS[:, b, 0:4 * n].rearrange("p (a t) -> p a t", a=4)
                op('dve', 'tensor_copy', out=XF[:, hh * 4:(hh + 1) * 4, t0:t0 + n], in_=pv)
                op('act', 'activation', out=XB[:, hh * 4:(hh + 1) * 4, t0:t0 + n], in_=XF[:, hh * 4:(hh + 1) * 4, t0:t0 + n], func=AF.Copy)

    def store_y():
        for ti, (t0, n) in enumerate(T128):
            s = IOS[iosc[0] % 2]; iosc[0] += 1
            for hh in range(2):
                b = bank()
                for q in range(4):
                    c = hh * 4 + q
                    op('pe', 'transpose', out=PS[0:n, b, q * 128:(q + 1) * 128], in_=XF[:, c, t0:t0 + n], identity=ident[:])
                if hh == 0:
                    op('dve', 'tensor_copy', out=s[0:n, 0:512], in_=PS[0:n, b, :])
                else:
                    op('act', 'activation', out=s[0:n, 512:1024], in_=PS[0:n, b, :], func=AF.Copy)
            dst = y_p[t0:t0 + n, :] if t0 < 2048 else y_s
            op('sp', 'dma_start', out=dst, in_=s[0:n, :])

    def layer_norm(gi, eps, tiles=None):
        for (t0, n) in (T512 if tiles is None else tiles):
            b1 = bank(); b2 = bank()
            for c in range(8):
                op('pe', 'matmul', out=PS[:, b1, 0:n], lhsT=ones[:], rhs=XF[:, c, t0:t0 + n], start=(c == 0), stop=(c == 7))
            for c in range(8):
                sq = SQ[c % 2]
                op('pool', 'tensor_tensor', out=sq[:, 0:n], in0=XF[:, c, t0:t0 + n], in1=XF[:, c, t0:t0 + n], op=ALU.mult)
                op('pe', 'matmul', out=PS[:, b2, 0:n], lhsT=ones[:], rhs=sq[:, 0:n], start=(c == 0), stop=(c == 7))
            op('dve', 'tensor_scalar', out=MEAN[:, 0:n], in0=PS[:, b1, 0:n], scalar1=1.0 / D, scalar2=None, op0=ALU.mult)
            op('dve', 'tensor_tensor', out=RSTD[:, 0:n], in0=MEAN[:, 0:n], in1=MEAN[:, 0:n], op=ALU.mult)
            op('dve', 'scalar_tensor_tensor', out=VAR[:, 0:n], in0=PS[:, b2, 0:n], scalar=1.0 / D, in1=RSTD[:, 0:n],
               op0=ALU.mult, op1=ALU.subtract)
            op('act', 'activation', out=VAR[:, 0:n], in_=VAR[:, 0:n], func=AF.Ln, bias=EPSC[:, epsi[float(eps)]:epsi[float(eps)] + 1], scale=1.0)
            op('act', 'activation', out=RSTD[:, 0:n], in_=VAR[:, 0:n], func=AF.Exp, scale=-0.5)
            for c in range(8):
                xc = XC[c % 2]
                op('pool' if c % 2 == 1 else 'dve', 'tensor_tensor', out=xc[:, 0:n], in0=XF[:, c, t0:t0 + n], in1=MEAN[:, 0:n], op=ALU.subtract)
                op('dve', 'tensor_tensor', out=xc[:, 0:n], in0=xc[:, 0:n], in1=RSTD[:, 0:n], op=ALU.mult)
                g = LNG[:, gi * 8 + c:gi * 8 + c + 1]
                bb = LNB[:, gi * 8 + c:gi * 8 + c + 1]
                op('act', 'activation', out=XF[:, c, t0:t0 + n], in_=xc[:, 0:n], func=AF.Identity, scale=g, bias=bb)
                if c % 4 != 3:
                    op('act', 'activation', out=XB[:, c, t0:t0 + n], in_=xc[:, 0:n], func=AF.Identity, scale=g, bias=bb)
                else:
                    op('dve', 'tensor_scalar', out=XB[:, c, t0:t0 + n], in0=xc[:, 0:n], scalar1=g, scalar2=bb, op0=ALU.mult, op1=ALU.add)

    H = sub(PH_OFF, [128, 22, 1088], BF16)

    def ffn(l, i, pre_ln=None, side=None):
        w_in = W["ffn_w_in"][l, i]
        w_out = W["ffn_w_out"][l, i]
        if pre_ln is not None:
            layer_norm(pre_ln[0], pre_ln[1], tiles=HALVES[0])
        for hi_, half in enumerate(HALVES):
            h0 = half[0][0]

            def body_in(j, cw):
                for (t0, n) in half:
                    bg = bank(); bu = bank()
                    for kc in range(8):
                        op('pe', 'matmul', out=PS[:, bg, 0:n], lhsT=cw[:, kc, 0:128], rhs=XB[:, kc, t0:t0 + n],
                           start=(kc == 0), stop=(kc == 7))
                    for kc in range(8):
                        op('pe', 'matmul', out=PS[:, bu, 0:n], lhsT=cw[:, kc, 128:256], rhs=XB[:, kc, t0:t0 + n],
                           start=(kc == 0), stop=(kc == 7))
                    tm = TMPA[tmc[0] % 2]; tmc[0] += 1
                    op('act', 'activation', out=tm[:, 0:n], in_=PS[:, bg, 0:n], func=AF.Silu)
                    op('dve', 'tensor_tensor', out=H[:, j, t0 - h0:t0 - h0 + n], in0=tm[:, 0:n], in1=PS[:, bu, 0:n], op=ALU.mult)
                if side is not None:
                    next(side, None)

            pipelined([[w_in[:, j * 128:(j + 1) * 128], w_in[:, 2816 + j * 128:2816 + (j + 1) * 128]] for j in range(22)],
                      1024, body_in)
            if pre_ln is not None and hi_ == 0:
                layer_norm(pre_ln[0], pre_ln[1], tiles=HALVES[1])

            def body_out(m, cw):
                for (t0, n) in half:
                    b = bank()
                    for fc in range(22):
                        op('pe', 'matmul', out=PS[:, b, 0:n], lhsT=cw[:, fc, :], rhs=H[:, fc, t0 - h0:t0 - h0 + n],
                           start=(fc == 0), stop=(fc == 21))
                    op('dve', 'scalar_tensor_tensor', out=XF[:, m, t0:t0 + n], in0=XF[:, m, t0:t0 + n], scalar=2.0 * ALPHA,
                       in1=PS[:, b, 0:n], op0=ALU.mult, op1=ALU.add)

            pipelined([[w_out[:, m * 128:(m + 1) * 128]] for m in range(8)], 2816, body_out)

    PT = sub(PH_OFF + 8192, [128, 2, TOK], BF16)

    def ple(l):
        for gi in range(4):
            s = IOS[iosc[0] % 2]; iosc[0] += 1
            t0 = gi * 512
            op('sp', 'dma_start', out=s[:, :].rearrange("p (n d) -> p n d", n=4),
               in_=p_p[l, t0:t0 + 512, :].rearrange("(n p) d -> p n d", p=128))
            for k2 in range(2):
                b = bank()
                for q in range(4):
                    op('pe', 'transpose', out=PS[:, b, q * 128:(q + 1) * 128],
                       in_=s[:, q * 256 + k2 * 128:q * 256 + (k2 + 1) * 128], identity=ident[:])
                if k2 == 0:
                    op('dve', 'tensor_copy', out=PT[:, k2, t0:t0 + 512], in_=PS[:, b, :])
                else:
                    op('act', 'activation', out=PT[:, k2, t0:t0 + 512], in_=PS[:, b, :], func=AF.Copy)
        s = IOS[iosc[0] % 2]; iosc[0] += 1
        op('sp', 'dma_start', out=s[0:64, 0:256], in_=p_s[l])
        b = bank()
        for k2 in range(2):
            op('pe', 'transpose', out=PS[:, b, k2 * 64:(k2 + 1) * 64], in_=s[0:64, k2 * 128:(k2 + 1) * 128], identity=ident[0:64, 0:64])
        op('dve', 'tensor_copy', out=PT[:, :, 2048:2112], in_=PS[:, b, 0:128].rearrange("p (a t) -> p a t", a=2))
        wp = wload([W["ple_w_proj"][l]], 256)
        WP = sub(PH_OFF + 8192 + 2 * TOK * 2 + 64, [128, 2, 1024], BF16)
        op('dve', 'tensor_copy', out=WP[:], in_=wp)
        wg = W["ple_w_gate"][l]

        def body(mb, cw):
            for mm_ in range(2):
                m = mb * 2 + mm_
                for (t0, n) in T512:
                    bg = bank(); bp = bank()
                    for kc in range(8):
                        op('pe', 'matmul', out=PS[:, bg, 0:n], lhsT=cw[:, kc, mm_ * 128:(mm_ + 1) * 128], rhs=XB[:, kc, t0:t0 + n],
                           start=(kc == 0), stop=(kc == 7))
                    for k2 in range(2):
                        op('pe', 'matmul', out=PS[:, bp, 0:n], lhsT=WP[:, k2, m * 128:(m + 1) * 128], rhs=PT[:, k2, t0:t0 + n],
                           start=(k2 == 0), stop=(k2 == 1))
                    tm = TMPA[tmc[0] % 2]; tmc[0] += 1
                    op('act', 'activation', out=tm[:, 0:n], in_=PS[:, bg, 0:n], func=AF.Sigmoid)
                    op('dve', 'tensor_tensor', out=tm[:, 0:n], in0=tm[:, 0:n], in1=PS[:, bp, 0:n], op=ALU.mult)
                    op('dve', 'tensor_tensor', out=XF[:, m, t0:t0 + n], in0=XF[:, m, t0:t0 + n], in1=tm[:, 0:n], op=ALU.add)

        pipelined([[wg[:, mb * 256:(mb + 1) * 256]] for mb in range(4)], 1024, body)
        for (t0, n) in T512:
            op('act', 'activation', out=XB[:, 0:4, t0:t0 + n], in_=XF[:, 0:4, t0:t0 + n], func=AF.Copy)
            op('dve', 'tensor_copy', out=XB[:, 4:8, t0:t0 + n], in_=XF[:, 4:8, t0:t0 + n])

    def mixer_stub():
        for (t0, n) in T512:
            op('dve', 'tensor_scalar', out=XF[:, :, t0:t0 + n], in0=XF[:, :, t0:t0 + n], scalar1=ALPHA, scalar2=None, op0=ALU.mult)

    op('dve', 'memset', ap=onesb[:], constant=1.0)
    op('act', 'activation', out=identb[:], in_=ident[:], func=AF.Copy)

    def setup_vecs():
        vec_cols("a_b_u0", W["a_b_in"][0:1, 0:2048].rearrange("o (c p) -> (o c) p", p=128), 16)
        vec_cols("a_b_u1", W["a_b_in"][1:2, 0:2048].rearrange("o (c p) -> (o c) p", p=128), 16)
        vec_cols("a_g0", W["a_ln_g"][0:1, :].rearrange("o (c p) -> (o c) p", p=128), 16)
        vec_cols("a_g1", W["a_ln_g"][1:2, :].rearrange("o (c p) -> (o c) p", p=128), 16)
        vec_cols("a_b0", W["a_ln_b"][0:1, :].rearrange("o (c p) -> (o c) p", p=128), 16)
        vec_cols("a_b1", W["a_ln_b"][1:2, :].rearrange("o (c p) -> (o c) p", p=128), 16)
        vec_cols("c_cw", W["c_conv_w"][0].rearrange("i (c p) -> (i c) p", p=128), 24)
        vec_cols("b_mu", W["b_mu"][0].rearrange("i (c p) -> (i c) p", p=128), 48)
        for nm in ["b_w0", "b_a0", "b_k_k", "b_k_a", "b_r_k"]:
            vec_cols(nm, W[nm].rearrange("o (c p) -> (o c) p", p=128), 8)

    setup_vecs()

    def vcol(key, i):
        return VEC[:, VC[key] + i:VC[key] + i + 1]

    def mixer_c(j):
        w_in = W["c_w_in"][j]
        w_out = W["c_w_out"][j]
        Z = sub(PH_OFF, [128, 8, 514], F32)
        ZS = sub(PH_OFF + 16448, [128, 8, 16, 6], F32)
        YC2 = [sub(PH_OFF + 16448 + 3072 + i * 8192, [128, 8, 512], BF16) for i in range(2)]
        CT = sub(PH_OFF + 16448 + 3072 + 16384, [128, 8, 32], F32)
        IOC = sub(PH_OFF + 16448 + 3072 + 16384 + 1024, [128, 1024], F32)
        op('dve', 'memset', ap=Z[:, :, 0:2], constant=0.0)
        op('sp', 'dma_start', out=IOC[0:32, :], in_=conv_in.rearrange("b i d -> (b i) d"))
        for hh in range(2):
            b = bank()
            for q in range(4):
                c = hh * 4 + q
                op('pe', 'transpose', out=PS[:, b, q * 32:(q + 1) * 32], in_=IOC[0:32, c * 128:(c + 1) * 128], identity=ident[0:32, 0:32])
            for q in range(4):
                c = hh * 4 + q
                op('dve', 'tensor_copy', out=ZS[:, c, :, 0:2], in_=PS[:, b, q * 32:(q + 1) * 32].rearrange("p (b i) -> p b i", i=2))

        def in_gen(gi):
            t0, n = T512[gi]
            samp = (t0 >= 2048)
            YC = YC2[gi % 2]

            def zview(m, lo):
                if samp:
                    return ZS[:, m, :, lo:lo + 4]
                return Z[:, m, lo:lo + n]

            def v3(ap):
                return ap.rearrange("p (b t) -> p b t", t=4) if samp else ap
            for m in range(8):
                cw = wload([w_in[:, 1024 + m * 128:1024 + (m + 1) * 128], w_in[:, 2048 + m * 128:2048 + (m + 1) * 128],
                            w_in[:, m * 128:(m + 1) * 128]], 1024, ("c", "in", m))
                b1 = bank(); b2 = bank(); b3 = bank()
                for bi, bb in enumerate((b1, b2, b3)):
                    for kc in range(8):
                        op('pe', 'matmul', out=PS[:, bb, 0:n], lhsT=cw[:, kc, bi * 128:(bi + 1) * 128], rhs=XB[:, kc, t0:t0 + n],
                           start=(kc == 0), stop=(kc == 7))
                tm = TMPA[tmc[0] % 2]; tmc[0] += 1
                cv = TMPB[tmc[0] % 2]
                op('act', 'activation', out=tm[:, 0:n], in_=PS[:, b1, 0:n], func=AF.Copy)
                op('dve', 'tensor_tensor', out=zview(m, 2), in0=v3(tm[:, 0:n]), in1=v3(PS[:, b2, 0:n]), op=ALU.mult)
                op('dve', 'tensor_scalar', out=v3(cv[:, 0:n]), in0=zview(m, 0), scalar1=vcol("c_cw", 0 * 8 + m), scalar2=None, op0=ALU.mult)
                op('dve', 'scalar_tensor_tensor', out=v3(cv[:, 0:n]), in0=zview(m, 1), scalar=vcol("c_cw", 1 * 8 + m), in1=v3(cv[:, 0:n]),
                   op0=ALU.mult, op1=ALU.add)
                op('dve', 'scalar_tensor_tensor', out=v3(cv[:, 0:n]), in0=zview(m, 2), scalar=vcol("c_cw", 2 * 8 + m), in1=v3(cv[:, 0:n]),
                   op0=ALU.mult, op1=ALU.add)
                op('dve', 'tensor_tensor', out=YC[:, m, 0:n], in0=cv[:, 0:n], in1=PS[:, b3, 0:n], op=ALU.mult)
                yield
            if not samp:
                if gi == 3:
                    for c in range(8):
                        op('sp', 'dma_start', out=o_conv_p[:, c * 128:(c + 1) * 128].rearrange("i p -> p i"), in_=Z[:, c, 512:514],
                           allow_slow_non_contiguous=True)
                else:
                    op('dve', 'tensor_copy', out=Z[:, :, 0:2], in_=Z[:, :, 512:514])
            else:
                for c in range(8):
                    op('dve', 'tensor_copy', out=CT[:, c, :].rearrange("p (b i) -> p b i", i=2), in_=ZS[:, c, :, 4:6])
                for hh in range(2):
                    b = bank()
                    for q in range(4):
                        c = hh * 4 + q
                        op('pe', 'transpose', out=PS[0:32, b, q * 128:(q + 1) * 128], in_=CT[:, c, :], identity=ident[:])
                    op('dve', 'tensor_copy', out=IOC[0:32, hh * 512:(hh + 1) * 512], in_=PS[0:32, b, :])
                op('sp', 'dma_start', out=o_conv_s.rearrange("b i d -> (b i) d"), in_=IOC[0:32, :])
            yield

        def out_gen(gi):
            t0, n = T512[gi]
            YC = YC2[gi % 2]
            for mb in range(4):
                cw = wload([w_out[:, mb * 256:(mb + 1) * 256]], 1024, ("c", "out", mb))
                for mm_ in range(2):
                    m = mb * 2 + mm_
                    b = bank()
                    for kc in range(8):
                        op('pe', 'matmul', out=PS[:, b, 0:n], lhsT=cw[:, kc, mm_ * 128:(mm_ + 1) * 128], rhs=YC[:, kc, 0:n],
                           start=(kc == 0), stop=(kc == 7))
                    op('dve', 'scalar_tensor_tensor', out=XF[:, m, t0:t0 + n], in0=XF[:, m, t0:t0 + n], scalar=ALPHA,
                       in1=PS[:, b, 0:n], op0=ALU.mult, op1=ALU.add)
                yield

        def run_all(gen):
            for _ in gen:
                pass

        def interleave_c(ga, gb):
            gens = [g_ for g_ in (ga, gb) if g_ is not None]
            while gens:
                for g_ in list(gens):
                    try:
                        next(g_)
                    except StopIteration:
                        gens.remove(g_)

        run_all(in_gen(0))
        for gi in range(len(T512)):
            interleave_c(in_gen(gi + 1) if gi + 1 < len(T512) else None, out_gen(gi))

    GA = [(i * 256, 256) for i in range(8)] + [(2048, 64)]

    def mixer_a(j):
        w_in = W["a_w_in"][j]
        w_out = W["a_w_out"][j]
        o = [PH_OFF]

        def pa(shape, dt):
            nb = int(np.prod(shape[1:])) * (2 if dt == BF16 else 4)
            t = sub(o[0], shape, dt)
            o[0] = (o[0] + nb + 63) // 64 * 64
            return t
        VF = pa([128, 2, 2048], F32)
        VNS = [pa([128, 2, 2048], BF16) for _ in range(2)]
        YA = pa([128, 16, 256], BF16)
        WST = pa([128, 8, 128], BF16)
        WSS = pa([128, 8, 64], BF16)
        RBC = sub(LNT_OFF + 4096, [128, 8, 128], F32)
        RSBC = sub(LNT_OFF + 8192, [128, 8, 64], F32)
        BSBC = pa([128, 8, 128], F32)
        BSSBC = sub(LNT_OFF + 14336, [128, 8, 64], F32)
        BVT = [pa([128, 256], F32) for _ in range(4)]
        BBT = [pa([128, 128], F32) for _ in range(2)]
        STAT = pa([128, 2, 24], F32)
        MV = pa([128, 2, 2], F32)
        RSTDA = pa([128, 2, 1], F32)
        assert o[0] <= SB_END
        GBROW = sub(PH_OFF + 8192, [33, 2048], F32)
        sv = VF[:, 0, 0:1024].rearrange("p (h t) -> p h t", h=8)
        op('sp', 'dma_start', out=sv, in_=W["a_w_sT"][j].rearrange("h s t -> s h t"))
        op('pool', 'affine_select', out=sv, in_=sv, pattern=[[0, 8], [1, 128]], compare_op=ALU.is_ge, fill=0.0,
           base=0, channel_multiplier=-1)
        op('dve', 'tensor_copy', out=WST[:], in_=sv)
        sv2 = VF[0:64, 1, 0:512].rearrange("p (h t) -> p h t", h=8)
        op('sp', 'dma_start', out=sv2, in_=W["a_w_sS"][j].rearrange("h s t -> s h t"))
        op('pool', 'affine_select', out=sv2, in_=sv2, pattern=[[0, 8], [1, 64]], compare_op=ALU.is_ge, fill=0.0,
           base=0, channel_multiplier=-1)
        op('dve', 'tensor_copy', out=WSS[0:64], in_=sv2)
        for hh in range(2):
            b = bank()
            for q in range(4):
                h = hh * 4 + q
                op('pe', 'matmul', out=PS[:, b, q * 128:(q + 1) * 128], lhsT=onesb[:], rhs=WST[:, h, :], start=True, stop=True)
            op('dve', 'tensor_copy', out=RBC[:, hh * 4:(hh + 1) * 4, :], in_=PS[:, b, :].rearrange("p (h t) -> p h t", h=4))
        b = bank()
        for h in range(8):
            op('pe', 'matmul', out=PS[:, b, h * 64:(h + 1) * 64], lhsT=onesb[0:64, :], rhs=WSS[0:64, h, :], start=True, stop=True)
        op('dve', 'tensor_copy', out=RSBC[:], in_=PS[:, b, :].rearrange("p (h t) -> p h t", h=8))
        op('sp', 'dma_start', out=BSBC[:].rearrange("p h t -> p (h t)"),
           in_=bcast_rows(W["a_b_s"][j:j + 1].rearrange("o h t -> o (h t)"), 128))
        op('sp', 'dma_start', out=BSSBC[:].rearrange("p h t -> p (h t)"),
           in_=bcast_rows(W["a_b_sS"][j:j + 1].rearrange("o h t -> o (h t)"), 128))

        def p1(gidx):
            g0, gn = GA[gidx]
            VN = VNS[gidx % 2]
            samp = (g0 >= 2048)
            tiles = [(g0, 64)] if samp else [(g0, 128), (g0 + 128, 128)]
            for q in range(8):
                cw = wload([w_in[:, 2048 + q * 256:2048 + (q + 1) * 256]], 1024, ("a", j, "v", q))
                bv = BVT[q % 4]
                op('pool', 'dma_start', out=bv[:], in_=bcast_rows(W["a_b_in"][j:j + 1, 2048 + q * 256:2048 + (q + 1) * 256], 128))
                for i, (t0, nt) in enumerate(tiles):
                    b = bank()
                    for kc in range(8):
                        op('pe', 'matmul', out=PS[0:nt, b, 0:256], lhsT=XB[:, kc, t0:t0 + nt], rhs=cw[:, kc, :],
                           start=(kc == 0), stop=(kc == 7))
                    op('dve', 'tensor_tensor', out=VF[0:nt, i, q * 256:(q + 1) * 256], in0=PS[0:nt, b, 0:256], in1=bv[0:nt, :], op=ALU.add)
                yield
            for i, (t0, nt) in enumerate(tiles):
                op('act', 'activation', out=VF[0:nt, i, :], in_=VF[0:nt, i, :], func=AF.Gelu)
                for q in range(4):
                    op('dve', 'bn_stats', out=STAT[0:nt, i, q * 6:(q + 1) * 6], in_=VF[0:nt, i, q * 512:(q + 1) * 512])
                op('dve', 'bn_aggr', out=MV[0:nt, i, :], in_=STAT[0:nt, i, :])
                op('act', 'activation', out=RSTDA[0:nt, i, :], in_=MV[0:nt, i, 1:2], func=AF.Sqrt, bias=LN_EPS, scale=1.0)
                op('dve', 'reciprocal', out=RSTDA[0:nt, i, :], in_=RSTDA[0:nt, i, :])
                yield
                op('dve', 'tensor_scalar', out=VF[0:nt, i, :], in0=VF[0:nt, i, :], scalar1=MV[0:nt, i, 0:1], scalar2=RSTDA[0:nt, i, :],
                   op0=ALU.subtract, op1=ALU.mult)
                op('act', 'activation', out=VN[0:nt, i, :], in_=VF[0:nt, i, :], func=AF.Copy)
                yield
            if samp:
                op('sp', 'dma_start', out=GBROW[0:1, :], in_=W["a_ln_g"][j:j + 1, :])
                op('sp', 'dma_start', out=GBROW[32:33, :], in_=W["a_ln_b"][j:j + 1, :])
                for q in range(4):
                    bg = bank(); bb = bank()
                    op('pe', 'matmul', out=PS[0:64, bg, :], lhsT=ones[0:1, 0:64], rhs=GBROW[0:1, q * 512:(q + 1) * 512], start=True, stop=True)
                    op('pe', 'matmul', out=PS[0:64, bb, :], lhsT=ones[32:33, 0:64], rhs=GBROW[32:33, q * 512:(q + 1) * 512], start=True, stop=True)
                    op('dve', 'tensor_tensor', out=VF[0:64, 0, q * 512:(q + 1) * 512], in0=VF[0:64, 0, q * 512:(q + 1) * 512], in1=PS[0:64, bg, :], op=ALU.mult)
                    op('dve', 'tensor_tensor', out=VF[0:64, 0, q * 512:(q + 1) * 512], in0=VF[0:64, 0, q * 512:(q + 1) * 512], in1=PS[0:64, bb, :], op=ALU.add)
                op('sp', 'dma_start', out=o_av[j], in_=VF[0:64, 0, :])
                yield

        def p23(gidx):
            g0, gn = GA[gidx]
            VN = VNS[gidx % 2]
            samp = (g0 >= 2048)
            tiles = [(g0, 64)] if samp else [(g0, 128), (g0 + 128, 128)]
            for bi in range(8):
                cw = wload([w_in[:, bi * 256:(bi + 1) * 256]], 1024, ("a", j, "u", bi))
                for f2 in range(2):
                    fc = bi * 2 + f2
                    h = fc // 2
                    bu = bank(); bm = bank()
                    for kc in range(8):
                        op('pe', 'matmul', out=PS[:, bu, 0:gn], lhsT=cw[:, kc, f2 * 128:(f2 + 1) * 128], rhs=XB[:, kc, g0:g0 + gn],
                           start=(kc == 0), stop=(kc == 7))
                    for i, (t0, nt) in enumerate(tiles):
                        wsp = WSS[0:64, h, :] if samp else WST[:, h, :]
                        op('pe', 'matmul', out=PS[:, bm, i * 128:i * 128 + nt], lhsT=VN[0:nt, i, fc * 128:(fc + 1) * 128], rhs=wsp,
                           start=True, stop=True)
                    tm = TMPA[tmc[0] % 2]
                    t2 = TMPB[tmc[0] % 2]
                    bbt = BBT[tmc[0] % 2]; tmc[0] += 1
                    op('act', 'activation', out=tm[:, 0:gn], in_=PS[:, bu, 0:gn], func=AF.Gelu, bias=vcol("a_b_u%d" % j, fc), scale=1.0)
                    nt = tiles[0][1]
                    rb = RSBC[:, h, :] if samp else RBC[:, h, :]
                    bs = BSSBC[:, h, :] if samp else BSBC[:, h, :]
                    op('dve', 'scalar_tensor_tensor', out=bbt[:, 0:nt], in0=rb, scalar=vcol("a_b%d" % j, fc), in1=bs, op0=ALU.mult, op1=ALU.add)
                    for i in range(len(tiles)):
                        op('dve', 'scalar_tensor_tensor', out=t2[:, i * 128:i * 128 + nt], in0=PS[:, bm, i * 128:i * 128 + nt],
                           scalar=vcol("a_g%d" % j, fc), in1=bbt[:, 0:nt], op0=ALU.mult, op1=ALU.add)
                    op('dve', 'tensor_tensor', out=YA[:, fc, 0:gn], in0=t2[:, 0:gn], in1=tm[:, 0:gn], op=ALU.mult)
                yield
            for m in range(8):
                cw = wload([w_out[:, m * 128:(m + 1) * 128]], 2048, ("a", j, "o", m))
                b = bank()
                for fc in range(16):
                    op('pe', 'matmul', out=PS[:, b, 0:gn], lhsT=cw[:, fc, :], rhs=YA[:, fc, 0:gn], start=(fc == 0), stop=(fc == 15))
                op('dve', 'scalar_tensor_tensor', out=XF[:, m, g0:g0 + gn], in0=XF[:, m, g0:g0 + gn], scalar=ALPHA,
                   in1=PS[:, b, 0:gn], op0=ALU.mult, op1=ALU.add)
                yield

        def interleave(ga, gb):
            gens = [g_ for g_ in (ga, gb) if g_ is not None]
            while gens:
                for g_ in list(gens):
                    try:
                        next(g_)
                    except StopIteration:
                        gens.remove(g_)

        interleave(p1(0), None)
        for gidx in range(len(GA)):
            interleave(p1(gidx + 1) if gidx + 1 < len(GA) else None, p23(gidx))

    def mixer_b(j):
        NEG_EM05 = -float(np.exp(-0.5))
        o1 = [XB_OFF]
        o2 = [LNT_OFF]

        def p1(shape, dt):
            nb_ = int(np.prod(shape[1:])) * (2 if dt == BF16 else 4)
            t = sub(o1[0], shape, dt)
            o1[0] = (o1[0] + nb_ + 63) // 64 * 64
            assert o1[0] <= XB_OFF + 8 * TOK * 2
            return t

        def p2(shape, dt, at=None):
            nb_ = int(np.prod(shape[1:])) * (2 if dt == BF16 else 4)
            if at is not None:
                return sub(at, shape, dt)
            t = sub(o2[0], shape, dt)
            o2[0] = (o2[0] + nb_ + 63) // 64 * 64
            assert o2[0] <= SB_END
            return t

        BLKf = p1([128, 128], F32)
        BLKb = p1([128, 128], BF16)
        IND2 = p1([128, 2], F32)
        EY = [p1([128, 256], BF16) for _ in range(2)]
        CARRY = p1([128, 8, 1], F32)
        SHIFTS = p1([128, 8, 16, 1], F32)
        CT2 = p1([128, 8, 16], F32)
        SEL = [p1([128, 64], BF16) for _ in range(4)]
        LOR = p1([128, 3, 128], BF16)
        RS = p1([128, 16, 1], F32)
        GS = p1([128, 4, 16, 1], F32)
        M_SU = p1([128, 128], F32)
        M_SL = p1([128, 128], F32)
        M_UI = p1([128, 128], F32)
        WC = p1([128, 8, 2, 1], F32)
        TMPS = p1([128, 2, 64], F32)
        tmps2_off = o1[0]
        o1[0] += 512
        g0t2_off = o1[0]
        o1[0] += 2048
        fm_off = o1[0]
        o1[0] += 5 * 4096 + 2 * 2048
        assert o1[0] <= XB_OFF + 8 * TOK * 2
        VTOK = p2([128, 1024], F32)
        VTOKB_OFF = o2[0]
        VTOKB = p2([128, 1024], BF16)
        GTOK = p2([128, 1024], F32)
        STS = [p2([128, 1, 512], F32) for _ in range(3)]
        ST_P = STS[0]
        grp_off = o2[0]
        GRP = [p2([128, 4, 128], F32) for _ in range(4)]
        MAKT, NRBT, NRKT, ZC = GRP
        PA, PTA, PB, PTB, ZCB = [p2([128, 4, 128], BF16) for _ in range(5)]
        ST_SS = [p2([128, 2, 512], F32, at=grp_off + i * 4096) for i in range(2)]
        PTMP_OFF = o2[0]
        o2[0] += 4096
        G0T = p2([128, 2, 2, 128], F32)
        H0 = p2([128, 2, 2, 64], F32)
        RQ4 = p2([128, 2, 2, 2, 128], F32)
        TMPS_2 = sub(tmps2_off, [128, 2, 64], F32)
        G0T_2 = sub(g0t2_off, [128, 2, 2, 128], F32)
        LXG = p2([128, 1024], F32)
        LXB = p2([128, 1024], F32)
        scr = o2[0]
        o2[0] += 20480
        assert o2[0] <= SB_END
        T1 = p2([128, 2, 512], F32, at=scr)
        T2 = p2([128, 2, 512], F32, at=scr + 4096)
        TMPb = p2([128, 2, 512], BF16, at=scr + 8192)
        TY = p2([128, 2, 512], BF16, at=scr + 10240)
        YT = p2([128, 16, 64], F32, at=scr + 12288)
        SQT = p2([128, 16, 64], F32, at=scr + 16384)
        SGT = p2([128, 1024], F32, at=scr + 12288)
        ATOK = p2([128, 1024], F32, at=scr)
        BTOK = p2([128, 1024], F32, at=scr + 4096)
        KTOK = p2([128, 1024], F32, at=scr + 8192)
        SN = p2([128, 1024], F32, at=scr + 12288)
        SO = p2([128, 1024], F32, at=scr + 16384)

        def fm_tiles(nt):
            d = {}
            names = ["R", "WDEC", "KMOD", "NKK", "KKA"]
            for i, nm in enumerate(names):
                d[nm] = sub(fm_off + i * 4096, [128, 8, nt], F32)
            d["MIX"] = [sub(fm_off + 5 * 4096 + i * 2048, [128, 8, nt], BF16) for i in range(2)]
            for i, nm in enumerate(["XX", "KRAW", "AA", "TA", "TB"]):
                d[nm] = sub(scr + i * 4096, [128, 8, nt], F32)
            d["ZFM"] = sub(scr + 8192, [128, 8, nt], BF16)
            return d

        op('dve', 'memset', ap=BLKf[:], constant=0.0)
        op('dve', 'memset', ap=BLKf[0:64, 0:64], constant=1.0)
        op('dve', 'memset', ap=BLKf[64:128, 64:128], constant=1.0)
        op('dve', 'tensor_copy', out=BLKb[:], in_=BLKf[:])
        op('dve', 'memset', ap=IND2[:], constant=0.0)
        op('dve', 'memset', ap=IND2[0:64, 0:1], constant=1.0)
        op('dve', 'memset', ap=IND2[64:128, 1:2], constant=1.0)
        for h2 in range(2):
            op('dve', 'memset', ap=EY[h2][:], constant=0.0)
            op('dve', 'memset', ap=EY[h2][h2 * 64:(h2 + 1) * 64, 127:128], constant=1.0)
        op('dve', 'memset', ap=CARRY[:], constant=0.0)
        op('dve', 'memset', ap=ST_P[:], constant=0.0)
        for msk, cm, pat, cmp_ in [(M_SU, -1, 1, ALU.is_gt), (M_SL, 1, -1, ALU.is_gt), (M_UI, -1, 1, ALU.is_ge)]:
            op('pool', 'memset', ap=msk[:], constant=1.0)
            op('pool', 'affine_select', out=msk[:], in_=msk[:], pattern=[[pat, 128]], compare_op=cmp_, fill=0.0, base=0, channel_multiplier=cm)
            op('pool', 'memset', ap=msk[0:64, 64:128], constant=0.0)
            op('pool', 'memset', ap=msk[64:128, 0:64], constant=0.0)
        stbase = [0]
        op('pool', 'memset', ap=G0T[:], constant=0.0)
        op('pool', 'memset', ap=RQ4[:], constant=0.0)
        op('sp', 'dma_start', out=LXG[:], in_=bcast_rows(W["b_lnx_g"][j:j + 1, :], 128))
        op('sp', 'dma_start', out=LXB[:], in_=bcast_rows(W["b_lnx_b"][j:j + 1, :], 128))
        op('sp', 'dma_start', out=SN[0:16, :], in_=shift_in)
        for hh in range(2):
            b = bank()
            for q in range(4):
                op('pe', 'transpose', out=PS[:, b, q * 16:(q + 1) * 16], in_=SN[0:16, (hh * 4 + q) * 128:(hh * 4 + q + 1) * 128],
                   identity=ident[0:16, 0:16])
            op('dve', 'tensor_copy', out=SHIFTS[:, hh * 4:(hh + 1) * 4, :, 0], in_=PS[:, b, 0:64].rearrange("p (c b) -> p c b", c=4))

        wr = W["b_w_rkv"][j, 0]; wk = W["b_w_rkv"][j, 1]; wv = W["b_w_rkv"][j, 2]
        mixc = [0]
        YB = [0, 1]
        SAB = [2, 3]
        VBB = [4, 5]
        selc = [0]

        def vc8(key):
            return VEC[:, VC[key]:VC[key] + 8]

        def b_tile(t0, nt, samp, first_tile, last_prompt):
            F = fm_tiles(nt)
            XX, KRAW, AA, TA, TB = F["XX"], F["KRAW"], F["AA"], F["TA"], F["TB"]
            R_, WDEC, KMOD, NKK, KKA, ZFM = F["R"], F["WDEC"], F["KMOD"], F["NKK"], F["KKA"], F["ZFM"]
            X = XF[:, :, t0:t0 + nt]

            def bc8(key, i0=0):
                return VEC[:, VC[key] + i0:VC[key] + i0 + 8].unsqueeze(2).to_broadcast([128, 8, nt])
            if not samp:
                op('dve', 'tensor_tensor', out=XX[:, :, 1:nt], in0=XF[:, :, t0:t0 + nt - 1], in1=XF[:, :, t0 + 1:t0 + nt], op=ALU.subtract)
                op('dve', 'tensor_tensor', out=XX[:, :, 0:1], in0=CARRY[:], in1=XF[:, :, t0:t0 + 1], op=ALU.subtract)
                op('act', 'activation', out=CARRY[:], in_=XF[:, :, t0 + nt - 1:t0 + nt], func=AF.Copy)
                if last_prompt:
                    b = bank()
                    op('pe', 'transpose', out=PS[0:8, b, 0:128], in_=CARRY[:, :, 0], identity=ident[:])
                    op('dve', 'tensor_copy', out=SO[0:8, 0:128], in_=PS[0:8, b, 0:128])
                    op('sp', 'dma_start', out=o_shift_p.rearrange("o (c p) -> (o c) p", p=128), in_=SO[0:8, 0:128])
            else:
                for c in range(8):
                    xv_ = XF[:, c, t0:t0 + nt].rearrange("p (b t) -> p b t", t=4)
                    xxv = XX[:, c, :].rearrange("p (b t) -> p b t", t=4)
                    op('dve', 'tensor_tensor', out=xxv[:, :, 1:4], in0=xv_[:, :, 0:3], in1=xv_[:, :, 1:4], op=ALU.subtract)
                    op('dve', 'tensor_tensor', out=xxv[:, :, 0:1], in0=SHIFTS[:, c, :, :], in1=xv_[:, :, 0:1], op=ALU.subtract)
                    op('dve', 'tensor_copy', out=CT2[:, c, :], in_=xv_[:, :, 3])
                for hh in range(2):
                    b = bank()
                    for q in range(4):
                        op('pe', 'transpose', out=PS[0:16, b, q * 128:(q + 1) * 128], in_=CT2[:, hh * 4 + q, :], identity=ident[:])
                    op('dve', 'tensor_copy', out=SO[0:16, hh * 512:(hh + 1) * 512], in_=PS[0:16, b, :])
                op('sp', 'dma_start', out=o_shift_s, in_=SO[0:16, :])

            PTMP = sub(PTMP_OFF, [128, 8, nt], F32)

            def mix(i):
                mx = F["MIX"][mixc[0] % 2]; mixc[0] += 1
                op('dve', 'tensor_tensor', out=PTMP[:], in0=XX[:], in1=bc8("b_mu", i * 8), op=ALU.mult)
                op('dve', 'tensor_tensor', out=mx[:], in0=PTMP[:], in1=X, op=ALU.add)
                return mx

            def proj_fm(mx, wmat, epi, kname):
                def body(mb, cw):
                    for mm_ in range(2):
                        m = mb * 2 + mm_
                        b = bank()
                        for kc in range(8):
                            op('pe', 'matmul', out=PS[:, b, 0:nt], lhsT=cw[:, kc, mm_ * 128:(mm_ + 1) * 128], rhs=mx[:, kc, :],
                               start=(kc == 0), stop=(kc == 7))
                        epi(m, PS[:, b, 0:nt])
                pipelined([[wmat[:, mb * 256:(mb + 1) * 256]] for mb in range(4)], 1024, body, keys=[("b", kname, mb) for mb in range(4)])

            mx = mix(0)
            proj_fm(mx, wr, lambda m, ps: op('act', 'activation', out=R_[:, m, :], in_=ps, func=AF.Copy), "r")
            mx = mix(1)
            cw = wload([W["b_w1"][j]], 1024, key=("b", "b_w1"))
            b = bank()
            for kc in range(8):
                op('pe', 'matmul', out=PS[0:64, b, 0:nt], lhsT=cw[:, kc, 0:64], rhs=mx[:, kc, :], start=(kc == 0), stop=(kc == 7))
            op('act', 'activation', out=LOR[0:64, 0, 0:nt], in_=PS[0:64, b, 0:nt], func=AF.Tanh)
            cw = wload([W["b_w2"][j]], 64, key=("b", "b_w2"))
            for m in range(8):
                b = bank()
                op('pe', 'matmul', out=PS[:, b, 0:nt], lhsT=cw[0:64, 0, m * 128:(m + 1) * 128], rhs=LOR[0:64, 0, 0:nt], start=True, stop=True)
                op('act', 'activation', out=WDEC[:, m, :], in_=PS[:, b, 0:nt], func=AF.Sigmoid, bias=vcol("b_w0", m), scale=1.0)
            if samp:
                op('act', 'activation', out=WDEC[:], in_=WDEC[:], func=AF.Exp, scale=NEG_EM05)
            mx = mix(2)
            proj_fm(mx, wk, lambda m, ps: op('act', 'activation', out=KRAW[:, m, :], in_=ps, func=AF.Copy), "k")
            mx = mix(3)

            def body_v(q, cw):
                b = bank()
                for kc in range(8):
                    op('pe', 'matmul', out=PS[0:nt, b, 0:256], lhsT=mx[:, kc, :], rhs=cw[:, kc, :], start=(kc == 0), stop=(kc == 7))
                op('act', 'activation', out=VTOK[0:nt, q * 256:(q + 1) * 256], in_=PS[0:nt, b, 0:256], func=AF.Copy)
            pipelined([[wv[:, q * 256:(q + 1) * 256]] for q in range(4)], 1024, body_v, keys=[("b", "v", q) for q in range(4)])
            if samp:
                op('act', 'activation', out=VTOKB[0:nt, :], in_=VTOK[0:nt, :], func=AF.Copy)
            mx = mix(4)
            cw = wload([W["b_a1"][j]], 1024, key=("b", "b_a1"))
            b = bank()
            for kc in range(8):
                op('pe', 'matmul', out=PS[0:64, b, 0:nt], lhsT=cw[:, kc, 0:64], rhs=mx[:, kc, :], start=(kc == 0), stop=(kc == 7))
            op('act', 'activation', out=LOR[0:64, 1, 0:nt], in_=PS[0:64, b, 0:nt], func=AF.Copy)
            cw = wload([W["b_a2"][j]], 64, key=("b", "b_a2"))
            for m in range(8):
                b = bank()
                op('pe', 'matmul', out=PS[:, b, 0:nt], lhsT=cw[0:64, 0, m * 128:(m + 1) * 128], rhs=LOR[0:64, 1, 0:nt], start=True, stop=True)
                op('act', 'activation', out=AA[:, m, :], in_=PS[:, b, 0:nt], func=AF.Sigmoid, bias=vcol("b_a0", m), scale=1.0)
            mx = mix(5)
            cw = wload([W["b_g1"][j]], 1024, key=("b", "b_g1"))
            b = bank()
            for kc in range(8):
                op('pe', 'matmul', out=PS[:, b, 0:nt], lhsT=cw[:, kc, 0:128], rhs=mx[:, kc, :], start=(kc == 0), stop=(kc == 7))
            op('act', 'activation', out=LOR[:, 2, 0:nt], in_=PS[:, b, 0:nt], func=AF.Sigmoid)
            cw = wload([W["b_g2"][j]], 128, key=("b", "b_g2"))
            for q in range(2):
                b = bank()
                op('pe', 'matmul', out=PS[0:nt, b, :], lhsT=LOR[:, 2, 0:nt], rhs=cw[:, 0, q * 512:(q + 1) * 512], start=True, stop=True)
                op('act', 'activation', out=GTOK[0:nt, q * 512:(q + 1) * 512], in_=PS[0:nt, b, :], func=AF.Copy)

            op('dve', 'tensor_tensor', out=TA[:], in0=KRAW[:], in1=bc8("b_k_k"), op=ALU.mult)
            op('dve', 'tensor_tensor', out=TB[:], in0=TA[:], in1=TA[:], op=ALU.mult)
            taf = TA[:].rearrange("p c t -> p (c t)")
            tbf = TB[:].rearrange("p c t -> p (c t)")
            nflat = 8 * nt
            for q0 in range(0, nflat, 512):
                b = bank()
                op('pe', 'matmul', out=PS[:, b, :], lhsT=BLKf[:], rhs=tbf[:, q0:q0 + 512], start=True, stop=True)
                op('act', 'activation', out=tbf[:, q0:q0 + 512], in_=PS[:, b, :], func=AF.Sqrt)
            op('dve', 'tensor_scalar', out=TB[:], in0=TB[:], scalar1=1e-12, scalar2=None, op0=ALU.max)
            op('dve', 'reciprocal', out=TB[:], in_=TB[:])
            op('dve', 'tensor_tensor', out=TA[:], in0=TA[:], in1=TB[:], op=ALU.mult)
            op('act', 'activation', out=NKK[:], in_=TA[:], func=AF.Copy, scale=-1.0)
            op('dve', 'tensor_tensor', out=KKA[:], in0=TA[:], in1=AA[:], op=ALU.mult)
            op('dve', 'scalar_tensor_tensor', out=TB[:], in0=AA[:], scalar=-1.0, in1=bc8("b_k_a"), op0=ALU.add, op1=ALU.mult)
            op('dve', 'scalar_tensor_tensor', out=KMOD[:], in0=TB[:], scalar=1.0, in1=KRAW[:], op0=ALU.add, op1=ALU.mult)
            op('dve', 'tensor_tensor', out=TB[:], in0=R_[:], in1=KMOD[:], op=ALU.mult)
            op('dve', 'tensor_tensor', out=TB[:], in0=TB[:], in1=bc8("b_r_k"), op=ALU.mult)
            b = bank()
            for c in range(8):
                op('pe', 'matmul', out=PS[0:nt, b, c * 2:(c + 1) * 2], lhsT=TB[:, c, :], rhs=IND2[:], start=True, stop=True)
            op('dve', 'tensor_copy', out=RS[0:nt, :, 0], in_=PS[0:nt, b, 0:16])

            for bb_ in range(6):
                reserved.add(bb_)
            stepc = [0]

            def scan_step(toks, ST, nb, first, last, t_local, b0):
                k = stepc[0]; stepc[0] += 1
                S = ST[:, 0:nb, :].rearrange("p b (c v) -> p b c v", v=64)

                def bcv(T):
                    if not samp:
                        return T[:, :, toks[0]:toks[0] + 1].unsqueeze(1).to_broadcast([128, 1, 8, 64])
                    v4 = T[:, :, :].rearrange("p c (b t) -> p c b t", t=4)[:, :, b0:b0 + nb, t_local:t_local + 1]
                    return v4.rearrange("p c b o -> p b c o").to_broadcast([128, nb, 8, 64])

                def v4(tile_):
                    return tile_[:, 0:nb, :].rearrange("p b (c v) -> p b c v", v=64)
                if nb == 1:
                    sa = [SAB[k % 2]]; vb = [VBB[k % 2]]
                    sa_ps = PS[:, sa[0]:sa[0] + 1, :]
                    vb_ps = PS[:, vb[0]:vb[0] + 1, :]
                else:
                    sa = SAB; vb = VBB
                    sa_ps = PS[:, 2:4, :]
                    vb_ps = PS[:, 4:6, :]
                op('dve', 'tensor_tensor', out=v4(TMPb), in0=S, in1=bcv(NKK), op=ALU.mult)
                for bi in range(nb):
                    op('pe', 'matmul', out=PS[:, sa[bi], :], lhsT=BLKb[:], rhs=TMPb[:, bi, :], start=True, stop=True)
                for bi in range(nb):
                    sl = SEL[selc[0] % 4]; selc[0] += 1
                    op('pool', 'tensor_copy', out=sl[0:nt, :], in_=identb[0:nt, toks[bi]:toks[bi] + 1].to_broadcast([nt, 64]))
                    vsrc = VTOKB[0:nt, :].rearrange("t (c h v) -> t c h v", h=2, v=64)
                    for h2 in range(2):
                        op('pe', 'matmul', out=PS[h2 * 64:(h2 + 1) * 64, vb[bi], :], lhsT=sl[0:nt, :], rhs=vsrc[:, :, h2, :],
                           start=True, stop=True)
                op('dve', 'tensor_tensor', out=v4(T1), in0=sa_ps.rearrange("p b (c v) -> p b c v", v=64), in1=bcv(KKA), op=ALU.mult)
                op('dve', 'tensor_tensor', out=S, in0=S, in1=bcv(WDEC), op=ALU.mult)
                op('dve', 'tensor_tensor', out=S, in0=S, in1=v4(T1), op=ALU.add)
                op('dve', 'tensor_tensor', out=v4(T2), in0=vb_ps.rearrange("p b (c v) -> p b c v", v=64), in1=bcv(KMOD), op=ALU.mult)
                op('dve', 'tensor_tensor', out=S, in0=S, in1=v4(T2), op=ALU.add)
                op('dve', 'tensor_tensor', out=v4(TY), in0=S, in1=bcv(R_), op=ALU.mult)
                for bi in range(nb):
                    for h2 in range(2):
                        op('pe', 'matmul', out=PS[0:nt, YB[h2], :], lhsT=EY[h2][:, 127 - toks[bi]:127 - toks[bi] + nt], rhs=TY[:, bi, :],
                           start=(first and bi == 0), stop=(last and bi == nb - 1))


            def chunked():
                SG = WDEC
                for hh in range(2):
                    b = bank()
                    for q in range(4):
                        op('pe', 'transpose', out=PS[:, b, q * 128:(q + 1) * 128], in_=SG[:, hh * 4 + q, :], identity=ident[:])
                    op('act', 'activation', out=SGT[:, hh * 512:(hh + 1) * 512], in_=PS[:, b, :], func=AF.Copy)
                EIN, EINV, EEX = XX, KRAW, AA
                for hh in range(2):
                    bi = bank(); be = bank()
                    for q in range(4):
                        c = hh * 4 + q
                        op('pe', 'matmul', out=PS[:, bi, q * 128:(q + 1) * 128], lhsT=SGT[:, c * 128:(c + 1) * 128], rhs=M_UI[:], start=True, stop=True)
                        op('pe', 'matmul', out=PS[:, be, q * 128:(q + 1) * 128], lhsT=SGT[:, c * 128:(c + 1) * 128], rhs=M_SU[:], start=True, stop=True)
                    pvi = PS[:, bi, :].rearrange("p (a t) -> p a t", a=4)
                    pve = PS[:, be, :].rearrange("p (a t) -> p a t", a=4)
                    op('act', 'activation', out=EIN[:, hh * 4:(hh + 1) * 4, :], in_=pvi, func=AF.Exp, scale=NEG_EM05)
                    op('act', 'activation', out=EINV[:, hh * 4:(hh + 1) * 4, :], in_=pvi, func=AF.Exp, scale=-NEG_EM05)
                    op('act', 'activation', out=EEX[:, hh * 4:(hh + 1) * 4, :], in_=pve, func=AF.Exp, scale=NEG_EM05)
                op('act', 'activation', out=WC[:], in_=EIN[:].rearrange("p c (q t) -> p c q t", t=64)[:, :, :, 63:64], func=AF.Copy)
                At, Bt, Kt, Rt = NKK, KKA, KMOD, R_
                op('dve', 'tensor_tensor', out=At[:], in0=NKK[:], in1=EEX[:], op=ALU.mult)
                op('dve', 'tensor_tensor', out=Bt[:], in0=KKA[:], in1=EINV[:], op=ALU.mult)
                op('dve', 'tensor_tensor', out=Kt[:], in0=KMOD[:], in1=EINV[:], op=ALU.mult)
                op('dve', 'tensor_tensor', out=Rt[:], in0=R_[:], in1=EIN[:], op=ALU.mult)
                for src, dst in [(At, ATOK), (Bt, BTOK), (Kt, KTOK)]:
                    for hh in range(2):
                        b = bank()
                        for q in range(4):
                            op('pe', 'transpose', out=PS[:, b, q * 128:(q + 1) * 128], in_=src[:, hh * 4 + q, :], identity=ident[:])
                        op('act', 'activation', out=dst[:, hh * 512:(hh + 1) * 512], in_=PS[:, b, :], func=AF.Copy)
                base = stbase[0]
                T1s = dict(PA=PA, PTA=PTA, PB=PB, PTB=PTB, MAKT=MAKT, NRBT=NRBT, NRKT=NRKT, ZC=ZC, ZCB=ZCB, G0T=G0T, H0=H0, RQ4=RQ4, TMPS=TMPS)
                s4 = scr + 16384
                wd = fm_off + 1 * 4096
                mxo = fm_off + 5 * 4096
                T2s = dict(MAKT=sub(s4, [128, 4, 128], F32), NRBT=sub(s4 + 2048, [128, 4, 128], F32),
                           NRKT=sub(wd, [128, 4, 128], F32), ZC=sub(wd + 2048, [128, 4, 128], F32),
                           PA=sub(mxo, [128, 4, 128], BF16), PTA=sub(mxo + 1024, [128, 4, 128], BF16),
                           PB=sub(mxo + 2048, [128, 4, 128], BF16), PTB=sub(mxo + 3072, [128, 4, 128], BF16),
                           ZCB=sub(VTOKB_OFF, [128, 4, 128], BF16), H0=sub(VTOKB_OFF + 1024, [128, 2, 2, 64], F32),
                           RQ4=sub(PTMP_OFF, [128, 2, 2, 2, 128], F32), G0T=G0T_2, TMPS=TMPS_2)
                op('pool', 'memset', ap=T2s['RQ4'][:], constant=0.0)
                op('pool', 'memset', ap=T2s['G0T'][:], constant=0.0)

                def grp_gen(g, T):
                    PA_, PTA_, PB_, PTB_ = T['PA'], T['PTA'], T['PB'], T['PTB']
                    MAKT_, NRBT_, NRKT_, ZC_, ZCB_ = T['MAKT'], T['NRBT'], T['NRKT'], T['ZC'], T['ZCB']
                    G0T_, H0_, RQ4_, TMPS_ = T['G0T'], T['H0'], T['RQ4'], T['TMPS']
                    heads = [(2 * g + cc, h2) for cc in range(2) for h2 in range(2)]

                    def fm(Tl, c, h2):
                        return Tl[h2 * 64:(h2 + 1) * 64, c, :]

                    def col(hl):
                        return g * 256 + hl * 64

                    def v4(ps_ap, a):
                        return ps_ap.rearrange("p (a t) -> p a t", a=a)
                    for dst, lt, rt, msk in [(PA_, Bt, At, M_SU), (PTA_, At, Bt, M_SL), (MAKT_, Kt, At, M_SU), (NRBT_, Bt, Rt, M_UI), (NRKT_, Kt, Rt, M_UI)]:
                        for h2 in range(2):
                            b = bank()
                            for cc in range(2):
                                c = 2 * g + cc
                                op('pe', 'matmul', out=PS[:, b, cc * 128:(cc + 1) * 128], lhsT=fm(lt, c, h2), rhs=fm(rt, c, h2), start=True, stop=True)
                            dv = dst[:].rearrange("p (cc h) t -> p cc h t", h=2)[:, :, h2, :]
                            op('dve', 'tensor_tensor', out=dv, in0=v4(PS[:, b, 0:256], 2), in1=msk[:].unsqueeze(1).to_broadcast([128, 2, 128]), op=ALU.mult)
                        yield
                    for cc in range(2):
                        hs = slice(2 * cc, 2 * cc + 2)
                        b = bank()
                        for hl in range(2 * cc, 2 * cc + 2):
                            op('pe', 'matmul', out=PS[:, b, (hl % 2) * 64:(hl % 2 + 1) * 64], lhsT=MAKT_[:, hl, :], rhs=VTOK[:, col(hl):col(hl) + 64], start=True, stop=True)
                        op('act', 'activation', out=ZC_[:, hs, 64:128], in_=v4(PS[:, b, 0:128], 2), func=AF.Copy)
                        op('dve', 'tensor_copy', out=ZC_[:, hs, 0:64], in_=ATOK[:, g * 256 + cc * 128:g * 256 + (cc + 1) * 128].rearrange("p (a t) -> p a t", a=2))
                        op('act', 'activation', out=ZCB_[:, hs, :], in_=ZC_[:, hs, :], func=AF.Copy)
                    yield
                    Pc, PTc, Pn, PTn = PA_, PTA_, PB_, PTB_
                    for s_ in range(6):
                        for cc in range(2):
                            hs = slice(2 * cc, 2 * cc + 2)
                            b = bank()
                            for hl in range(2 * cc, 2 * cc + 2):
                                op('pe', 'matmul', out=PS[:, b, (hl % 2) * 128:(hl % 2 + 1) * 128], lhsT=Pc[:, hl, :], rhs=ZCB_[:, hl, :], start=True, stop=True)
                            op('dve', 'tensor_tensor', out=ZC_[:, hs, :], in0=ZC_[:, hs, :], in1=v4(PS[:, b, 0:256], 2), op=ALU.add)
                            if s_ < 5:
                                op('act', 'activation', out=ZCB_[:, hs, :], in_=ZC_[:, hs, :], func=AF.Copy)
                                b1 = bank(); b2 = bank()
                                for hl in range(2 * cc, 2 * cc + 2):
                                    op('pe', 'matmul', out=PS[:, b1, (hl % 2) * 128:(hl % 2 + 1) * 128], lhsT=PTc[:, hl, :], rhs=Pc[:, hl, :], start=True, stop=True)
                                for hl in range(2 * cc, 2 * cc + 2):
                                    op('pe', 'matmul', out=PS[:, b2, (hl % 2) * 128:(hl % 2 + 1) * 128], lhsT=Pc[:, hl, :], rhs=PTc[:, hl, :], start=True, stop=True)
                                op('act', 'activation', out=Pn[:, hs, :], in_=v4(PS[:, b1, 0:256], 2), func=AF.Copy)
                                op('dve', 'tensor_copy', out=PTn[:, hs, :], in_=v4(PS[:, b2, 0:256], 2))
                            yield
                        Pc, PTc, Pn, PTn = Pn, PTn, Pc, PTc
                    b = bank()
                    for hl, (c, h2) in enumerate(heads):
                        cc = c - 2 * g
                        op('pe', 'matmul', out=PS[h2 * 64:(h2 + 1) * 64, b, cc * 128:(cc + 1) * 128], lhsT=ZC_[:, hl, 0:64], rhs=NRBT_[:, hl, :], start=True, stop=True)
                    for h2 in range(2):
                        for q in range(2):
                            hp = slice(h2 * 64, (h2 + 1) * 64)
                            op('dve', 'tensor_tensor', out=RQ4_[hp, q, :, h2, q * 64:(q + 1) * 64], in0=Rt[hp, 2 * g:2 * g + 2, q * 64:(q + 1) * 64],
                               in1=v4(PS[hp, b, 0:256], 2)[:, :, q * 64:(q + 1) * 64], op=ALU.add)
                    yield
                    for q in range(2):
                        r0, r1 = q * 64, (q + 1) * 64
                        bg = bank(); bh = bank()
                        for hl, (c, h2) in enumerate(heads):
                            cc = c - 2 * g
                            hp = slice(h2 * 64, (h2 + 1) * 64)
                            op('pe', 'matmul', out=PS[hp, bg, cc * 64:(cc + 1) * 64], lhsT=ZC_[r0:r1, hl, 0:64], rhs=BTOK[r0:r1, col(hl):col(hl) + 64], start=True, stop=True)
                            op('pe', 'matmul', out=PS[hp, bh, cc * 64:(cc + 1) * 64], lhsT=BTOK[r0:r1, col(hl):col(hl) + 64], rhs=ZC_[r0:r1, hl, 64:128], start=True, stop=False)
                            op('pe', 'matmul', out=PS[hp, bh, cc * 64:(cc + 1) * 64], lhsT=KTOK[r0:r1, col(hl):col(hl) + 64], rhs=VTOK[r0:r1, col(hl):col(hl) + 64], start=False, stop=True)
                        for h2 in range(2):
                            hp = slice(h2 * 64, (h2 + 1) * 64)
                            op('act', 'activation', out=G0T_[hp, :, q, h2 * 64:(h2 + 1) * 64], in_=v4(PS[hp, bg, 0:128], 2), func=AF.Copy)
                        op('act', 'activation', out=H0_[:, :, q, :], in_=v4(PS[:, bh, 0:128], 2), func=AF.Copy)
                        yield
                    for q in range(2):
                        Sc = STS[(base + q) % 3][:, 0, :].rearrange("p (c v) -> p c v", v=64)
                        Sn = STS[(base + q + 1) % 3][:, 0, :].rearrange("p (c v) -> p c v", v=64)
                        b = bank()
                        for cc in range(2):
                            c = 2 * g + cc
                            op('pe', 'matmul', out=PS[:, b, cc * 64:(cc + 1) * 64], lhsT=G0T_[:, cc, q, :], rhs=Sc[:, c, :], start=True, stop=True)
                        op('dve', 'tensor_tensor', out=TMPS_[:], in0=v4(PS[:, b, 0:128], 2), in1=Sc[:, 2 * g:2 * g + 2, :], op=ALU.add)
                        op('dve', 'tensor_tensor', out=TMPS_[:], in0=TMPS_[:], in1=H0_[:, :, q, :], op=ALU.add)
                        op('dve', 'tensor_tensor', out=Sn[:, 2 * g:2 * g + 2, :], in0=TMPS_[:], in1=WC[:, 2 * g:2 * g + 2, q, :].to_broadcast([128, 2, 64]), op=ALU.mult)
                        yield
                    by = bank()
                    for hl, (c, h2) in enumerate(heads):
                        cc = c - 2 * g
                        o0 = hl * 64
                        op('pe', 'matmul', out=PS[:, by, o0:o0 + 64], lhsT=NRBT_[:, hl, :], rhs=ZC_[:, hl, 64:128], start=True, stop=False)
                        op('pe', 'matmul', out=PS[:, by, o0:o0 + 64], lhsT=NRKT_[:, hl, :], rhs=VTOK[:, col(hl):col(hl) + 64], start=False, stop=False)
                        for q in range(2):
                            Sq = STS[(base + q) % 3][:, 0, :].rearrange("p (c v) -> p c v", v=64)
                            op('pe', 'matmul', out=PS[:, by, o0:o0 + 64], lhsT=RQ4_[:, q, cc, h2, :], rhs=Sq[:, c, :], start=False, stop=(q == 1))
                    op('act', 'activation', out=YT[:, g * 4:(g + 1) * 4, :], in_=v4(PS[:, by, 0:256], 4), func=AF.Copy)
                    yield

                def interleave2(ga, gb):
                    gens = [ga, gb]
                    while gens:
                        for g_ in list(gens):
                            try:
                                next(g_)
                            except StopIteration:
                                gens.remove(g_)

                interleave2(grp_gen(0, T1s), grp_gen(1, T2s))
                interleave2(grp_gen(2, T1s), grp_gen(3, T2s))
                stbase[0] = (base + 2) % 3

            def store_state(ST, bi, dst):
                for hh in range(2):
                    b = bank()
                    for q in range(4):
                        c = hh * 4 + q
                        op('pe', 'transpose', out=PS[0:64, b, q * 128:(q + 1) * 128], in_=ST[:, bi, c * 64:(c + 1) * 64], identity=ident[:])
                    op('act', 'activation', out=SO[0:64, hh * 512:(hh + 1) * 512], in_=PS[0:64, b, :], func=AF.Copy)
                op('sp', 'dma_start', out=dst.rearrange("h v k -> v h k"), in_=SO[0:64, :].rearrange("v (h k) -> v h k", k=64))

            if not samp:
                for bb_ in range(6):
                    reserved.discard(bb_)
                chunked()
                if last_prompt:
                    store_state(STS[stbase[0]], 0, o_wkv_p)
            else:
                def load_states(g):
                    for bi in range(2):
                        op('sp', 'dma_start', out=SN[0:64, :].rearrange("v (h k) -> v h k", k=64), in_=wkv_in[g * 2 + bi].rearrange("h v k -> v h k"))
                        b = bank()
                        for c in range(8):
                            op('pe', 'transpose', out=PS[:, b, c * 64:(c + 1) * 64], in_=SN[0:64, c * 128:(c + 1) * 128], identity=ident[0:64, 0:64])
                        op('act', 'activation', out=ST_SS[g % 2][:, bi, :], in_=PS[:, b, :], func=AF.Copy)
                load_states(0)
                for g in range(8):
                    b0 = g * 2
                    ST_S = ST_SS[g % 2]
                    if g + 1 < 8:
                        load_states(g + 1)
                    for t in range(4):
                        toks = [(b0 + bi) * 4 + t for bi in range(2)]
                        scan_step(toks, ST_S, 2, (g == 0 and t == 0), (g == 7 and t == 3), t, b0)
                    for bi in range(2):
                        store_state(ST_S, bi, o_wkv_s[b0 + bi])
            for bb_ in range(6):
                reserved.discard(bb_)

            if samp:
                op('act', 'activation', out=YT[0:nt, :, :].rearrange("t (c h) v -> t c h v", h=2)[:, :, 0, :],
                   in_=PS[0:nt, YB[0], :].rearrange("t (c v) -> t c v", v=64), func=AF.Copy)
                op('dve', 'tensor_copy', out=YT[0:nt, :, :].rearrange("t (c h) v -> t c h v", h=2)[:, :, 1, :],
                   in_=PS[0:nt, YB[1], :].rearrange("t (c v) -> t c v", v=64))
            op('dve', 'tensor_reduce', out=GS[0:nt, 0, :, 0], in_=YT[0:nt, :, :], axis=AX.X, op=ALU.add)
            op('act', 'activation', out=SQT[0:nt, :, :], in_=YT[0:nt, :, :], func=AF.Square)
            op('dve', 'tensor_reduce', out=GS[0:nt, 1, :, 0], in_=SQT[0:nt, :, :], axis=AX.X, op=ALU.add)
            op('dve', 'tensor_scalar', out=GS[0:nt, 2, :, :], in0=GS[0:nt, 0, :, :], scalar1=1.0 / 64, scalar2=None, op0=ALU.mult)
            op('dve', 'tensor_tensor', out=GS[0:nt, 3, :, :], in0=GS[0:nt, 2, :, :], in1=GS[0:nt, 2, :, :], op=ALU.mult)
            op('dve', 'scalar_tensor_tensor', out=GS[0:nt, 1, :, :], in0=GS[0:nt, 1, :, :], scalar=1.0 / 64, in1=GS[0:nt, 3, :, :],
               op0=ALU.mult, op1=ALU.subtract)
            op('act', 'activation', out=GS[0:nt, 1, :, :], in_=GS[0:nt, 1, :, :], func=AF.Sqrt, bias=GN_EPS, scale=1.0)
            op('dve', 'reciprocal', out=GS[0:nt, 1, :, :], in_=GS[0:nt, 1, :, :])
            op('dve', 'tensor_tensor', out=YT[0:nt], in0=YT[0:nt], in1=GS[0:nt, 2, :, :].to_broadcast([nt, 16, 64]), op=ALU.subtract)
            op('dve', 'tensor_tensor', out=YT[0:nt], in0=YT[0:nt], in1=GS[0:nt, 1, :, :].to_broadcast([nt, 16, 64]), op=ALU.mult)
            ytf = YT[0:nt, :, :].rearrange("t h v -> t (h v)")
            op('dve', 'tensor_tensor', out=ytf, in0=ytf, in1=LXG[0:nt, :], op=ALU.mult)
            op('dve', 'tensor_tensor', out=ytf, in0=ytf, in1=LXB[0:nt, :], op=ALU.add)
            op('dve', 'tensor_tensor', out=SQT[0:nt], in0=VTOK[0:nt, :].rearrange("t (h v) -> t h v", v=64),
               in1=RS[0:nt, :, :].to_broadcast([nt, 16, 64]), op=ALU.mult)
            op('dve', 'tensor_tensor', out=YT[0:nt], in0=YT[0:nt], in1=SQT[0:nt], op=ALU.add)
            op('dve', 'tensor_tensor', out=ytf, in0=ytf, in1=GTOK[0:nt, :], op=ALU.mult)
            for hh in range(2):
                b = bank()
                for q in range(4):
                    c = hh * 4 + q
                    op('pe', 'transpose', out=PS[:, b, q * nt:(q + 1) * nt], in_=ytf[:, c * 128:(c + 1) * 128], identity=ident[0:nt, 0:nt])
                op('act', 'activation', out=ZFM[:, hh * 4:(hh + 1) * 4, :], in_=PS[:, b, 0:4 * nt].rearrange("p (a t) -> p a t", a=4), func=AF.Copy)

            def body_o(mb, cw):
                for mm_ in range(2):
                    m = mb * 2 + mm_
                    b = bank()
                    for kc in range(8):
                        op('pe', 'matmul', out=PS[:, b, 0:nt], lhsT=cw[:, kc, mm_ * 128:(mm_ + 1) * 128], rhs=ZFM[:, kc, :],
                           start=(kc == 0), stop=(kc == 7))
                    op('dve', 'scalar_tensor_tensor', out=XF[:, m, t0:t0 + nt], in0=XF[:, m, t0:t0 + nt], scalar=ALPHA,
                       in1=PS[:, b, 0:nt], op0=ALU.mult, op1=ALU.add)
            pipelined([[W["b_w_o"][j][:, mb * 256:(mb + 1) * 256]] for mb in range(4)], 1024, body_o, keys=[("b", "o", mb) for mb in range(4)])

        for ti, (t0, nt) in enumerate(T128):
            b_tile(t0, nt, t0 >= 2048, ti == 0, ti == 15)

    def warm_cache(kind, j):
        specs = []
        if kind == 0:
            w_in = W["a_w_in"][j]; w_out = W["a_w_out"][j]
            specs += [([w_in[:, 2048 + q * 256:2048 + (q + 1) * 256]], 1024, ("a", j, "v", q)) for q in range(8)]
            specs += [([w_in[:, bi * 256:(bi + 1) * 256]], 1024, ("a", j, "u", bi)) for bi in range(8)]
            specs += [([w_out[:, m * 128:(m + 1) * 128]], 2048, ("a", j, "o", m)) for m in range(8)]
        elif kind == 1:
            wr = W["b_w_rkv"][j, 0]; wk = W["b_w_rkv"][j, 1]; wv = W["b_w_rkv"][j, 2]
            specs += [([wr[:, mb * 256:(mb + 1) * 256]], 1024, ("b", "r", mb)) for mb in range(4)]
            specs += [([W["b_w1"][j]], 1024, ("b", "b_w1")), ([W["b_w2"][j]], 64, ("b", "b_w2"))]
            specs += [([wk[:, mb * 256:(mb + 1) * 256]], 1024, ("b", "k", mb)) for mb in range(4)]
            specs += [([wv[:, q * 256:(q + 1) * 256]], 1024, ("b", "v", q)) for q in range(4)]
            specs += [([W["b_a1"][j]], 1024, ("b", "b_a1")), ([W["b_a2"][j]], 64, ("b", "b_a2")),
                      ([W["b_g1"][j]], 1024, ("b", "b_g1")), ([W["b_g2"][j]], 128, ("b", "b_g2"))]
            specs += [([W["b_w_o"][j][:, mb * 256:(mb + 1) * 256]], 1024, ("b", "o", mb)) for mb in range(4)]
        else:
            w_in = W["c_w_in"][j]; w_out = W["c_w_out"][j]
            specs += [([w_in[:, 1024 + m * 128:1024 + (m + 1) * 128], w_in[:, 2048 + m * 128:2048 + (m + 1) * 128],
                        w_in[:, m * 128:(m + 1) * 128]], 1024, ("c", "in", m)) for m in range(8)]
            specs += [([w_out[:, mb * 256:(mb + 1) * 256]], 1024, ("c", "out", mb)) for mb in range(4)]
        for parts, K, key in specs:
            wload(parts, K, key)
            yield

    def kind_of(l):
        return (l % 3) if kinds is None else kinds[l]

    load_x()
    for l in range(n_layers):
        ffn(l, 0)
        layer_norm(l * 3 + 0, 4 * LN_EPS)
        kind = (l % 3) if kinds is None else kinds[l]
        if stub_mixers:
            mixer_stub()
        elif kind == 0:
            mixer_a(l // 3 if kinds is None else 0)
        elif kind == 1:
            mixer_b(0)
        else:
            mixer_c(0)
        ffn(l, 1, pre_ln=(l * 3 + 1, LN_EPS))
        layer_norm(l * 3 + 2, 4 * LN_EPS)
        ple(l)
    store_y()
    if DRY:
        return REQ
    P.emit()
    return nc, P


def _shard_inputs(inp, c):
    f = lambda a: np.ascontiguousarray(a, dtype=np.float32)
    sl = slice(16 * c, 16 * c + 16)
    m = {}
    m["x_p"] = f(inp["x_prompt"][c])
    m["x_s"] = f(inp["x_sample"][sl].reshape(64, 1024))
    m["wkv_in"] = f(inp["state_b_wkv"][0, sl])
    m["shift_in"] = f(inp["state_b_shift"][0, sl])
    m["conv_in"] = f(inp["state_c_conv"][0, sl])
    m["p_p"] = f(inp["p_prompt"][:, c])
    m["p_s"] = f(inp["p_sample"][:, sl].reshape(4, 64, 256))
    return m


def _weights(inp):
    f = lambda a: np.ascontiguousarray(a, dtype=np.float32)
    w = {}
    for k in ["ln_g", "ln_b", "ffn_w_in", "ffn_w_out", "ple_w_gate", "ple_w_proj", "a_w_in", "a_b_in", "a_ln_g", "a_ln_b",
              "a_b_s", "a_w_out", "b_mu", "b_w_rkv", "b_w0", "b_w1", "b_w2", "b_a0", "b_a1", "b_a2", "b_g1", "b_g2",
              "b_k_k", "b_k_a", "b_lnx_g", "b_lnx_b", "b_w_o", "c_w_in", "c_conv_w", "c_w_out"]:
        w[k] = f(inp[k])
    ws = np.asarray(inp["a_w_s"], dtype=np.float32)
    w["a_w_sT"] = f(np.transpose(ws, (0, 1, 3, 2)))
    blk = np.zeros((2, 8, 64, 64), np.float32)
    for b in range(16):
        blk[:, :, 4 * b:4 * b + 4, 4 * b:4 * b + 4] = np.transpose(ws[:, :, :4, :4], (0, 1, 3, 2))
    w["a_w_sS"] = blk
    w["a_b_sS"] = f(np.tile(np.asarray(inp["a_b_s"], dtype=np.float32)[:, :, :4], (1, 1, 16)))
    w["b_r_k"] = f(np.asarray(inp["b_r_k"]).reshape(1, 1024))
    return w


def kernel(**inp):
    nc, _ = build_program()
    w = _weights(inp)
    in_maps = []
    for c in range(8):
        m = _shard_inputs(inp, c)
        m.update(w)
        in_maps.append(m)
    res = run_bass_kernel_spmd(nc, in_maps, core_ids=list(range(8)))
    R = res.results
    cat = lambda k: [np.asarray(R[c][k]) for c in range(8)]
    y_p = np.stack(cat("y_p"), 0)
    y_s = np.concatenate([a.reshape(16, 4, 1024) for a in cat("y_s")], 0)
    a_v = np.concatenate([a.reshape(2, 16, 4, 2048) for a in cat("o_av")], 1)
    wkv_p = np.stack(cat("o_wkv_p"), 0)[None]
    shift_p = np.stack([a.reshape(1024) for a in cat("o_shift_p")], 0)[None]
    conv_p = np.stack(cat("o_conv_p"), 0)[None]
    wkv_s = np.concatenate(cat("o_wkv_s"), 0)[None]
    shift_s = np.concatenate(cat("o_shift_s"), 0)[None]
    conv_s = np.concatenate(cat("o_conv_s"), 0)[None]
    outs = (y_p, y_s, a_v, wkv_p, shift_p, conv_p, wkv_s, shift_s, conv_s)
    return tuple(np.ascontiguousarray(o, dtype=np.float32) for o in outs)
```

```python
import numpy as np
import concourse.bass as bass
import concourse.mybir as mybir
from concourse.bass_utils import run_bass_kernel_spmd
from contextlib import ExitStack

F32 = mybir.dt.float32
BF16 = mybir.dt.bfloat16
AF = mybir.ActivationFunctionType
ALU = mybir.AluOpType
AX = mybir.AxisListType

ENG_ATTR = {'pe': 'tensor', 'act': 'scalar', 'dve': 'vector', 'pool': 'gpsimd', 'sp': 'sync'}
SEM_LIMIT = 30000
NDMA_SEMS = 12


def _is_ap(v):
    return hasattr(v, 'tensor') and hasattr(v, 'ap') and hasattr(v, 'offset')


class Inst:
    __slots__ = ('eng', 'fn', 'is_dma', 'deps', 'sig', 'tok', 'waits', 'ord', 'dma_idx')

    def __init__(self, eng, fn, is_dma):
        self.eng = eng
        self.fn = fn
        self.is_dma = is_dma
        self.deps = {}
        self.sig = False
        self.tok = None
        self.waits = []
        self.ord = 0
        self.dma_idx = -1


class Prog:
    def __init__(self, nc):
        self.nc = nc
        self.insts = []
        self.reg = {}
        self.buckets = {}
        self.sb_off = 0
        self.out_dmas = []

    def sbuf(self, name, shape, dtype, offset):
        t = self.nc.alloc_sbuf_tensor_at(name, list(shape), dtype, offset=offset)
        es = 2 if dtype == BF16 else 4
        fs = int(np.prod(shape[1:]))
        self.reg[t.name] = ('sb', offset, fs, es)
        return t

    def psum(self, name, shape, dtype=F32):
        t = self.nc.alloc_psum_tensor(name, list(shape), dtype)
        fs = int(np.prod(shape[1:]))
        self.reg[t.name] = ('ps', 0, fs, 4)
        return t

    def _rect(self, ap):
        info = self.reg.get(ap.tensor.name)
        if info is None:
            return None
        space, base, fs, es = info
        off = int(ap.offset)
        dims = ap.ap
        p0 = off // fs
        lo = off % fs
        npart = dims[0][1] if dims[0][0] != 0 else 1
        ext = 0
        for st, cnt in dims[1:]:
            ext += (cnt - 1) * abs(st)
        hi = lo + ext + 1
        if space == 'ps':
            b0 = (lo * es) // 2048
            b1 = (hi * es - 1) // 2048
            return (space, 0, 128, b0 * 2048, (b1 + 1) * 2048)
        return (space, p0, p0 + npart, base + lo * es, base + hi * es)

    def _track(self, inst, rect, is_write):
        space, p0, p1, lo, hi = rect
        BS = 1024
        b0, b1 = lo // BS, (hi - 1) // BS
        key = (p0, p1, lo, hi, inst.eng if not inst.is_dma else ('dma', id(inst)), is_write)
        for b in range(b0, b1 + 1):
            d = self.buckets.setdefault((space, b), {})
            dead = []
            for k, rec in d.items():
                q0, q1, l2, h2, other, w2 = rec
                if q0 >= p1 or q1 <= p0 or l2 >= hi or h2 <= lo:
                    continue
                if other is inst:
                    continue
                if is_write or w2:
                    kind = 'raw' if (w2 and not is_write) else 'waw_war'
                    prev = inst.deps.get(id(other))
                    if prev is None or kind == 'raw':
                        inst.deps[id(other)] = (other, kind)
                if is_write and q0 >= p0 and q1 <= p1 and l2 >= lo and h2 <= hi:
                    dead.append(k)
            for k in dead:
                del d[k]
            d[key] = (p0, p1, lo, hi, inst, is_write)

    def add(self, eng, fn, reads, writes, is_dma=False):
        inst = Inst(eng, fn, is_dma)
        for ap in reads:
            r = self._rect(ap)
            if r is not None:
                self._track(inst, r, False)
        for ap in writes:
            r = self._rect(ap)
            if r is not None:
                self._track(inst, r, True)
        self.insts.append(inst)
        return inst

    def op(self, eng, meth, **kw):
        reads, writes = [], []
        for k, v in kw.items():
            if _is_ap(v):
                (writes if k in ('out', 'accum_out', 'ap') else reads).append(v)
        is_dma = (meth == 'dma_start')
        inst = self.add(eng, lambda e: getattr(e, meth)(**kw), reads, writes, is_dma)
        if eng == 'pe':
            src = kw.get('lhsT', kw.get('in_'))
            ro = self._rect(src)
            rw = self._rect(kw['out'])
            if ro is not None and rw is not None:
                rows = (ro[1], ro[2])
                if not hasattr(self, 'pe_rows'):
                    self.pe_rows = {}
                for bnk in range(rw[3] // 2048, (rw[4] - 1) // 2048 + 1):
                    prev = self.pe_rows.get(bnk)
                    if prev is not None and (prev[0][1] <= rows[0] or rows[1] <= prev[0][0]):
                        inst.deps[id(prev[1])] = (prev[1], 'rowgrp')
                    self.pe_rows[bnk] = (rows, inst)
        if is_dma and _is_ap(kw['out']) and self.reg.get(kw['out'].tensor.name) is None:
            self.out_dmas.append(inst)
        return inst

    def emit(self):
        nc = self.nc
        with ExitStack() as es:
            fence = Inst('sp', None, False)
            for d in self.out_dmas:
                fence.deps[id(d)] = (d, 'raw')
            self.insts.append(fence)
            for inst in self.insts:
                nd = {}
                for k, (d, kind) in inst.deps.items():
                    if d.eng == 'pe' and inst.eng == 'pe' and not d.is_dma and not inst.is_dma and kind != 'rowgrp':
                        continue
                    nd[k] = (d, kind)
                    d.sig = True
                inst.deps = nd
            eng_sem = {}
            eng_cnt = {}
            eng_ord = {}
            dma_sems = {}
            dma_cnt = {}
            dma_hist = {}
            nsem = [0]

            def newsem():
                nsem[0] += 1
                return es.enter_context(nc.semaphore("s%d" % nsem[0]))

            per_eng = {e: [] for e in ENG_ATTR}
            for inst in self.insts:
                per_eng[inst.eng].append(inst)
                if inst.is_dma:
                    q = inst.eng
                    if q not in dma_sems:
                        dma_sems[q] = [newsem() for _ in range(NDMA_SEMS)]
                        dma_cnt[q] = [0] * NDMA_SEMS
                        dma_hist[q] = []
                    i = len(dma_hist[q])
                    s = i % NDMA_SEMS
                    dma_cnt[q][s] += 16
                    inst.tok = (dma_sems[q][s], dma_cnt[q][s])
                    inst.sig = True
                    inst.dma_idx = i
                    dma_hist[q].append(inst)
                elif inst.sig:
                    e = inst.eng
                    if e not in eng_sem or eng_cnt[e] >= SEM_LIMIT:
                        eng_sem[e] = newsem()
                        eng_cnt[e] = 0
                    eng_cnt[e] += 1
                    eng_ord[e] = eng_ord.get(e, 0) + 1
                    inst.tok = (eng_sem[e], eng_cnt[e])
                    inst.ord = eng_ord[e]
            waited = {e: {} for e in ENG_ATTR}
            dwaited = {e: set() for e in ENG_ATTR}
            for inst in self.insts:
                e = inst.eng
                deps = [d for d, _ in inst.deps.values()]
                if inst.is_dma and inst.dma_idx >= NDMA_SEMS:
                    deps.append(dma_hist[e][inst.dma_idx - NDMA_SEMS])
                for d in deps:
                    if d.is_dma:
                        if id(d) in dwaited[e]:
                            continue
                        dwaited[e].add(id(d))
                        inst.waits.append(d.tok)
                    else:
                        if waited[e].get(d.eng, 0) >= d.ord:
                            continue
                        waited[e][d.eng] = d.ord
                        inst.waits.append(d.tok)
            self.nsem = nsem[0]

            def run(ename, eobj):
                for inst in per_eng[ename]:
                    for (s, v) in inst.waits:
                        eobj.wait_ge(s, v)
                    if inst.fn is None:
                        continue
                    r = inst.fn(eobj)
                    if inst.sig:
                        r.then_inc(inst.tok[0], 16 if inst.is_dma else 1)

            with nc.Block() as block:
                @block.sync
                def _(e):
                    run('sp', e)

                @block.scalar
                def _(e):
                    run('act', e)

                @block.vector
                def _(e):
                    run('dve', e)

                @block.gpsimd
                def _(e):
                    run('pool', e)

                @block.tensor
                def _(e):
                    run('pe', e)

D = 1024
TOK = 2112
ALPHA = 8.0 ** 0.25
LN_EPS = 1e-5
GN_EPS = 64e-5
T512 = [(0, 512), (512, 512), (1024, 512), (1536, 512), (2048, 64)]
T128 = [(i * 128, 128) for i in range(16)] + [(2048, 64)]
HALVES = [T512[0:2], T512[2:5]]
SB_BASE = 16512
SB_END = 229376
STG_ELEMS = 3072


def build_program(stub_mixers=False, n_layers=4, kinds=None, _reqs=None):
    if _reqs is None:
        _reqs = build_program(stub_mixers, n_layers, kinds, _reqs="dry")
    DRY = (_reqs == "dry")
    REQ = []
    nc = bass.Bass("TRN2", target_bir_lowering=False)
    P = Prog(nc)

    def din(name, shape):
        return nc.dram_tensor(name, list(shape), F32, kind="ExternalInput").ap()

    def dout(name, shape):
        return nc.dram_tensor(name, list(shape), F32, kind="ExternalOutput").ap()

    x_p = din("x_p", [2048, 1024]); x_s = din("x_s", [64, 1024])
    wkv_in = din("wkv_in", [16, 16, 64, 64]); shift_in = din("shift_in", [16, 1024])
    conv_in = din("conv_in", [16, 2, 1024])
    p_p = din("p_p", [4, 2048, 256]); p_s = din("p_s", [4, 64, 256])
    W = {}
    for nm, shp in [("ln_g", [4, 3, 1024]), ("ln_b", [4, 3, 1024]), ("ffn_w_in", [4, 2, 1024, 5632]),
                    ("ffn_w_out", [4, 2, 2816, 1024]), ("ple_w_gate", [4, 1024, 1024]), ("ple_w_proj", [4, 256, 1024]),
                    ("a_w_in", [2, 1024, 4096]), ("a_b_in", [2, 4096]), ("a_ln_g", [2, 2048]), ("a_ln_b", [2, 2048]),
                    ("a_w_sT", [2, 8, 128, 128]), ("a_w_sS", [2, 8, 64, 64]), ("a_b_s", [2, 8, 128]), ("a_b_sS", [2, 8, 64]),
                    ("a_w_out", [2, 2048, 1024]),
                    ("b_mu", [1, 6, 1024]), ("b_w_rkv", [1, 3, 1024, 1024]), ("b_w0", [1, 1024]), ("b_w1", [1, 1024, 64]),
                    ("b_w2", [1, 64, 1024]), ("b_a0", [1, 1024]), ("b_a1", [1, 1024, 64]), ("b_a2", [1, 64, 1024]),
                    ("b_g1", [1, 1024, 128]), ("b_g2", [1, 128, 1024]), ("b_k_k", [1, 1024]), ("b_k_a", [1, 1024]),
                    ("b_r_k", [1, 1024]), ("b_lnx_g", [1, 1024]), ("b_lnx_b", [1, 1024]), ("b_w_o", [1, 1024, 1024]),
                    ("c_w_in", [1, 1024, 3072]), ("c_conv_w", [1, 3, 1024]), ("c_w_out", [1, 1024, 1024])]:
        W[nm] = din(nm, shp)
    y_p = dout("y_p", [2048, 1024]); y_s = dout("y_s", [64, 1024])
    o_av = dout("o_av", [2, 64, 2048]); o_wkv_p = dout("o_wkv_p", [16, 64, 64]); o_shift_p = dout("o_shift_p", [1, 1024])
    o_conv_p = dout("o_conv_p", [2, 1024]); o_wkv_s = dout("o_wkv_s", [16, 16, 64, 64])
    o_shift_s = dout("o_shift_s", [16, 1024]); o_conv_s = dout("o_conv_s", [16, 2, 1024])

    cur = [SB_BASE]
    ncnt = [0]

    def alloc(shape, dt, at=None):
        ncnt[0] += 1
        nb = int(np.prod(shape[1:])) * (2 if dt == BF16 else 4)
        if at is None:
            at = cur[0]
            cur[0] = (at + nb + 63) // 64 * 64
            assert cur[0] <= SB_END, "sbuf overflow"
        else:
            assert at + nb <= SB_END, ("sbuf overflow", at, nb)
        return P.sbuf("t%d" % ncnt[0], shape, dt, at)

    ident = alloc([128, 128], F32)
    ones = alloc([128, 128], F32)
    LNG = alloc([128, 96], F32)
    LNB = alloc([128, 96], F32)
    VEC = alloc([128, 256], F32)
    EPSC = alloc([128, 2], F32)
    epsi = {float(LN_EPS): 0, float(4 * LN_EPS): 1}
    onesb = alloc([128, 128], BF16)
    identb = alloc([128, 128], BF16)
    XF = alloc([128, 8, TOK], F32)
    XB = alloc([128, 8, TOK], BF16)
    XB_OFF = cur[0] - 8 * TOK * 2
    STG = [alloc([128, STG_ELEMS], F32) for _ in range(2)]
    WBS = [alloc([128, STG_ELEMS], BF16) for _ in range(2)]
    STG_OFF = cur[0] - 2 * STG_ELEMS * 2 - 2 * STG_ELEMS * 4
    LNT_OFF = cur[0]
    LNT_SIZE = 16384
    cur[0] += LNT_SIZE
    PH_OFF = cur[0]
    PH_SIZE = SB_END - PH_OFF
    PS = P.psum("ps", [128, 8, 512])
    BSLOTS = WBS + [alloc([128, STG_ELEMS], BF16, at=STG_OFF + i * STG_ELEMS * 2) for i in range(4)]

    def sub(off0, shape, dt):
        return alloc(shape, dt, at=off0)
    SQ = [sub(LNT_OFF + i * 2048, [128, 512], F32) for i in range(2)]
    MEAN = sub(LNT_OFF + 4096, [128, 512], F32)
    VAR = sub(LNT_OFF + 6144, [128, 512], F32)
    RSTD = sub(LNT_OFF + 8192, [128, 512], F32)
    XC = [sub(LNT_OFF + 10240 + i * 2048, [128, 512], F32) for i in range(2)]
    TMPA = [sub(LNT_OFF + 10240 + i * 2048, [128, 512], F32) for i in range(2)]
    TMPB = [sub(LNT_OFF + i * 2048, [128, 512], F32) for i in range(2)]

    bankc = [0]
    tmc = [0]

    reserved = set()

    def bank():
        while True:
            b = bankc[0] % 8
            bankc[0] += 1
            if b not in reserved:
                return b

    def bcast_rows(ap2d, nparts):
        return bass.AP(ap2d.tensor, ap2d.offset, [[0, nparts], [1, int(ap2d.shape[-1])]])

    def op(eng, meth, **kw):
        if DRY:
            return None
        return P.op(eng, meth, **kw)

    op('pool', 'memset', ap=ident[:], constant=1.0)
    op('pool', 'affine_select', out=ident[:], in_=ident[:], pattern=[[-1, 128]], compare_op=ALU.is_equal,
       fill=0.0, base=0, channel_multiplier=1)
    op('dve', 'memset', ap=ones[:], constant=1.0)
    op('dve', 'memset', ap=EPSC[:, 1:2], constant=float(4 * LN_EPS))
    op('dve', 'memset', ap=EPSC[:, 0:1], constant=float(LN_EPS))
    op('dve', 'memset', ap=EPSC[:, 1:2], constant=float(4 * LN_EPS))

    IOS = [sub(PH_OFF + i * 4096, [128, 1024], F32) for i in range(2)]
    iosc = [0]

    def load_cols(dst_ap, src2d, R):
        s = IOS[iosc[0] % 2]; iosc[0] += 1
        op('sp', 'dma_start', out=s[0:R, 0:128], in_=src2d)
        b = bank()
        op('pe', 'transpose', out=PS[:, b, 0:R], in_=s[0:R, 0:128], identity=ident[0:R, 0:R])
        op('dve', 'tensor_copy', out=dst_ap, in_=PS[:, b, 0:R])

    load_cols(LNG[:], W["ln_g"].rearrange("l i (c p) -> (l i c) p", p=128), 96)
    load_cols(LNB[:], W["ln_b"].rearrange("l i (c p) -> (l i c) p", p=128), 96)
    VC = {}
    vcur = [0]

    def vec_cols(key, src2d, R):
        load_cols(VEC[:, vcur[0]:vcur[0] + R], src2d, R)
        VC[key] = vcur[0]
        vcur[0] += R

    wsc = [0]
    cast_engs = ['dve', 'act']

    WCACHE = {}
    bsc = [0]
    issued = {}
    nreq = [0]
    nissued = [0]

    live = {}
    SLOTN = ['B2', 'B3', 'B4', 'B5', 'W0', 'W1']

    def _slot_ap(nm):
        return BSLOTS[{'W0': 0, 'W1': 1, 'B2': 2, 'B3': 3, 'B4': 4, 'B5': 5}[nm]]

    def _try_issue(r):
        parts, K, key = _reqs[r]
        kc = max(1, K // 128)
        kp = min(K, 128)
        ntot = sum(int(a.shape[1]) for a in parts)
        assert kc * ntot <= STG_ELEMS
        if key is not None and key in WCACHE:
            free = [s_ for s_ in SLOTN if s_ not in live]
            if not free:
                return None
            live[free[0]] = r
            dt_, wr = WCACHE[key]
            bs = _slot_ap(free[0])
            inst = op('sp', 'dma_start', out=bs[0:kp, 0:kc * ntot], in_=dt_)
            inst.deps[id(wr)] = (wr, 'raw')
            return bs[0:kp, 0:kc * ntot].rearrange("p (c n) -> p c n", c=kc)
        wfree = [s_ for s_ in ('W0', 'W1') if s_ not in live]
        tfree = [t for t in (0, 1) if ('B%d' % (2 * t + 2)) not in live and ('B%d' % (2 * t + 3)) not in live]
        if not wfree or not tfree:
            return None
        t = tfree[wsc[0] % len(tfree)]
        slot = 0 if wfree[0] == 'W0' else 1
        if len(wfree) == 2:
            slot = wsc[0] % 2
        live['W%d' % slot] = r
        ce = cast_engs[wsc[0] % len(cast_engs)]
        wsc[0] += 1
        sv = STG[t][0:kp, 0:kc * ntot].rearrange("p (c n) -> p c n", c=kc)
        wv = WBS[slot][0:kp, 0:kc * ntot].rearrange("p (c n) -> p c n", c=kc)
        col = 0
        for a in parts:
            n = int(a.shape[1])
            src = a.rearrange("(c p) n -> p c n", p=kp)
            op('sp', 'dma_start', out=sv[:, :, col:col + n], in_=src)
            col += n
        if ce == 'act':
            op('act', 'activation', out=wv, in_=sv, func=AF.Copy)
        else:
            op(ce, 'tensor_copy', out=wv, in_=sv)
        if key is not None:
            dt_ = nc.dram_tensor("wc%d" % len(WCACHE), [kp, kc * ntot], BF16, kind="Internal").ap()
            wr = op('pool', 'dma_start', out=dt_, in_=WBS[slot][0:kp, 0:kc * ntot])
            WCACHE[key] = (dt_, wr)
        return wv

    def wload(parts, K, key=None):
        i = nreq[0]; nreq[0] += 1
        kc = max(1, K // 128); kp = min(K, 128)
        ntot = sum(int(a.shape[1]) for a in parts)
        if DRY:
            REQ.append((parts, K, key))
            return WBS[0][0:kp, 0:kc * ntot].rearrange("p (c n) -> p c n", c=kc)
        for s_ in [s_ for s_, r_ in live.items() if r_ < i]:
            del live[s_]
        while nissued[0] < len(_reqs) and nissued[0] <= i + 4:
            r = nissued[0]
            k2 = _reqs[r][2]
            cached = (k2 is not None and k2 in WCACHE)
            if not cached and r > i + 1:
                break
            ap_ = _try_issue(r)
            if ap_ is None:
                assert r > i, "no weight slot for a mandatory load"
                break
            issued[r] = ap_
            nissued[0] += 1
        return issued.pop(i)

    def pipelined(blocks, K, body, keys=None):
        for i in range(len(blocks)):
            body(i, wload(blocks[i], K, None if keys is None else keys[i]))

    def load_x():
        for ti, (t0, n) in enumerate(T128):
            s = IOS[iosc[0] % 2]; iosc[0] += 1
            src = x_p[t0:t0 + n, :] if t0 < 2048 else x_s
            op('sp', 'dma_start', out=s[0:n, :], in_=src)
            for hh in range(2):
                b = bank()
                for q in range(4):
                    c = hh * 4 + q
                    op('pe', 'transpose', out=PS[:, b, q * n:(q + 1) * n], in_=s[0:n, c * 128:(c + 1) * 128],
                       identity=ident[0:n, 0:n])
                pv = PS[:, b, 0:4 * n].rearrange("p (a t) -> p a t", a=4)
                op('dve', 'tensor_copy', out=XF[:, hh * 4:(hh + 1) * 4, t0:t0 + n], in_=pv)
                op('act', 'activation', out=XB[:, hh * 4:(hh + 1) * 4, t0:t0 + n], in_=XF[:, hh * 4:(hh + 1) * 4, t0:t0 + n], func=AF.Copy)

    def store_y():
        for ti, (t0, n) in enumerate(T128):
            s = IOS[iosc[0] % 2]; iosc[0] += 1
            for hh in range(2):
                b = bank()
                for q in range(4):
                    c = hh * 4 + q
                    op('pe', 'transpose', out=PS[0:n, b, q * 128:(q + 1) * 128], in_=XF[:, c, t0:t0 + n], identity=ident[:])
                if hh == 0:
                    op('dve', 'tensor_copy', out=s[0:n, 0:512], in_=PS[0:n, b, :])
                else:
                    op('act', 'activation', out=s[0:n, 512:1024], in_=PS[0:n, b, :], func=AF.Copy)
            dst = y_p[t0:t0 + n, :] if t0 < 2048 else y_s
            op('sp', 'dma_start', out=dst, in_=s[0:n, :])

    def layer_norm(gi, eps, tiles=None):
        for (t0, n) in (T512 if tiles is None else tiles):
            b1 = bank(); b2 = bank()
            for c in range(8):
                op('pe', 'matmul', out=PS[:, b1, 0:n], lhsT=ones[:], rhs=XF[:, c, t0:t0 + n], start=(c == 0), stop=(c == 7))
            for c in range(8):
                sq = SQ[c % 2]
                op('pool', 'tensor_tensor', out=sq[:, 0:n], in0=XF[:, c, t0:t0 + n], in1=XF[:, c, t0:t0 + n], op=ALU.mult)
                op('pe', 'matmul', out=PS[:, b2, 0:n], lhsT=ones[:], rhs=sq[:, 0:n], start=(c == 0), stop=(c == 7))
            op('dve', 'tensor_scalar', out=MEAN[:, 0:n], in0=PS[:, b1, 0:n], scalar1=1.0 / D, scalar2=None, op0=ALU.mult)
            op('dve', 'tensor_tensor', out=RSTD[:, 0:n], in0=MEAN[:, 0:n], in1=MEAN[:, 0:n], op=ALU.mult)
            op('dve', 'scalar_tensor_tensor', out=VAR[:, 0:n], in0=PS[:, b2, 0:n], scalar=1.0 / D, in1=RSTD[:, 0:n],
               op0=ALU.mult, op1=ALU.subtract)
            op('act', 'activation', out=VAR[:, 0:n], in_=VAR[:, 0:n], func=AF.Ln, bias=EPSC[:, epsi[float(eps)]:epsi[float(eps)] + 1], scale=1.0)
            op('act', 'activation', out=RSTD[:, 0:n], in_=VAR[:, 0:n], func=AF.Exp, scale=-0.5)
            for c in range(8):
                xc = XC[c % 2]
                op('pool' if c % 2 == 1 else 'dve', 'tensor_tensor', out=xc[:, 0:n], in0=XF[:, c, t0:t0 + n], in1=MEAN[:, 0:n], op=ALU.subtract)
                op('dve', 'tensor_tensor', out=xc[:, 0:n], in0=xc[:, 0:n], in1=RSTD[:, 0:n], op=ALU.mult)
                g = LNG[:, gi * 8 + c:gi * 8 + c + 1]
                bb = LNB[:, gi * 8 + c:gi * 8 + c + 1]
                op('act', 'activation', out=XF[:, c, t0:t0 + n], in_=xc[:, 0:n], func=AF.Identity, scale=g, bias=bb)
                if c % 4 != 3:
                    op('act', 'activation', out=XB[:, c, t0:t0 + n], in_=xc[:, 0:n], func=AF.Identity, scale=g, bias=bb)
                else:
                    op('dve', 'tensor_scalar', out=XB[:, c, t0:t0 + n], in0=xc[:, 0:n], scalar1=g, scalar2=bb, op0=ALU.mult, op1=ALU.add)

    H = sub(PH_OFF, [128, 22, 1088], BF16)

    def ffn(l, i, pre_ln=None, side=None):
        w_in = W["ffn_w_in"][l, i]
        w_out = W["ffn_w_out"][l, i]
        if pre_ln is not None:
            layer_norm(pre_ln[0], pre_ln[1], tiles=HALVES[0])
        for hi_, half in enumerate(HALVES):
            h0 = half[0][0]

            def body_in(j, cw):
                for (t0, n) in half:
                    bg = bank(); bu = bank()
                    for kc in range(8):
                        op('pe', 'matmul', out=PS[:, bg, 0:n], lhsT=cw[:, kc, 0:128], rhs=XB[:, kc, t0:t0 + n],
                           start=(kc == 0), stop=(kc == 7))
                    for kc in range(8):
                        op('pe', 'matmul', out=PS[:, bu, 0:n], lhsT=cw[:, kc, 128:256], rhs=XB[:, kc, t0:t0 + n],
                           start=(kc == 0), stop=(kc == 7))
                    tm = TMPA[tmc[0] % 2]; tmc[0] += 1
                    op('act', 'activation', out=tm[:, 0:n], in_=PS[:, bg, 0:n], func=AF.Silu)
                    op('dve', 'tensor_tensor', out=H[:, j, t0 - h0:t0 - h0 + n], in0=tm[:, 0:n], in1=PS[:, bu, 0:n], op=ALU.mult)
                if side is not None:
                    next(side, None)

            pipelined([[w_in[:, j * 128:(j + 1) * 128], w_in[:, 2816 + j * 128:2816 + (j + 1) * 128]] for j in range(22)],
                      1024, body_in)
            if pre_ln is not None and hi_ == 0:
                layer_norm(pre_ln[0], pre_ln[1], tiles=HALVES[1])

            def body_out(m, cw):
                for (t0, n) in half:
                    b = bank()
                    for fc in range(22):
                        op('pe', 'matmul', out=PS[:, b, 0:n], lhsT=cw[:, fc, :], rhs=H[:, fc, t0 - h0:t0 - h0 + n],
                           start=(fc == 0), stop=(fc == 21))
                    op('dve', 'scalar_tensor_tensor', out=XF[:, m, t0:t0 + n], in0=XF[:, m, t0:t0 + n], scalar=2.0 * ALPHA,
                       in1=PS[:, b, 0:n], op0=ALU.mult, op1=ALU.add)

            pipelined([[w_out[:, m * 128:(m + 1) * 128]] for m in range(8)], 2816, body_out)

    PT = sub(PH_OFF + 8192, [128, 2, TOK], BF16)

    def ple(l):
        for gi in range(4):
            s = IOS[iosc[0] % 2]; iosc[0] += 1
            t0 = gi * 512
            op('sp', 'dma_start', out=s[:, :].rearrange("p (n d) -> p n d", n=4),
               in_=p_p[l, t0:t0 + 512, :].rearrange("(n p) d -> p n d", p=128))
            for k2 in range(2):
                b = bank()
                for q in range(4):
                    op('pe', 'transpose', out=PS[:, b, q * 128:(q + 1) * 128],
                       in_=s[:, q * 256 + k2 * 128:q * 256 + (k2 + 1) * 128], identity=ident[:])
                if k2 == 0:
                    op('dve', 'tensor_copy', out=PT[:, k2, t0:t0 + 512], in_=PS[:, b, :])
                else:
                    op('act', 'activation', out=PT[:, k2, t0:t0 + 512], in_=PS[:, b, :], func=AF.Copy)
        s = IOS[iosc[0] % 2]; iosc[0] += 1
        op('sp', 'dma_start', out=s[0:64, 0:256], in_=p_s[l])
        b = bank()
        for k2 in range(2):
            op('pe', 'transpose', out=PS[:, b, k2 * 64:(k2 + 1) * 64], in_=s[0:64, k2 * 128:(k2 + 1) * 128], identity=ident[0:64, 0:64])
        op('dve', 'tensor_copy', out=PT[:, :, 2048:2112], in_=PS[:, b, 0:128].rearrange("p (a t) -> p a t", a=2))
        wp = wload([W["ple_w_proj"][l]], 256)
        WP = sub(PH_OFF + 8192 + 2 * TOK * 2 + 64, [128, 2, 1024], BF16)
        op('dve', 'tensor_copy', out=WP[:], in_=wp)
        wg = W["ple_w_gate"][l]

        def body(mb, cw):
            for mm_ in range(2):
                m = mb * 2 + mm_
                for (t0, n) in T512:
                    bg = bank(); bp = bank()
                    for kc in range(8):
                        op('pe', 'matmul', out=PS[:, bg, 0:n], lhsT=cw[:, kc, mm_ * 128:(mm_ + 1) * 128], rhs=XB[:, kc, t0:t0 + n],
                           start=(kc == 0), stop=(kc == 7))
                    for k2 in range(2):
                        op('pe', 'matmul', out=PS[:, bp, 0:n], lhsT=WP[:, k2, m * 128:(m + 1) * 128], rhs=PT[:, k2, t0:t0 + n],
                           start=(k2 == 0), stop=(k2 == 1))
                    tm = TMPA[tmc[0] % 2]; tmc[0] += 1
                    op('act', 'activation', out=tm[:, 0:n], in_=PS[:, bg, 0:n], func=AF.Sigmoid)
                    op('dve', 'tensor_tensor', out=tm[:, 0:n], in0=tm[:, 0:n], in1=PS[:, bp, 0:n], op=ALU.mult)
                    op('dve', 'tensor_tensor', out=XF[:, m, t0:t0 + n], in0=XF[:, m, t0:t0 + n], in1=tm[:, 0:n], op=ALU.add)

        pipelined([[wg[:, mb * 256:(mb + 1) * 256]] for mb in range(4)], 1024, body)
        for (t0, n) in T512:
            op('act', 'activation', out=XB[:, 0:4, t0:t0 + n], in_=XF[:, 0:4, t0:t0 + n], func=AF.Copy)
            op('dve', 'tensor_copy', out=XB[:, 4:8, t0:t0 + n], in_=XF[:, 4:8, t0:t0 + n])

    def mixer_stub():
        for (t0, n) in T512:
            op('dve', 'tensor_scalar', out=XF[:, :, t0:t0 + n], in0=XF[:, :, t0:t0 + n], scalar1=ALPHA, scalar2=None, op0=ALU.mult)

    op('dve', 'memset', ap=onesb[:], constant=1.0)
    op('act', 'activation', out=identb[:], in_=ident[:], func=AF.Copy)

    def setup_vecs():
        vec_cols("a_b_u0", W["a_b_in"][0:1, 0:2048].rearrange("o (c p) -> (o c) p", p=128), 16)
        vec_cols("a_b_u1", W["a_b_in"][1:2, 0:2048].rearrange("o (c p) -> (o c) p", p=128), 16)
        vec_cols("a_g0", W["a_ln_g"][0:1, :].rearrange("o (c p) -> (o c) p", p=128), 16)
        vec_cols("a_g1", W["a_ln_g"][1:2, :].rearrange("o (c p) -> (o c) p", p=128), 16)
        vec_cols("a_b0", W["a_ln_b"][0:1, :].rearrange("o (c p) -> (o c) p", p=128), 16)
        vec_cols("a_b1", W["a_ln_b"][1:2, :].rearrange("o (c p) -> (o c) p", p=128), 16)
        vec_cols("c_cw", W["c_conv_w"][0].rearrange("i (c p) -> (i c) p", p=128), 24)
        vec_cols("b_mu", W["b_mu"][0].rearrange("i (c p) -> (i c) p", p=128), 48)
        for nm in ["b_w0", "b_a0", "b_k_k", "b_k_a", "b_r_k"]:
            vec_cols(nm, W[nm].rearrange("o (c p) -> (o c) p", p=128), 8)

    setup_vecs()

    def vcol(key, i):
        return VEC[:, VC[key] + i:VC[key] + i + 1]

    def mixer_c(j):
        w_in = W["c_w_in"][j]
        w_out = W["c_w_out"][j]
        Z = sub(PH_OFF, [128, 8, 514], F32)
        ZS = sub(PH_OFF + 16448, [128, 8, 16, 6], F32)
        YC2 = [sub(PH_OFF + 16448 + 3072 + i * 8192, [128, 8, 512], BF16) for i in range(2)]
        CT = sub(PH_OFF + 16448 + 3072 + 16384, [128, 8, 32], F32)
        IOC = sub(PH_OFF + 16448 + 3072 + 16384 + 1024, [128, 1024], F32)
        op('dve', 'memset', ap=Z[:, :, 0:2], constant=0.0)
        op('sp', 'dma_start', out=IOC[0:32, :], in_=conv_in.rearrange("b i d -> (b i) d"))
        for hh in range(2):
            b = bank()
            for q in range(4):
                c = hh * 4 + q
                op('pe', 'transpose', out=PS[:, b, q * 32:(q + 1) * 32], in_=IOC[0:32, c * 128:(c + 1) * 128], identity=ident[0:32, 0:32])
            for q in range(4):
                c = hh * 4 + q
                op('dve', 'tensor_copy', out=ZS[:, c, :, 0:2], in_=PS[:, b, q * 32:(q + 1) * 32].rearrange("p (b i) -> p b i", i=2))

        def in_gen(gi):
            t0, n = T512[gi]
            samp = (t0 >= 2048)
            YC = YC2[gi % 2]

            def zview(m, lo):
                if samp:
                    return ZS[:, m, :, lo:lo + 4]
                return Z[:, m, lo:lo + n]

            def v3(ap):
                return ap.rearrange("p (b t) -> p b t", t=4) if samp else ap
            for m in range(8):
                cw = wload([w_in[:, 1024 + m * 128:1024 + (m + 1) * 128], w_in[:, 2048 + m * 128:2048 + (m + 1) * 128],
                            w_in[:, m * 128:(m + 1) * 128]], 1024, ("c", "in", m))
                b1 = bank(); b2 = bank(); b3 = bank()
                for bi, bb in enumerate((b1, b2, b3)):
                    for kc in range(8):
                        op('pe', 'matmul', out=PS[:, bb, 0:n], lhsT=cw[:, kc, bi * 128:(bi + 1) * 128], rhs=XB[:, kc, t0:t0 + n],
                           start=(kc == 0), stop=(kc == 7))
                tm = TMPA[tmc[0] % 2]; tmc[0] += 1
                cv = TMPB[tmc[0] % 2]
                op('act', 'activation', out=tm[:, 0:n], in_=PS[:, b1, 0:n], func=AF.Copy)
                op('dve', 'tensor_tensor', out=zview(m, 2), in0=v3(tm[:, 0:n]), in1=v3(PS[:, b2, 0:n]), op=ALU.mult)
                op('dve', 'tensor_scalar', out=v3(cv[:, 0:n]), in0=zview(m, 0), scalar1=vcol("c_cw", 0 * 8 + m), scalar2=None, op0=ALU.mult)
                op('dve', 'scalar_tensor_tensor', out=v3(cv[:, 0:n]), in0=zview(m, 1), scalar=vcol("c_cw", 1 * 8 + m), in1=v3(cv[:, 0:n]),
                   op0=ALU.mult, op1=ALU.add)
                op('dve', 'scalar_tensor_tensor', out=v3(cv[:, 0:n]), in0=zview(m, 2), scalar=vcol("c_cw", 2 * 8 + m), in1=v3(cv[:, 0:n]),
                   op0=ALU.mult, op1=ALU.add)
                op('dve', 'tensor_tensor', out=YC[:, m, 0:n], in0=cv[:, 0:n], in1=PS[:, b3, 0:n], op=ALU.mult)
                yield
            if not samp:
                if gi == 3:
                    for c in range(8):
                        op('sp', 'dma_start', out=o_conv_p[:, c * 128:(c + 1) * 128].rearrange("i p -> p i"), in_=Z[:, c, 512:514],
                           allow_slow_non_contiguous=True)
                else:
                    op('dve', 'tensor_copy', out=Z[:, :, 0:2], in_=Z[:, :, 512:514])
            else:
                for c in range(8):
                    op('dve', 'tensor_copy', out=CT[:, c, :].rearrange("p (b i) -> p b i", i=2), in_=ZS[:, c, :, 4:6])
                for hh in range(2):
                    b = bank()
                    for q in range(4):
                        c = hh * 4 + q
                        op('pe', 'transpose', out=PS[0:32, b, q * 128:(q + 1) * 128], in_=CT[:, c, :], identity=ident[:])
                    op('dve', 'tensor_copy', out=IOC[0:32, hh * 512:(hh + 1) * 512], in_=PS[0:32, b, :])
                op('sp', 'dma_start', out=o_conv_s.rearrange("b i d -> (b i) d"), in_=IOC[0:32, :])
            yield

        def out_gen(gi):
            t0, n = T512[gi]
            YC = YC2[gi % 2]
            for mb in range(4):
                cw = wload([w_out[:, mb * 256:(mb + 1) * 256]], 1024, ("c", "out", mb))
                for mm_ in range(2):
                    m = mb * 2 + mm_
                    b = bank()
                    for kc in range(8):
                        op('pe', 'matmul', out=PS[:, b, 0:n], lhsT=cw[:, kc, mm_ * 128:(mm_ + 1) * 128], rhs=YC[:, kc, 0:n],
                           start=(kc == 0), stop=(kc == 7))
                    op('dve', 'scalar_tensor_tensor', out=XF[:, m, t0:t0 + n], in0=XF[:, m, t0:t0 + n], scalar=ALPHA,
                       in1=PS[:, b, 0:n], op0=ALU.mult, op1=ALU.add)
                yield

        def run_all(gen):
            for _ in gen:
                pass

        def interleave_c(ga, gb):
            gens = [g_ for g_ in (ga, gb) if g_ is not None]
            while gens:
                for g_ in list(gens):
                    try:
                        next(g_)
                    except StopIteration:
                        gens.remove(g_)

        run_all(in_gen(0))
        for gi in range(len(T512)):
            interleave_c(in_gen(gi + 1) if gi + 1 < len(T512) else None, out_gen(gi))

    GA = [(i * 256, 256) for i in range(8)] + [(2048, 64)]

    def mixer_a(j):
        w_in = W["a_w_in"][j]
        w_out = W["a_w_out"][j]
        o = [PH_OFF]

        def pa(shape, dt):
            nb = int(np.prod(shape[1:])) * (2 if dt == BF16 else 4)
            t = sub(o[0], shape, dt)
            o[0] = (o[0] + nb + 63) // 64 * 64
            return t
        VF = pa([128, 2, 2048], F32)
        VNS = [pa([128, 2, 2048], BF16) for _ in range(2)]
        YA = pa([128, 16, 256], BF16)
        WST = pa([128, 8, 128], BF16)
        WSS = pa([128, 8, 64], BF16)
        RBC = sub(LNT_OFF + 4096, [128, 8, 128], F32)
        RSBC = sub(LNT_OFF + 8192, [128, 8, 64], F32)
        BSBC = pa([128, 8, 128], F32)
        BSSBC = sub(LNT_OFF + 14336, [128, 8, 64], F32)
        BVT = [pa([128, 256], F32) for _ in range(4)]
        BBT = [pa([128, 128], F32) for _ in range(2)]
        STAT = pa([128, 2, 24], F32)
        MV = pa([128, 2, 2], F32)
        RSTDA = pa([128, 2, 1], F32)
        assert o[0] <= SB_END
        GBROW = sub(PH_OFF + 8192, [33, 2048], F32)
        sv = VF[:, 0, 0:1024].rearrange("p (h t) -> p h t", h=8)
        op('sp', 'dma_start', out=sv, in_=W["a_w_sT"][j].rearrange("h s t -> s h t"))
        op('pool', 'affine_select', out=sv, in_=sv, pattern=[[0, 8], [1, 128]], compare_op=ALU.is_ge, fill=0.0,
           base=0, channel_multiplier=-1)
        op('dve', 'tensor_copy', out=WST[:], in_=sv)
        sv2 = VF[0:64, 1, 0:512].rearrange("p (h t) -> p h t", h=8)
        op('sp', 'dma_start', out=sv2, in_=W["a_w_sS"][j].rearrange("h s t -> s h t"))
        op('pool', 'affine_select', out=sv2, in_=sv2, pattern=[[0, 8], [1, 64]], compare_op=ALU.is_ge, fill=0.0,
           base=0, channel_multiplier=-1)
        op('dve', 'tensor_copy', out=WSS[0:64], in_=sv2)
        for hh in range(2):
            b = bank()
            for q in range(4):
                h = hh * 4 + q
                op('pe', 'matmul', out=PS[:, b, q * 128:(q + 1) * 128], lhsT=onesb[:], rhs=WST[:, h, :], start=True, stop=True)
            op('dve', 'tensor_copy', out=RBC[:, hh * 4:(hh + 1) * 4, :], in_=PS[:, b, :].rearrange("p (h t) -> p h t", h=4))
        b = bank()
        for h in range(8):
            op('pe', 'matmul', out=PS[:, b, h * 64:(h + 1) * 64], lhsT=onesb[0:64, :], rhs=WSS[0:64, h, :], start=True, stop=True)
        op('dve', 'tensor_copy', out=RSBC[:], in_=PS[:, b, :].rearrange("p (h t) -> p h t", h=8))
        op('sp', 'dma_start', out=BSBC[:].rearrange("p h t -> p (h t)"),
           in_=bcast_rows(W["a_b_s"][j:j + 1].rearrange("o h t -> o (h t)"), 128))
        op('sp', 'dma_start', out=BSSBC[:].rearrange("p h t -> p (h t)"),
           in_=bcast_rows(W["a_b_sS"][j:j + 1].rearrange("o h t -> o (h t)"), 128))

        def p1(gidx):
            g0, gn = GA[gidx]
            VN = VNS[gidx % 2]
            samp = (g0 >= 2048)
            tiles = [(g0, 64)] if samp else [(g0, 128), (g0 + 128, 128)]
            for q in range(8):
                cw = wload([w_in[:, 2048 + q * 256:2048 + (q + 1) * 256]], 1024, ("a", j, "v", q))
                bv = BVT[q % 4]
                op('pool', 'dma_start', out=bv[:], in_=bcast_rows(W["a_b_in"][j:j + 1, 2048 + q * 256:2048 + (q + 1) * 256], 128))
                for i, (t0, nt) in enumerate(tiles):
                    b = bank()
                    for kc in range(8):
                        op('pe', 'matmul', out=PS[0:nt, b, 0:256], lhsT=XB[:, kc, t0:t0 + nt], rhs=cw[:, kc, :],
                           start=(kc == 0), stop=(kc == 7))
                    op('dve', 'tensor_tensor', out=VF[0:nt, i, q * 256:(q + 1) * 256], in0=PS[0:nt, b, 0:256], in1=bv[0:nt, :], op=ALU.add)
                yield
            for i, (t0, nt) in enumerate(tiles):
                op('act', 'activation', out=VF[0:nt, i, :], in_=VF[0:nt, i, :], func=AF.Gelu)
                for q in range(4):
                    op('dve', 'bn_stats', out=STAT[0:nt, i, q * 6:(q + 1) * 6], in_=VF[0:nt, i, q * 512:(q + 1) * 512])
                op('dve', 'bn_aggr', out=MV[0:nt, i, :], in_=STAT[0:nt, i, :])
                op('act', 'activation', out=RSTDA[0:nt, i, :], in_=MV[0:nt, i, 1:2], func=AF.Sqrt, bias=LN_EPS, scale=1.0)
                op('dve', 'reciprocal', out=RSTDA[0:nt, i, :], in_=RSTDA[0:nt, i, :])
                yield
                op('dve', 'tensor_scalar', out=VF[0:nt, i, :], in0=VF[0:nt, i, :], scalar1=MV[0:nt, i, 0:1], scalar2=RSTDA[0:nt, i, :],
                   op0=ALU.subtract, op1=ALU.mult)
                op('act', 'activation', out=VN[0:nt, i, :], in_=VF[0:nt, i, :], func=AF.Copy)
                yield
            if samp:
                op('sp', 'dma_start', out=GBROW[0:1, :], in_=W["a_ln_g"][j:j + 1, :])
                op('sp', 'dma_start', out=GBROW[32:33, :], in_=W["a_ln_b"][j:j + 1, :])
                for q in range(4):
                    bg = bank(); bb = bank()
                    op('pe', 'matmul', out=PS[0:64, bg, :], lhsT=ones[0:1, 0:64], rhs=GBROW[0:1, q * 512:(q + 1) * 512], start=True, stop=True)
                    op('pe', 'matmul', out=PS[0:64, bb, :], lhsT=ones[32:33, 0:64], rhs=GBROW[32:33, q * 512:(q + 1) * 512], start=True, stop=True)
                    op('dve', 'tensor_tensor', out=VF[0:64, 0, q * 512:(q + 1) * 512], in0=VF[0:64, 0, q * 512:(q + 1) * 512], in1=PS[0:64, bg, :], op=ALU.mult)
                    op('dve', 'tensor_tensor', out=VF[0:64, 0, q * 512:(q + 1) * 512], in0=VF[0:64, 0, q * 512:(q + 1) * 512], in1=PS[0:64, bb, :], op=ALU.add)
                op('sp', 'dma_start', out=o_av[j], in_=VF[0:64, 0, :])
                yield

        def p23(gidx):
            g0, gn = GA[gidx]
            VN = VNS[gidx % 2]
            samp = (g0 >= 2048)
            tiles = [(g0, 64)] if samp else [(g0, 128), (g0 + 128, 128)]
            for bi in range(8):
                cw = wload([w_in[:, bi * 256:(bi + 1) * 256]], 1024, ("a", j, "u", bi))
                for f2 in range(2):
                    fc = bi * 2 + f2
                    h = fc // 2
                    bu = bank(); bm = bank()
                    for kc in range(8):
                        op('pe', 'matmul', out=PS[:, bu, 0:gn], lhsT=cw[:, kc, f2 * 128:(f2 + 1) * 128], rhs=XB[:, kc, g0:g0 + gn],
                           start=(kc == 0), stop=(kc == 7))
                    for i, (t0, nt) in enumerate(tiles):
                        wsp = WSS[0:64, h, :] if samp else WST[:, h, :]
                        op('pe', 'matmul', out=PS[:, bm, i * 128:i * 128 + nt], lhsT=VN[0:nt, i, fc * 128:(fc + 1) * 128], rhs=wsp,
                           start=True, stop=True)
                    tm = TMPA[tmc[0] % 2]
                    t2 = TMPB[tmc[0] % 2]
                    bbt = BBT[tmc[0] % 2]; tmc[0] += 1
                    op('act', 'activation', out=tm[:, 0:gn], in_=PS[:, bu, 0:gn], func=AF.Gelu, bias=vcol("a_b_u%d" % j, fc), scale=1.0)
                    nt = tiles[0][1]
                    rb = RSBC[:, h, :] if samp else RBC[:, h, :]
                    bs = BSSBC[:, h, :] if samp else BSBC[:, h, :]
                    op('dve', 'scalar_tensor_tensor', out=bbt[:, 0:nt], in0=rb, scalar=vcol("a_b%d" % j, fc), in1=bs, op0=ALU.mult, op1=ALU.add)
                    for i in range(len(tiles)):
                        op('dve', 'scalar_tensor_tensor', out=t2[:, i * 128:i * 128 + nt], in0=PS[:, bm, i * 128:i * 128 + nt],
                           scalar=vcol("a_g%d" % j, fc), in1=bbt[:, 0:nt], op0=ALU.mult, op1=ALU.add)
                    op('dve', 'tensor_tensor', out=YA[:, fc, 0:gn], in0=t2[:, 0:gn], in1=tm[:, 0:gn], op=ALU.mult)
                yield
            for m in range(8):
                cw = wload([w_out[:, m * 128:(m + 1) * 128]], 2048, ("a", j, "o", m))
                b = bank()
                for fc in range(16):
                    op('pe', 'matmul', out=PS[:, b, 0:gn], lhsT=cw[:, fc, :], rhs=YA[:, fc, 0:gn], start=(fc == 0), stop=(fc == 15))
                op('dve', 'scalar_tensor_tensor', out=XF[:, m, g0:g0 + gn], in0=XF[:, m, g0:g0 + gn], scalar=ALPHA,
                   in1=PS[:, b, 0:gn], op0=ALU.mult, op1=ALU.add)
                yield

        def interleave(ga, gb):
            gens = [g_ for g_ in (ga, gb) if g_ is not None]
            while gens:
                for g_ in list(gens):
                    try:
                        next(g_)
                    except StopIteration:
                        gens.remove(g_)

        interleave(p1(0), None)
        for gidx in range(len(GA)):
            interleave(p1(gidx + 1) if gidx + 1 < len(GA) else None, p23(gidx))

    def mixer_b(j):
        NEG_EM05 = -float(np.exp(-0.5))
        o1 = [XB_OFF]
        o2 = [LNT_OFF]

        def p1(shape, dt):
            nb_ = int(np.prod(shape[1:])) * (2 if dt == BF16 else 4)
            t = sub(o1[0], shape, dt)
            o1[0] = (o1[0] + nb_ + 63) // 64 * 64
            assert o1[0] <= XB_OFF + 8 * TOK * 2
            return t

        def p2(shape, dt, at=None):
            nb_ = int(np.prod(shape[1:])) * (2 if dt == BF16 else 4)
            if at is not None:
                return sub(at, shape, dt)
            t = sub(o2[0], shape, dt)
            o2[0] = (o2[0] + nb_ + 63) // 64 * 64
            assert o2[0] <= SB_END
            return t

        BLKf = p1([128, 128], F32)
        BLKb = p1([128, 128], BF16)
        IND2 = p1([128, 2], F32)
        EY = [p1([128, 256], BF16) for _ in range(2)]
        CARRY = p1([128, 8, 1], F32)
        SHIFTS = p1([128, 8, 16, 1], F32)
        CT2 = p1([128, 8, 16], F32)
        SEL = [p1([128, 64], BF16) for _ in range(4)]
        LOR = p1([128, 3, 128], BF16)
        RS = p1([128, 16, 1], F32)
        GS = p1([128, 4, 16, 1], F32)
        M_SU = p1([128, 128], F32)
        M_SL = p1([128, 128], F32)
        M_UI = p1([128, 128], F32)
        WC = p1([128, 8, 2, 1], F32)
        TMPS = p1([128, 2, 64], F32)
        tmps2_off = o1[0]
        o1[0] += 512
        g0t2_off = o1[0]
        o1[0] += 2048
        fm_off = o1[0]
        o1[0] += 5 * 4096 + 2 * 2048
        assert o1[0] <= XB_OFF + 8 * TOK * 2
        VTOK = p2([128, 1024], F32)
        VTOKB_OFF = o2[0]
        VTOKB = p2([128, 1024], BF16)
        GTOK = p2([128, 1024], F32)
        STS = [p2([128, 1, 512], F32) for _ in range(3)]
        ST_P = STS[0]
        grp_off = o2[0]
        GRP = [p2([128, 4, 128], F32) for _ in range(4)]
        MAKT, NRBT, NRKT, ZC = GRP
        PA, PTA, PB, PTB, ZCB = [p2([128, 4, 128], BF16) for _ in range(5)]
        ST_SS = [p2([128, 2, 512], F32, at=grp_off + i * 4096) for i in range(2)]
        PTMP_OFF = o2[0]
        o2[0] += 4096
        G0T = p2([128, 2, 2, 128], F32)
        H0 = p2([128, 2, 2, 64], F32)
        RQ4 = p2([128, 2, 2, 2, 128], F32)
        TMPS_2 = sub(tmps2_off, [128, 2, 64], F32)
        G0T_2 = sub(g0t2_off, [128, 2, 2, 128], F32)
        LXG = p2([128, 1024], F32)
        LXB = p2([128, 1024], F32)
        scr = o2[0]
        o2[0] += 20480
        assert o2[0] <= SB_END
        T1 = p2([128, 2, 512], F32, at=scr)
        T2 = p2([128, 2, 512], F32, at=scr + 4096)
        TMPb = p2([128, 2, 512], BF16, at=scr + 8192)
        TY = p2([128, 2, 512], BF16, at=scr + 10240)
        YT = p2([128, 16, 64], F32, at=scr + 12288)
        SQT = p2([128, 16, 64], F32, at=scr + 16384)
        SGT = p2([128, 1024], F32, at=scr + 12288)
        ATOK = p2([128, 1024], F32, at=scr)
        BTOK = p2([128, 1024], F32, at=scr + 4096)
        KTOK = p2([128, 1024], F32, at=scr + 8192)
        SN = p2([128, 1024], F32, at=scr + 12288)
        SO = p2([128, 1024], F32, at=scr + 16384)

        def fm_tiles(nt):
            d = {}
            names = ["R", "WDEC", "KMOD", "NKK", "KKA"]
            for i, nm in enumerate(names):
                d[nm] = sub(fm_off + i * 4096, [128, 8, nt], F32)
            d["MIX"] = [sub(fm_off + 5 * 4096 + i * 2048, [128, 8, nt], BF16) for i in range(2)]
            for i, nm in enumerate(["XX", "KRAW", "AA", "TA", "TB"]):
                d[nm] = sub(scr + i * 4096, [128, 8, nt], F32)
            d["ZFM"] = sub(scr + 8192, [128, 8, nt], BF16)
            return d

        op('dve', 'memset', ap=BLKf[:], constant=0.0)
        op('dve', 'memset', ap=BLKf[0:64, 0:64], constant=1.0)
        op('dve', 'memset', ap=BLKf[64:128, 64:128], constant=1.0)
        op('dve', 'tensor_copy', out=BLKb[:], in_=BLKf[:])
        op('dve', 'memset', ap=IND2[:], constant=0.0)
        op('dve', 'memset', ap=IND2[0:64, 0:1], constant=1.0)
        op('dve', 'memset', ap=IND2[64:128, 1:2], constant=1.0)
        for h2 in range(2):
            op('dve', 'memset', ap=EY[h2][:], constant=0.0)
            op('dve', 'memset', ap=EY[h2][h2 * 64:(h2 + 1) * 64, 127:128], constant=1.0)
        op('dve', 'memset', ap=CARRY[:], constant=0.0)
        op('dve', 'memset', ap=ST_P[:], constant=0.0)
        for msk, cm, pat, cmp_ in [(M_SU, -1, 1, ALU.is_gt), (M_SL, 1, -1, ALU.is_gt), (M_UI, -1, 1, ALU.is_ge)]:
            op('pool', 'memset', ap=msk[:], constant=1.0)
            op('pool', 'affine_select', out=msk[:], in_=msk[:], pattern=[[pat, 128]], compare_op=cmp_, fill=0.0, base=0, channel_multiplier=cm)
            op('pool', 'memset', ap=msk[0:64, 64:128], constant=0.0)
            op('pool', 'memset', ap=msk[64:128, 0:64], constant=0.0)
        stbase = [0]
        op('pool', 'memset', ap=G0T[:], constant=0.0)
        op('pool', 'memset', ap=RQ4[:], constant=0.0)
        op('sp', 'dma_start', out=LXG[:], in_=bcast_rows(W["b_lnx_g"][j:j + 1, :], 128))
        op('sp', 'dma_start', out=LXB[:], in_=bcast_rows(W["b_lnx_b"][j:j + 1, :], 128))
        op('sp', 'dma_start', out=SN[0:16, :], in_=shift_in)
        for hh in range(2):
            b = bank()
            for q in range(4):
                op('pe', 'transpose', out=PS[:, b, q * 16:(q + 1) * 16], in_=SN[0:16, (hh * 4 + q) * 128:(hh * 4 + q + 1) * 128],
                   identity=ident[0:16, 0:16])
            op('dve', 'tensor_copy', out=SHIFTS[:, hh * 4:(hh + 1) * 4, :, 0], in_=PS[:, b, 0:64].rearrange("p (c b) -> p c b", c=4))

        wr = W["b_w_rkv"][j, 0]; wk = W["b_w_rkv"][j, 1]; wv = W["b_w_rkv"][j, 2]
        mixc = [0]
        YB = [0, 1]
        SAB = [2, 3]
        VBB = [4, 5]
        selc = [0]

        def vc8(key):
            return VEC[:, VC[key]:VC[key] + 8]

        def b_tile(t0, nt, samp, first_tile, last_prompt):
            F = fm_tiles(nt)
            XX, KRAW, AA, TA, TB = F["XX"], F["KRAW"], F["AA"], F["TA"], F["TB"]
            R_, WDEC, KMOD, NKK, KKA, ZFM = F["R"], F["WDEC"], F["KMOD"], F["NKK"], F["KKA"], F["ZFM"]
            X = XF[:, :, t0:t0 + nt]

            def bc8(key, i0=0):
                return VEC[:, VC[key] + i0:VC[key] + i0 + 8].unsqueeze(2).to_broadcast([128, 8, nt])
            if not samp:
                op('dve', 'tensor_tensor', out=XX[:, :, 1:nt], in0=XF[:, :, t0:t0 + nt - 1], in1=XF[:, :, t0 + 1:t0 + nt], op=ALU.subtract)
                op('dve', 'tensor_tensor', out=XX[:, :, 0:1], in0=CARRY[:], in1=XF[:, :, t0:t0 + 1], op=ALU.subtract)
                op('act', 'activation', out=CARRY[:], in_=XF[:, :, t0 + nt - 1:t0 + nt], func=AF.Copy)
                if last_prompt:
                    b = bank()
                    op('pe', 'transpose', out=PS[0:8, b, 0:128], in_=CARRY[:, :, 0], identity=ident[:])
                    op('dve', 'tensor_copy', out=SO[0:8, 0:128], in_=PS[0:8, b, 0:128])
                    op('sp', 'dma_start', out=o_shift_p.rearrange("o (c p) -> (o c) p", p=128), in_=SO[0:8, 0:128])
            else:
                for c in range(8):
                    xv_ = XF[:, c, t0:t0 + nt].rearrange("p (b t) -> p b t", t=4)
                    xxv = XX[:, c, :].rearrange("p (b t) -> p b t", t=4)
                    op('dve', 'tensor_tensor', out=xxv[:, :, 1:4], in0=xv_[:, :, 0:3], in1=xv_[:, :, 1:4], op=ALU.subtract)
                    op('dve', 'tensor_tensor', out=xxv[:, :, 0:1], in0=SHIFTS[:, c, :, :], in1=xv_[:, :, 0:1], op=ALU.subtract)
                    op('dve', 'tensor_copy', out=CT2[:, c, :], in_=xv_[:, :, 3])
                for hh in range(2):
                    b = bank()
                    for q in range(4):
                        op('pe', 'transpose', out=PS[0:16, b, q * 128:(q + 1) * 128], in_=CT2[:, hh * 4 + q, :], identity=ident[:])
                    op('dve', 'tensor_copy', out=SO[0:16, hh * 512:(hh + 1) * 512], in_=PS[0:16, b, :])
                op('sp', 'dma_start', out=o_shift_s, in_=SO[0:16, :])

            PTMP = sub(PTMP_OFF, [128, 8, nt], F32)

            def mix(i):
                mx = F["MIX"][mixc[0] % 2]; mixc[0] += 1
                op('dve', 'tensor_tensor', out=PTMP[:], in0=XX[:], in1=bc8("b_mu", i * 8), op=ALU.mult)
                op('dve', 'tensor_tensor', out=mx[:], in0=PTMP[:], in1=X, op=ALU.add)
                return mx

            def proj_fm(mx, wmat, epi, kname):
                def body(mb, cw):
                    for mm_ in range(2):
                        m = mb * 2 + mm_
                        b = bank()
                        for kc in range(8):
                            op('pe', 'matmul', out=PS[:, b, 0:nt], lhsT=cw[:, kc, mm_ * 128:(mm_ + 1) * 128], rhs=mx[:, kc, :],
                               start=(kc == 0), stop=(kc == 7))
                        epi(m, PS[:, b, 0:nt])
                pipelined([[wmat[:, mb * 256:(mb + 1) * 256]] for mb in range(4)], 1024, body, keys=[("b", kname, mb) for mb in range(4)])

            mx = mix(0)
            proj_fm(mx, wr, lambda m, ps: op('act', 'activation', out=R_[:, m, :], in_=ps, func=AF.Copy), "r")
            mx = mix(1)
            cw = wload([W["b_w1"][j]], 1024, key=("b", "b_w1"))
            b = bank()
            for kc in range(8):
                op('pe', 'matmul', out=PS[0:64, b, 0:nt], lhsT=cw[:, kc, 0:64], rhs=mx[:, kc, :], start=(kc == 0), stop=(kc == 7))
            op('act', 'activation', out=LOR[0:64, 0, 0:nt], in_=PS[0:64, b, 0:nt], func=AF.Tanh)
            cw = wload([W["b_w2"][j]], 64, key=("b", "b_w2"))
            for m in range(8):
                b = bank()
                op('pe', 'matmul', out=PS[:, b, 0:nt], lhsT=cw[0:64, 0, m * 128:(m + 1) * 128], rhs=LOR[0:64, 0, 0:nt], start=True, stop=True)
                op('act', 'activation', out=WDEC[:, m, :], in_=PS[:, b, 0:nt], func=AF.Sigmoid, bias=vcol("b_w0", m), scale=1.0)
            if samp:
                op('act', 'activation', out=WDEC[:], in_=WDEC[:], func=AF.Exp, scale=NEG_EM05)
            mx = mix(2)
            proj_fm(mx, wk, lambda m, ps: op('act', 'activation', out=KRAW[:, m, :], in_=ps, func=AF.Copy), "k")
            mx = mix(3)

            def body_v(q, cw):
                b = bank()
                for kc in range(8):
                    op('pe', 'matmul', out=PS[0:nt, b, 0:256], lhsT=mx[:, kc, :], rhs=cw[:, kc, :], start=(kc == 0), stop=(kc == 7))
                op('act', 'activation', out=VTOK[0:nt, q * 256:(q + 1) * 256], in_=PS[0:nt, b, 0:256], func=AF.Copy)
            pipelined([[wv[:, q * 256:(q + 1) * 256]] for q in range(4)], 1024, body_v, keys=[("b", "v", q) for q in range(4)])
            if samp:
                op('act', 'activation', out=VTOKB[0:nt, :], in_=VTOK[0:nt, :], func=AF.Copy)
            mx = mix(4)
            cw = wload([W["b_a1"][j]], 1024, key=("b", "b_a1"))
            b = bank()
            for kc in range(8):
                op('pe', 'matmul', out=PS[0:64, b, 0:nt], lhsT=cw[:, kc, 0:64], rhs=mx[:, kc, :], start=(kc == 0), stop=(kc == 7))
            op('act', 'activation', out=LOR[0:64, 1, 0:nt], in_=PS[0:64, b, 0:nt], func=AF.Copy)
            cw = wload([W["b_a2"][j]], 64, key=("b", "b_a2"))
            for m in range(8):
                b = bank()
                op('pe', 'matmul', out=PS[:, b, 0:nt], lhsT=cw[0:64, 0, m * 128:(m + 1) * 128], rhs=LOR[0:64, 1, 0:nt], start=True, stop=True)
                op('act', 'activation', out=AA[:, m, :], in_=PS[:, b, 0:nt], func=AF.Sigmoid, bias=vcol("b_a0", m), scale=1.0)
            mx = mix(5)
            cw = wload([W["b_g1"][j]], 1024, key=("b", "b_g1"))
            b = bank()
            for kc in range(8):
                op('pe', 'matmul', out=PS[:, b, 0:nt], lhsT=cw[:, kc, 0:128], rhs=mx[:, kc, :], start=(kc == 0), stop=(kc == 7))
            op('act', 'activation', out=LOR[:, 2, 0:nt], in_=PS[:, b, 0:nt], func=AF.Sigmoid)
            cw = wload([W["b_g2"][j]], 128, key=("b", "b_g2"))
            for q in range(2):
                b = bank()
                op('pe', 'matmul', out=PS[0:nt, b, :], lhsT=LOR[:, 2, 0:nt], rhs=cw[:, 0, q * 512:(q + 1) * 512], start=True, stop=True)
                op('act', 'activation', out=GTOK[0:nt, q * 512:(q + 1) * 512], in_=PS[0:nt, b, :], func=AF.Copy)

            op('dve', 'tensor_tensor', out=TA[:], in0=KRAW[:], in1=bc8("b_k_k"), op=ALU.mult)
            op('dve', 'tensor_tensor', out=TB[:], in0=TA[:], in1=TA[:], op=ALU.mult)
            taf = TA[:].rearrange("p c t -> p (c t)")
            tbf = TB[:].rearrange("p c t -> p (c t)")
            nflat = 8 * nt
            for q0 in range(0, nflat, 512):
                b = bank()
                op('pe', 'matmul', out=PS[:, b, :], lhsT=BLKf[:], rhs=tbf[:, q0:q0 + 512], start=True, stop=True)
                op('act', 'activation', out=tbf[:, q0:q0 + 512], in_=PS[:, b, :], func=AF.Sqrt)
            op('dve', 'tensor_scalar', out=TB[:], in0=TB[:], scalar1=1e-12, scalar2=None, op0=ALU.max)
            op('dve', 'reciprocal', out=TB[:], in_=TB[:])
            op('dve', 'tensor_tensor', out=TA[:], in0=TA[:], in1=TB[:], op=ALU.mult)
            op('act', 'activation', out=NKK[:], in_=TA[:], func=AF.Copy, scale=-1.0)
            op('dve', 'tensor_tensor', out=KKA[:], in0=TA[:], in1=AA[:], op=ALU.mult)
            op('dve', 'scalar_tensor_tensor', out=TB[:], in0=AA[:], scalar=-1.0, in1=bc8("b_k_a"), op0=ALU.add, op1=ALU.mult)
            op('dve', 'scalar_tensor_tensor', out=KMOD[:], in0=TB[:], scalar=1.0, in1=KRAW[:], op0=ALU.add, op1=ALU.mult)
            op('dve', 'tensor_tensor', out=TB[:], in0=R_[:], in1=KMOD[:], op=ALU.mult)
            op('dve', 'tensor_tensor', out=TB[:], in0=TB[:], in1=bc8("b_r_k"), op=ALU.mult)
            b = bank()
            for c in range(8):
                op('pe', 'matmul', out=PS[0:nt, b, c * 2:(c + 1) * 2], lhsT=TB[:, c, :], rhs=IND2[:], start=True, stop=True)
            op('dve', 'tensor_copy', out=RS[0:nt, :, 0], in_=PS[0:nt, b, 0:16])

            for bb_ in range(6):
                reserved.add(bb_)
            stepc = [0]

            def scan_step(toks, ST, nb, first, last, t_local, b0):
                k = stepc[0]; stepc[0] += 1
                S = ST[:, 0:nb, :].rearrange("p b (c v) -> p b c v", v=64)

                def bcv(T):
                    if not samp:
                        return T[:, :, toks[0]:toks[0] + 1].unsqueeze(1).to_broadcast([128, 1, 8, 64])
                    v4 = T[:, :, :].rearrange("p c (b t) -> p c b t", t=4)[:, :, b0:b0 + nb, t_local:t_local + 1]
                    return v4.rearrange("p c b o -> p b c o").to_broadcast([128, nb, 8, 64])

                def v4(tile_):
                    return tile_[:, 0:nb, :].rearrange("p b (c v) -> p b c v", v=64)
                if nb == 1:
                    sa = [SAB[k % 2]]; vb = [VBB[k % 2]]
                    sa_ps = PS[:, sa[0]:sa[0] + 1, :]
                    vb_ps = PS[:, vb[0]:vb[0] + 1, :]
                else:
                    sa = SAB; vb = VBB
                    sa_ps = PS[:, 2:4, :]
                    vb_ps = PS[:, 4:6, :]
                op('dve', 'tensor_tensor', out=v4(TMPb), in0=S, in1=bcv(NKK), op=ALU.mult)
                for bi in range(nb):
                    op('pe', 'matmul', out=PS[:, sa[bi], :], lhsT=BLKb[:], rhs=TMPb[:, bi, :], start=True, stop=True)
                for bi in range(nb):
                    sl = SEL[selc[0] % 4]; selc[0] += 1
                    op('pool', 'tensor_copy', out=sl[0:nt, :], in_=identb[0:nt, toks[bi]:toks[bi] + 1].to_broadcast([nt, 64]))
                    vsrc = VTOKB[0:nt, :].rearrange("t (c h v) -> t c h v", h=2, v=64)
                    for h2 in range(2):
                        op('pe', 'matmul', out=PS[h2 * 64:(h2 + 1) * 64, vb[bi], :], lhsT=sl[0:nt, :], rhs=vsrc[:, :, h2, :],
                           start=True, stop=True)
                op('dve', 'tensor_tensor', out=v4(T1), in0=sa_ps.rearrange("p b (c v) -> p b c v", v=64), in1=bcv(KKA), op=ALU.mult)
                op('dve', 'tensor_tensor', out=S, in0=S, in1=bcv(WDEC), op=ALU.mult)
                op('dve', 'tensor_tensor', out=S, in0=S, in1=v4(T1), op=ALU.add)
                op('dve', 'tensor_tensor', out=v4(T2), in0=vb_ps.rearrange("p b (c v) -> p b c v", v=64), in1=bcv(KMOD), op=ALU.mult)
                op('dve', 'tensor_tensor', out=S, in0=S, in1=v4(T2), op=ALU.add)
                op('dve', 'tensor_tensor', out=v4(TY), in0=S, in1=bcv(R_), op=ALU.mult)
                for bi in range(nb):
                    for h2 in range(2):
                        op('pe', 'matmul', out=PS[0:nt, YB[h2], :], lhsT=EY[h2][:, 127 - toks[bi]:127 - toks[bi] + nt], rhs=TY[:, bi, :],
                           start=(first and bi == 0), stop=(last and bi == nb - 1))


            def chunked():
                SG = WDEC
                for hh in range(2):
                    b = bank()
                    for q in range(4):
                        op('pe', 'transpose', out=PS[:, b, q * 128:(q + 1) * 128], in_=SG[:, hh * 4 + q, :], identity=ident[:])
                    op('act', 'activation', out=SGT[:, hh * 512:(hh + 1) * 512], in_=PS[:, b, :], func=AF.Copy)
                EIN, EINV, EEX = XX, KRAW, AA
                for hh in range(2):
                    bi = bank(); be = bank()
                    for q in range(4):
                        c = hh * 4 + q
                        op('pe', 'matmul', out=PS[:, bi, q * 128:(q + 1) * 128], lhsT=SGT[:, c * 128:(c + 1) * 128], rhs=M_UI[:], start=True, stop=True)
                        op('pe', 'matmul', out=PS[:, be, q * 128:(q + 1) * 128], lhsT=SGT[:, c * 128:(c + 1) * 128], rhs=M_SU[:], start=True, stop=True)
                    pvi = PS[:, bi, :].rearrange("p (a t) -> p a t", a=4)
                    pve = PS[:, be, :].rearrange("p (a t) -> p a t", a=4)
                    op('act', 'activation', out=EIN[:, hh * 4:(hh + 1) * 4, :], in_=pvi, func=AF.Exp, scale=NEG_EM05)
                    op('act', 'activation', out=EINV[:, hh * 4:(hh + 1) * 4, :], in_=pvi, func=AF.Exp, scale=-NEG_EM05)
                    op('act', 'activation', out=EEX[:, hh * 4:(hh + 1) * 4, :], in_=pve, func=AF.Exp, scale=NEG_EM05)
                op('act', 'activation', out=WC[:], in_=EIN[:].rearrange("p c (q t) -> p c q t", t=64)[:, :, :, 63:64], func=AF.Copy)
                At, Bt, Kt, Rt = NKK, KKA, KMOD, R_
                op('dve', 'tensor_tensor', out=At[:], in0=NKK[:], in1=EEX[:], op=ALU.mult)
                op('dve', 'tensor_tensor', out=Bt[:], in0=KKA[:], in1=EINV[:], op=ALU.mult)
                op('dve', 'tensor_tensor', out=Kt[:], in0=KMOD[:], in1=EINV[:], op=ALU.mult)
                op('dve', 'tensor_tensor', out=Rt[:], in0=R_[:], in1=EIN[:], op=ALU.mult)
                for src, dst in [(At, ATOK), (Bt, BTOK), (Kt, KTOK)]:
                    for hh in range(2):
                        b = bank()
                        for q in range(4):
                            op('pe', 'transpose', out=PS[:, b, q * 128:(q + 1) * 128], in_=src[:, hh * 4 + q, :], identity=ident[:])
                        op('act', 'activation', out=dst[:, hh * 512:(hh + 1) * 512], in_=PS[:, b, :], func=AF.Copy)
                base = stbase[0]
                T1s = dict(PA=PA, PTA=PTA, PB=PB, PTB=PTB, MAKT=MAKT, NRBT=NRBT, NRKT=NRKT, ZC=ZC, ZCB=ZCB, G0T=G0T, H0=H0, RQ4=RQ4, TMPS=TMPS)
                s4 = scr + 16384
                wd = fm_off + 1 * 4096
                mxo = fm_off + 5 * 4096
                T2s = dict(MAKT=sub(s4, [128, 4, 128], F32), NRBT=sub(s4 + 2048, [128, 4, 128], F32),
                           NRKT=sub(wd, [128, 4, 128], F32), ZC=sub(wd + 2048, [128, 4, 128], F32),
                           PA=sub(mxo, [128, 4, 128], BF16), PTA=sub(mxo + 1024, [128, 4, 128], BF16),
                           PB=sub(mxo + 2048, [128, 4, 128], BF16), PTB=sub(mxo + 3072, [128, 4, 128], BF16),
                           ZCB=sub(VTOKB_OFF, [128, 4, 128], BF16), H0=sub(VTOKB_OFF + 1024, [128, 2, 2, 64], F32),
                           RQ4=sub(PTMP_OFF, [128, 2, 2, 2, 128], F32), G0T=G0T_2, TMPS=TMPS_2)
                op('pool', 'memset', ap=T2s['RQ4'][:], constant=0.0)
                op('pool', 'memset', ap=T2s['G0T'][:], constant=0.0)

                def grp_gen(g, T):
                    PA_, PTA_, PB_, PTB_ = T['PA'], T['PTA'], T['PB'], T['PTB']
                    MAKT_, NRBT_, NRKT_, ZC_, ZCB_ = T['MAKT'], T['NRBT'], T['NRKT'], T['ZC'], T['ZCB']
                    G0T_, H0_, RQ4_, TMPS_ = T['G0T'], T['H0'], T['RQ4'], T['TMPS']
                    heads = [(2 * g + cc, h2) for cc in range(2) for h2 in range(2)]

                    def fm(Tl, c, h2):
                        return Tl[h2 * 64:(h2 + 1) * 64, c, :]

                    def col(hl):
                        return g * 256 + hl * 64

                    def v4(ps_ap, a):
                        return ps_ap.rearrange("p (a t) -> p a t", a=a)
                    for dst, lt, rt, msk in [(PA_, Bt, At, M_SU), (PTA_, At, Bt, M_SL), (MAKT_, Kt, At, M_SU), (NRBT_, Bt, Rt, M_UI), (NRKT_, Kt, Rt, M_UI)]:
                        for h2 in range(2):
                            b = bank()
                            for cc in range(2):
                                c = 2 * g + cc
                                op('pe', 'matmul', out=PS[:, b, cc * 128:(cc + 1) * 128], lhsT=fm(lt, c, h2), rhs=fm(rt, c, h2), start=True, stop=True)
                            dv = dst[:].rearrange("p (cc h) t -> p cc h t", h=2)[:, :, h2, :]
                            op('dve', 'tensor_tensor', out=dv, in0=v4(PS[:, b, 0:256], 2), in1=msk[:].unsqueeze(1).to_broadcast([128, 2, 128]), op=ALU.mult)
                        yield
                    for cc in range(2):
                        hs = slice(2 * cc, 2 * cc + 2)
                        b = bank()
                        for hl in range(2 * cc, 2 * cc + 2):
                            op('pe', 'matmul', out=PS[:, b, (hl % 2) * 64:(hl % 2 + 1) * 64], lhsT=MAKT_[:, hl, :], rhs=VTOK[:, col(hl):col(hl) + 64], start=True, stop=True)
                        op('act', 'activation', out=ZC_[:, hs, 64:128], in_=v4(PS[:, b, 0:128], 2), func=AF.Copy)
                        op('dve', 'tensor_copy', out=ZC_[:, hs, 0:64], in_=ATOK[:, g * 256 + cc * 128:g * 256 + (cc + 1) * 128].rearrange("p (a t) -> p a t", a=2))
                        op('act', 'activation', out=ZCB_[:, hs, :], in_=ZC_[:, hs, :], func=AF.Copy)
                    yield
                    Pc, PTc, Pn, PTn = PA_, PTA_, PB_, PTB_
                    for s_ in range(6):
                        for cc in range(2):
                            hs = slice(2 * cc, 2 * cc + 2)
                            b = bank()
                            for hl in range(2 * cc, 2 * cc + 2):
                                op('pe', 'matmul', out=PS[:, b, (hl % 2) * 128:(hl % 2 + 1) * 128], lhsT=Pc[:, hl, :], rhs=ZCB_[:, hl, :], start=True, stop=True)
                            op('dve', 'tensor_tensor', out=ZC_[:, hs, :], in0=ZC_[:, hs, :], in1=v4(PS[:, b, 0:256], 2), op=ALU.add)
                            if s_ < 5:
                                op('act', 'activation', out=ZCB_[:, hs, :], in_=ZC_[:, hs, :], func=AF.Copy)
                                b1 = bank(); b2 = bank()
                                for hl in range(2 * cc, 2 * cc + 2):
                                    op('pe', 'matmul', out=PS[:, b1, (hl % 2) * 128:(hl % 2 + 1) * 128], lhsT=PTc[:, hl, :], rhs=Pc[:, hl, :], start=True, stop=True)
                                for hl in range(2 * cc, 2 * cc + 2):
                                    op('pe', 'matmul', out=PS[:, b2, (hl % 2) * 128:(hl % 2 + 1) * 128], lhsT=Pc[:, hl, :], rhs=PTc[:, hl, :], start=True, stop=True)
                                op('act', 'activation', out=Pn[:, hs, :], in_=v4(PS[:, b1, 0:256], 2), func=AF.Copy)
                                op('dve', 'tensor_copy', out=PTn[:, hs, :], in_=v4(PS[:, b2, 0:256], 2))
                            yield
                        Pc, PTc, Pn, PTn = Pn, PTn, Pc, PTc
                    b = bank()
                    for hl, (c, h2) in enumerate(heads):
                        cc = c - 2 * g
                        op('pe', 'matmul', out=PS[h2 * 64:(h2 + 1) * 64, b, cc * 128:(cc + 1) * 128], lhsT=ZC_[:, hl, 0:64], rhs=NRBT_[:, hl, :], start=True, stop=True)
                    for h2 in range(2):
                        for q in range(2):
                            hp = slice(h2 * 64, (h2 + 1) * 64)
                            op('dve', 'tensor_tensor', out=RQ4_[hp, q, :, h2, q * 64:(q + 1) * 64], in0=Rt[hp, 2 * g:2 * g + 2, q * 64:(q + 1) * 64],
                               in1=v4(PS[hp, b, 0:256], 2)[:, :, q * 64:(q + 1) * 64], op=ALU.add)
                    yield
                    for q in range(2):
                        r0, r1 = q * 64, (q + 1) * 64
                        bg = bank(); bh = bank()
                        for hl, (c, h2) in enumerate(heads):
                            cc = c - 2 * g
                            hp = slice(h2 * 64, (h2 + 1) * 64)
                            op('pe', 'matmul', out=PS[hp, bg, cc * 64:(cc + 1) * 64], lhsT=ZC_[r0:r1, hl, 0:64], rhs=BTOK[r0:r1, col(hl):col(hl) + 64], start=True, stop=True)
                            op('pe', 'matmul', out=PS[hp, bh, cc * 64:(cc + 1) * 64], lhsT=BTOK[r0:r1, col(hl):col(hl) + 64], rhs=ZC_[r0:r1, hl, 64:128], start=True, stop=False)
                            op('pe', 'matmul', out=PS[hp, bh, cc * 64:(cc + 1) * 64], lhsT=KTOK[r0:r1, col(hl):col(hl) + 64], rhs=VTOK[r0:r1, col(hl):col(hl) + 64], start=False, stop=True)
                        for h2 in range(2):
                            hp = slice(h2 * 64, (h2 + 1) * 64)
                            op('act', 'activation', out=G0T_[hp, :, q, h2 * 64:(h2 + 1) * 64], in_=v4(PS[hp, bg, 0:128], 2), func=AF.Copy)
                        op('act', 'activation', out=H0_[:, :, q, :], in_=v4(PS[:, bh, 0:128], 2), func=AF.Copy)
                        yield
                    for q in range(2):
                        Sc = STS[(base + q) % 3][:, 0, :].rearrange("p (c v) -> p c v", v=64)
                        Sn = STS[(base + q + 1) % 3][:, 0, :].rearrange("p (c v) -> p c v", v=64)
                        b = bank()
                        for cc in range(2):
                            c = 2 * g + cc
                            op('pe', 'matmul', out=PS[:, b, cc * 64:(cc + 1) * 64], lhsT=G0T_[:, cc, q, :], rhs=Sc[:, c, :], start=True, stop=True)
                        op('dve', 'tensor_tensor', out=TMPS_[:], in0=v4(PS[:, b, 0:128], 2), in1=Sc[:, 2 * g:2 * g + 2, :], op=ALU.add)
                        op('dve', 'tensor_tensor', out=TMPS_[:], in0=TMPS_[:], in1=H0_[:, :, q, :], op=ALU.add)
                        op('dve', 'tensor_tensor', out=Sn[:, 2 * g:2 * g + 2, :], in0=TMPS_[:], in1=WC[:, 2 * g:2 * g + 2, q, :].to_broadcast([128, 2, 64]), op=ALU.mult)
                        yield
                    by = bank()
                    for hl, (c, h2) in enumerate(heads):
                        cc = c - 2 * g
                        o0 = hl * 64
                        op('pe', 'matmul', out=PS[:, by, o0:o0 + 64], lhsT=NRBT_[:, hl, :], rhs=ZC_[:, hl, 64:128], start=True, stop=False)
                        op('pe', 'matmul', out=PS[:, by, o0:o0 + 64], lhsT=NRKT_[:, hl, :], rhs=VTOK[:, col(hl):col(hl) + 64], start=False, stop=False)
                        for q in range(2):
                            Sq = STS[(base + q) % 3][:, 0, :].rearrange("p (c v) -> p c v", v=64)
                            op('pe', 'matmul', out=PS[:, by, o0:o0 + 64], lhsT=RQ4_[:, q, cc, h2, :], rhs=Sq[:, c, :], start=False, stop=(q == 1))
                    op('act', 'activation', out=YT[:, g * 4:(g + 1) * 4, :], in_=v4(PS[:, by, 0:256], 4), func=AF.Copy)
                    yield

                def interleave2(ga, gb):
                    gens = [ga, gb]
                    while gens:
                        for g_ in list(gens):
                            try:
                                next(g_)
                            except StopIteration:
                                gens.remove(g_)

                interleave2(grp_gen(0, T1s), grp_gen(1, T2s))
                interleave2(grp_gen(2, T1s), grp_gen(3, T2s))
                stbase[0] = (base + 2) % 3

            def store_state(ST, bi, dst):
                for hh in range(2):
                    b = bank()
                    for q in range(4):
                        c = hh * 4 + q
                        op('pe', 'transpose', out=PS[0:64, b, q * 128:(q + 1) * 128], in_=ST[:, bi, c * 64:(c + 1) * 64], identity=ident[:])
                    op('act', 'activation', out=SO[0:64, hh * 512:(hh + 1) * 512], in_=PS[0:64, b, :], func=AF.Copy)
                op('sp', 'dma_start', out=dst.rearrange("h v k -> v h k"), in_=SO[0:64, :].rearrange("v (h k) -> v h k", k=64))

            if not samp:
                for bb_ in range(6):
                    reserved.discard(bb_)
                chunked()
                if last_prompt:
                    store_state(STS[stbase[0]], 0, o_wkv_p)
            else:
                def load_states(g):
                    for bi in range(2):
                        op('sp', 'dma_start', out=SN[0:64, :].rearrange("v (h k) -> v h k", k=64), in_=wkv_in[g * 2 + bi].rearrange("h v k -> v h k"))
                        b = bank()
                        for c in range(8):
                            op('pe', 'transpose', out=PS[:, b, c * 64:(c + 1) * 64], in_=SN[0:64, c * 128:(c + 1) * 128], identity=ident[0:64, 0:64])
                        op('act', 'activation', out=ST_SS[g % 2][:, bi, :], in_=PS[:, b, :], func=AF.Copy)
                load_states(0)
                for g in range(8):
                    b0 = g * 2
                    ST_S = ST_SS[g % 2]
                    if g + 1 < 8:
                        load_states(g + 1)
                    for t in range(4):
                        toks = [(b0 + bi) * 4 + t for bi in range(2)]
                        scan_step(toks, ST_S, 2, (g == 0 and t == 0), (g == 7 and t == 3), t, b0)
                    for bi in range(2):
                        store_state(ST_S, bi, o_wkv_s[b0 + bi])
            for bb_ in range(6):
                reserved.discard(bb_)

            if samp:
                op('act', 'activation', out=YT[0:nt, :, :].rearrange("t (c h) v -> t c h v", h=2)[:, :, 0, :],
                   in_=PS[0:nt, YB[0], :].rearrange("t (c v) -> t c v", v=64), func=AF.Copy)
                op('dve', 'tensor_copy', out=YT[0:nt, :, :].rearrange("t (c h) v -> t c h v", h=2)[:, :, 1, :],
                   in_=PS[0:nt, YB[1], :].rearrange("t (c v) -> t c v", v=64))
            op('dve', 'tensor_reduce', out=GS[0:nt, 0, :, 0], in_=YT[0:nt, :, :], axis=AX.X, op=ALU.add)
            op('act', 'activation', out=SQT[0:nt, :, :], in_=YT[0:nt, :, :], func=AF.Square)
            op('dve', 'tensor_reduce', out=GS[0:nt, 1, :, 0], in_=SQT[0:nt, :, :], axis=AX.X, op=ALU.add)
            op('dve', 'tensor_scalar', out=GS[0:nt, 2, :, :], in0=GS[0:nt, 0, :, :], scalar1=1.0 / 64, scalar2=None, op0=ALU.mult)
            op('dve', 'tensor_tensor', out=GS[0:nt, 3, :, :], in0=GS[0:nt, 2, :, :], in1=GS[0:nt, 2, :, :], op=ALU.mult)
            op('dve', 'scalar_tensor_tensor', out=GS[0:nt, 1, :, :], in0=GS[0:nt, 1, :, :], scalar=1.0 / 64, in1=GS[0:nt, 3, :, :],
               op0=ALU.mult, op1=ALU.subtract)
            op('act', 'activation', out=GS[0:nt, 1, :, :], in_=GS[0:nt, 1, :, :], func=AF.Sqrt, bias=GN_EPS, scale=1.0)
            op('dve', 'reciprocal', out=GS[0:nt, 1, :, :], in_=GS[0:nt, 1, :, :])
            op('dve', 'tensor_tensor', out=YT[0:nt], in0=YT[0:nt], in1=GS[0:nt, 2, :, :].to_broadcast([nt, 16, 64]), op=ALU.subtract)
            op('dve', 'tensor_tensor', out=YT[0:nt], in0=YT[0:nt], in1=GS[0:nt, 1, :, :].to_broadcast([nt, 16, 64]), op=ALU.mult)
            ytf = YT[0:nt, :, :].rearrange("t h v -> t (h v)")
            op('dve', 'tensor_tensor', out=ytf, in0=ytf, in1=LXG[0:nt, :], op=ALU.mult)
            op('dve', 'tensor_tensor', out=ytf, in0=ytf, in1=LXB[0:nt, :], op=ALU.add)
            op('dve', 'tensor_tensor', out=SQT[0:nt], in0=VTOK[0:nt, :].rearrange("t (h v) -> t h v", v=64),
               in1=RS[0:nt, :, :].to_broadcast([nt, 16, 64]), op=ALU.mult)
            op('dve', 'tensor_tensor', out=YT[0:nt], in0=YT[0:nt], in1=SQT[0:nt], op=ALU.add)
            op('dve', 'tensor_tensor', out=ytf, in0=ytf, in1=GTOK[0:nt, :], op=ALU.mult)
            for hh in range(2):
                b = bank()
                for q in range(4):
                    c = hh * 4 + q
                    op('pe', 'transpose', out=PS[:, b, q * nt:(q + 1) * nt], in_=ytf[:, c * 128:(c + 1) * 128], identity=ident[0:nt, 0:nt])
                op('act', 'activation', out=ZFM[:, hh * 4:(hh + 1) * 4, :], in_=PS[:, b, 0:4 * nt].rearrange("p (a t) -> p a t", a=4), func=AF.Copy)

            def body_o(mb, cw):
                for mm_ in range(2):
                    m = mb * 2 + mm_
                    b = bank()
                    for kc in range(8):
                        op('pe', 'matmul', out=PS[:, b, 0:nt], lhsT=cw[:, kc, mm_ * 128:(mm_ + 1) * 128], rhs=ZFM[:, kc, :],
                           start=(kc == 0), stop=(kc == 7))
                    op('dve', 'scalar_tensor_tensor', out=XF[:, m, t0:t0 + nt], in0=XF[:, m, t0:t0 + nt], scalar=ALPHA,
                       in1=PS[:, b, 0:nt], op0=ALU.mult, op1=ALU.add)
            pipelined([[W["b_w_o"][j][:, mb * 256:(mb + 1) * 256]] for mb in range(4)], 1024, body_o, keys=[("b", "o", mb) for mb in range(4)])

        for ti, (t0, nt) in enumerate(T128):
            b_tile(t0, nt, t0 >= 2048, ti == 0, ti == 15)

    def warm_cache(kind, j):
        specs = []
        if kind == 0:
            w_in = W["a_w_in"][j]; w_out = W["a_w_out"][j]
            specs += [([w_in[:, 2048 + q * 256:2048 + (q + 1) * 256]], 1024, ("a", j, "v", q)) for q in range(8)]
            specs += [([w_in[:, bi * 256:(bi + 1) * 256]], 1024, ("a", j, "u", bi)) for bi in range(8)]
            specs += [([w_out[:, m * 128:(m + 1) * 128]], 2048, ("a", j, "o", m)) for m in range(8)]
        elif kind == 1:
            wr = W["b_w_rkv"][j, 0]; wk = W["b_w_rkv"][j, 1]; wv = W["b_w_rkv"][j, 2]
            specs += [([wr[:, mb * 256:(mb + 1) * 256]], 1024, ("b", "r", mb)) for mb in range(4)]
            specs += [([W["b_w1"][j]], 1024, ("b", "b_w1")), ([W["b_w2"][j]], 64, ("b", "b_w2"))]
            specs += [([wk[:, mb * 256:(mb + 1) * 256]], 1024, ("b", "k", mb)) for mb in range(4)]
            specs += [([wv[:, q * 256:(q + 1) * 256]], 1024, ("b", "v", q)) for q in range(4)]
            specs += [([W["b_a1"][j]], 1024, ("b", "b_a1")), ([W["b_a2"][j]], 64, ("b", "b_a2")),
                      ([W["b_g1"][j]], 1024, ("b", "b_g1")), ([W["b_g2"][j]], 128, ("b", "b_g2"))]
            specs += [([W["b_w_o"][j][:, mb * 256:(mb + 1) * 256]], 1024, ("b", "o", mb)) for mb in range(4)]
        else:
            w_in = W["c_w_in"][j]; w_out = W["c_w_out"][j]
            specs += [([w_in[:, 1024 + m * 128:1024 + (m + 1) * 128], w_in[:, 2048 + m * 128:2048 + (m + 1) * 128],
                        w_in[:, m * 128:(m + 1) * 128]], 1024, ("c", "in", m)) for m in range(8)]
            specs += [([w_out[:, mb * 256:(mb + 1) * 256]], 1024, ("c", "out", mb)) for mb in range(4)]
        for parts, K, key in specs:
            wload(parts, K, key)
            yield

    def kind_of(l):
        return (l % 3) if kinds is None else kinds[l]

    load_x()
    for l in range(n_layers):
        ffn(l, 0)
        layer_norm(l * 3 + 0, 4 * LN_EPS)
        kind = (l % 3) if kinds is None else kinds[l]
        if stub_mixers:
            mixer_stub()
        elif kind == 0:
            mixer_a(l // 3 if kinds is None else 0)
        elif kind == 1:
            mixer_b(0)
        else:
            mixer_c(0)
        ffn(l, 1, pre_ln=(l * 3 + 1, LN_EPS))
        layer_norm(l * 3 + 2, 4 * LN_EPS)
        ple(l)
    store_y()
    if DRY:
        return REQ
    P.emit()
    return nc, P


def _shard_inputs(inp, c):
    f = lambda a: np.ascontiguousarray(a, dtype=np.float32)
    sl = slice(16 * c, 16 * c + 16)
    m = {}
    m["x_p"] = f(inp["x_prompt"][c])
    m["x_s"] = f(inp["x_sample"][sl].reshape(64, 1024))
    m["wkv_in"] = f(inp["state_b_wkv"][0, sl])
    m["shift_in"] = f(inp["state_b_shift"][0, sl])
    m["conv_in"] = f(inp["state_c_conv"][0, sl])
    m["p_p"] = f(inp["p_prompt"][:, c])
    m["p_s"] = f(inp["p_sample"][:, sl].reshape(4, 64, 256))
    return m


def _weights(inp):
    f = lambda a: np.ascontiguousarray(a, dtype=np.float32)
    w = {}
    for k in ["ln_g", "ln_b", "ffn_w_in", "ffn_w_out", "ple_w_gate", "ple_w_proj", "a_w_in", "a_b_in", "a_ln_g", "a_ln_b",
              "a_b_s", "a_w_out", "b_mu", "b_w_rkv", "b_w0", "b_w1", "b_w2", "b_a0", "b_a1", "b_a2", "b_g1", "b_g2",
              "b_k_k", "b_k_a", "b_lnx_g", "b_lnx_b", "b_w_o", "c_w_in", "c_conv_w", "c_w_out"]:
        w[k] = f(inp[k])
    ws = np.asarray(inp["a_w_s"], dtype=np.float32)
    w["a_w_sT"] = f(np.transpose(ws, (0, 1, 3, 2)))
    blk = np.zeros((2, 8, 64, 64), np.float32)
    for b in range(16):
        blk[:, :, 4 * b:4 * b + 4, 4 * b:4 * b + 4] = np.transpose(ws[:, :, :4, :4], (0, 1, 3, 2))
    w["a_w_sS"] = blk
    w["a_b_sS"] = f(np.tile(np.asarray(inp["a_b_s"], dtype=np.float32)[:, :, :4], (1, 1, 16)))
    w["b_r_k"] = f(np.asarray(inp["b_r_k"]).reshape(1, 1024))
    return w


def kernel(**inp):
    nc, _ = build_program()
    w = _weights(inp)
    in_maps = []
    for c in range(8):
        m = _shard_inputs(inp, c)
        m.update(w)
        in_maps.append(m)
    res = run_bass_kernel_spmd(nc, in_maps, core_ids=list(range(8)))
    R = res.results
    cat = lambda k: [np.asarray(R[c][k]) for c in range(8)]
    y_p = np.stack(cat("y_p"), 0)
    y_s = np.concatenate([a.reshape(16, 4, 1024) for a in cat("y_s")], 0)
    a_v = np.concatenate([a.reshape(2, 16, 4, 2048) for a in cat("o_av")], 1)
    wkv_p = np.stack(cat("o_wkv_p"), 0)[None]
    shift_p = np.stack([a.reshape(1024) for a in cat("o_shift_p")], 0)[None]
    conv_p = np.stack(cat("o_conv_p"), 0)[None]
    wkv_s = np.concatenate(cat("o_wkv_s"), 0)[None]
    shift_s = np.concatenate(cat("o_shift_s"), 0)[None]
    conv_s = np.concatenate(cat("o_conv_s"), 0)[None]
    outs = (y_p, y_s, a_v, wkv_p, shift_p, conv_p, wkv_s, shift_s, conv_s)
    return tuple(np.ascontiguousarray(o, dtype=np.float32) for o in outs)
```
